# Optimizing a Trainium2 kernel written in Bass

```python
import math
import jax
import jax.numpy as jnp
from jax import lax
import numpy as np

D_MODEL = 1024
BATCH = 2
SEQ = 8192
DEPTH = 4
DEC_BATCH = 32
DEC_SEQ = 32
PAST_LEN = 1024

CHUNK = 64
Q_BLOCK = 128
HEAD_DIM = 64
ROT_DIM = HEAD_DIM // 4
ROPE_THETA = 500000.0
A_HEADS = D_MODEL // 128
B_HEADS = D_MODEL // 128
A_WIDTH = A_HEADS * HEAD_DIM
B_WIDTH = B_HEADS * HEAD_DIM
IDX_HEADS = 8
IDX_DIM = 64
IDX_TOPK_MAX = 256
C_HEADS = D_MODEL // 128
C_QK_DIM = 64
C_V_DIM = 2 * C_QK_DIM
C_WIDTH = C_HEADS * C_V_DIM
N_EVEN = (DEPTH + 1) // 2
N_ODD = DEPTH // 2
EVEN_SPLITS = (A_WIDTH, A_WIDTH, A_WIDTH, A_WIDTH, A_HEADS,
               B_WIDTH, B_WIDTH, B_WIDTH, B_WIDTH, IDX_HEADS * IDX_DIM, IDX_DIM, IDX_HEADS)
EVEN_IN = sum(EVEN_SPLITS)
EVEN_MIX = A_WIDTH + B_WIDTH
ODD_IN = 4 * C_WIDTH
FORGET_BIAS = 2.0
NORM_EPS = 1e-6
NEG_INF = -1e30

kernel_name = 'hybrid_fox_dsa_diff_stream_step'


def rms_norm(x, g):
    xf = x.astype(jnp.float32)
    y = xf * lax.rsqrt(jnp.mean(xf * xf, axis=-1, keepdims=True) + NORM_EPS)
    return (y * g.astype(jnp.float32)).astype(x.dtype)


def partial_rope(x, pos):
    half = ROT_DIM // 2
    inv = ROPE_THETA ** (-jnp.arange(half, dtype=jnp.float32) * 2.0 / ROT_DIM)
    ang = pos.astype(jnp.float32)[:, None] * inv[None, :]
    ang = ang.reshape(ang.shape[0], *([1] * (x.ndim - 3)), half)
    cos, sin = jnp.cos(ang), jnp.sin(ang)
    xr = x[..., :ROT_DIM].astype(jnp.float32)
    x1, x2 = xr[..., :half], xr[..., half:]
    rot = jnp.concatenate([x1 * cos - x2 * sin, x2 * cos + x1 * sin], axis=-1).astype(x.dtype)
    return jnp.concatenate([rot, x[..., ROT_DIM:]], axis=-1)


def with_past(cache, new):
    return new if cache is None else jnp.concatenate([cache, new], axis=1)


def sweep_query_blocks(block_fn, n_q):
    if n_q <= Q_BLOCK:
        return block_fn(0, n_q)
    n_blocks = n_q // Q_BLOCK
    out = lax.map(lambda i: block_fn(i * Q_BLOCK, Q_BLOCK), jnp.arange(n_blocks))
    out = jnp.moveaxis(out, 0, 1)
    return out.reshape(out.shape[0], n_q, *out.shape[3:])


def fox_attention(q, k_all, v_all, logf_all, past_len):
    n_q, n_k = q.shape[1], k_all.shape[1]
    cum = jnp.cumsum(logf_all.astype(jnp.float32), axis=1)
    cum_k = jnp.swapaxes(cum, 1, 2)
    cum_q = cum[:, past_len:]
    k_pos = jnp.arange(n_k)
    scale = HEAD_DIM ** -0.5

    def block(start, size):
        qb = lax.dynamic_slice_in_dim(q, start, size, axis=1)
        cq = jnp.swapaxes(lax.dynamic_slice_in_dim(cum_q, start, size, axis=1), 1, 2)
        q_pos = past_len + start + jnp.arange(size)
        s = jnp.einsum('bqhd,bkhd->bhqk', qb, k_all).astype(jnp.float32) * scale
        s = s + cq[..., None] - cum_k[:, :, None, :]
        s = jnp.where(k_pos[None, :] <= q_pos[:, None], s, NEG_INF)
        p = jax.nn.softmax(s, axis=-1).astype(v_all.dtype)
        return jnp.einsum('bhqk,bkhd->bqhd', p, v_all)

    return sweep_query_blocks(block, n_q)


def gather_rows(rows, sel):
    return jax.vmap(lambda r, i: r[i])(rows, sel)


def dsa_attention(q, k_all, v_all, qi, ki_all, wi, past_len):
    n_q, n_k = q.shape[1], k_all.shape[1]
    topk = min(IDX_TOPK_MAX, n_k // 4)
    k_chunk = jnp.arange(n_k) // CHUNK
    scale = HEAD_DIM ** -0.5

    def block(start, size):
        qb = lax.dynamic_slice_in_dim(q, start, size, axis=1)
        qib = lax.dynamic_slice_in_dim(qi, start, size, axis=1)
        wib = lax.dynamic_slice_in_dim(wi, start, size, axis=1)
        q_chunk = (past_len + start + jnp.arange(size)) // CHUNK
        dots = jnp.einsum('bqhe,bke->bqhk', qib, ki_all).astype(jnp.float32) * (IDX_DIM ** -0.5)
        score = jnp.einsum('bqh,bqhk->bqk', wib.astype(jnp.float32) * (IDX_HEADS ** -0.5), jax.nn.relu(dots))
        score = jnp.where((k_chunk[None, :] <= q_chunk[:, None])[None], score, NEG_INF)
        _, sel = lax.top_k(score, topk)
        valid = (sel // CHUNK) <= q_chunk[None, :, None]
        k_sel = gather_rows(k_all, sel)
        v_sel = gather_rows(v_all, sel)
        s = jnp.einsum('bqhd,bqkhd->bhqk', qb, k_sel).astype(jnp.float32) * scale
        s = jnp.where(valid[:, None], s, NEG_INF)
        p = jax.nn.softmax(s, axis=-1).astype(v_all.dtype)
        return jnp.einsum('bhqk,bqkhd->bqhd', p, v_sel)

    return sweep_query_blocks(block, n_q)


def diff_attention(q, k_all, v_all, lam, past_len):
    n_q, n_k = q.shape[1], k_all.shape[1]
    k_chunk = jnp.arange(n_k) // CHUNK
    scale = C_QK_DIM ** -0.5

    def block(start, size):
        qb = lax.dynamic_slice_in_dim(q, start, size, axis=1)
        q_chunk = (past_len + start + jnp.arange(size)) // CHUNK
        s = jnp.einsum('bqhcd,bkhcd->bhcqk', qb, k_all).astype(jnp.float32) * scale
        s = jnp.where(k_chunk[None, :] <= q_chunk[:, None], s, NEG_INF)
        p = jax.nn.softmax(s, axis=-1)
        a = (p[:, :, 0] - lam * p[:, :, 1]).astype(v_all.dtype)
        return jnp.einsum('bhqk,bkhe->bqhe', a, v_all)

    return sweep_query_blocks(block, n_q)


def even_mixer(h, w_in, w_out, b_forget, past_len, cache):
    n_b, n_l, _ = h.shape
    pos = past_len + jnp.arange(n_l)
    cuts = [int(c) for c in np.cumsum(EVEN_SPLITS)[:-1]]
    aq, ak, av, ag, af, bq, bk, bv, bg, qi, ki, wi = jnp.split(h @ w_in, cuts, axis=-1)
    c_ak, c_av, c_af, c_bk, c_bv, c_bki = cache if cache is not None else (None,) * 6
    aq = aq.reshape(n_b, n_l, A_HEADS, HEAD_DIM)
    ak = ak.reshape(n_b, n_l, A_HEADS, HEAD_DIM)
    av = av.reshape(n_b, n_l, A_HEADS, HEAD_DIM)
    logf = jax.nn.log_sigmoid(af.astype(jnp.float32) + b_forget.astype(jnp.float32))
    a_out = fox_attention(aq, with_past(c_ak, ak), with_past(c_av, av), with_past(c_af, logf), past_len)
    bq = partial_rope(bq.reshape(n_b, n_l, B_HEADS, HEAD_DIM), pos)
    bk = partial_rope(bk.reshape(n_b, n_l, B_HEADS, HEAD_DIM), pos)
    bv = bv.reshape(n_b, n_l, B_HEADS, HEAD_DIM)
    qi = partial_rope(qi.reshape(n_b, n_l, IDX_HEADS, IDX_DIM), pos)
    ki = partial_rope(ki, pos)
    b_out = dsa_attention(bq, with_past(c_bk, bk), with_past(c_bv, bv), qi, with_past(c_bki, ki), wi, past_len)
    mixed = jnp.concatenate([a_out.reshape(n_b, n_l, A_WIDTH) * jax.nn.silu(ag),
                             b_out.reshape(n_b, n_l, B_WIDTH) * jax.nn.silu(bg)], axis=-1)
    return mixed @ w_out, (ak, av, logf, bk, bv, ki)


def odd_mixer(h, w_in, lam_params, head_gain, w_out, lam_init, past_len, cache):
    n_b, n_l, _ = h.shape
    pos = past_len + jnp.arange(n_l)
    q, k, v, g = jnp.split(h @ w_in, 4, axis=-1)
    c_ck, c_cv = cache if cache is not None else (None, None)
    q = partial_rope(q.reshape(n_b, n_l, C_HEADS, 2, C_QK_DIM), pos)
    k = partial_rope(k.reshape(n_b, n_l, C_HEADS, 2, C_QK_DIM), pos)
    v = v.reshape(n_b, n_l, C_HEADS, C_V_DIM)
    lp = lam_params.astype(jnp.float32)
    lam = jnp.exp(jnp.sum(lp[0] * lp[1])) - jnp.exp(jnp.sum(lp[2] * lp[3])) + lam_init
    k_flat = k.reshape(n_b, n_l, C_HEADS, 2 * C_QK_DIM)
    k_all = with_past(c_ck, k_flat)
    k_all = k_all.reshape(n_b, k_all.shape[1], C_HEADS, 2, C_QK_DIM)
    o = diff_attention(q, k_all, with_past(c_cv, v), lam, past_len)
    o = rms_norm(o, head_gain) * (1.0 - lam_init)
    mixed = o.reshape(n_b, n_l, C_WIDTH) * jax.nn.silu(g)
    return mixed @ w_out, (k_flat, v)


def setup_inputs(seed: int = 0) -> dict:
    key = jax.random.key(seed)
    ks = jax.random.split(key, 20)

    def nrm(k, shape, s):
        return jax.random.normal(k, shape, jnp.float32) * s

    return {
        'x_prompt': nrm(ks[0], (BATCH, SEQ, D_MODEL), 1.0),
        'x_sample': nrm(ks[1], (DEC_BATCH, DEC_SEQ, D_MODEL), 1.0),
        'cache_a_k': nrm(ks[2], (N_EVEN, DEC_BATCH, PAST_LEN, A_HEADS, HEAD_DIM), 1.0),
        'cache_a_v': nrm(ks[3], (N_EVEN, DEC_BATCH, PAST_LEN, A_HEADS, HEAD_DIM), 1.0),
        'cache_a_logf': jax.nn.log_sigmoid(FORGET_BIAS + nrm(ks[4], (N_EVEN, DEC_BATCH, PAST_LEN, A_HEADS), 1.0)),
        'cache_b_k': nrm(ks[5], (N_EVEN, DEC_BATCH, PAST_LEN, B_HEADS, HEAD_DIM), 1.0),
        'cache_b_v': nrm(ks[6], (N_EVEN, DEC_BATCH, PAST_LEN, B_HEADS, HEAD_DIM), 1.0),
        'cache_b_kidx': nrm(ks[7], (N_EVEN, DEC_BATCH, PAST_LEN, IDX_DIM), 1.0),
        'cache_c_k': nrm(ks[8], (N_ODD, DEC_BATCH, PAST_LEN, C_HEADS, 2 * C_QK_DIM), 1.0),
        'cache_c_v': nrm(ks[9], (N_ODD, DEC_BATCH, PAST_LEN, C_HEADS, C_V_DIM), 1.0),
        'norm_gain': 1.0 + nrm(ks[10], (DEPTH, D_MODEL), 0.02),
        'final_gain': 1.0 + nrm(ks[11], (D_MODEL,), 0.02),
        'w_in_even': nrm(ks[12], (N_EVEN, D_MODEL, EVEN_IN), D_MODEL ** -0.5),
        'b_forget': FORGET_BIAS + nrm(ks[13], (N_EVEN, A_HEADS), 0.5),
        'w_out_even': nrm(ks[14], (N_EVEN, EVEN_MIX, D_MODEL), EVEN_MIX ** -0.5),
        'w_in_odd': nrm(ks[15], (N_ODD, D_MODEL, ODD_IN), D_MODEL ** -0.5),
        'lambda_params': nrm(ks[16], (N_ODD, 4, C_QK_DIM), 0.1),
        'c_head_gain': 1.0 + nrm(ks[17], (N_ODD, C_V_DIM), 0.02),
        'w_out_odd': nrm(ks[18], (N_ODD, C_WIDTH, D_MODEL), C_WIDTH ** -0.5),
    }


def reference(x_prompt, x_sample, cache_a_k, cache_a_v, cache_a_logf, cache_b_k, cache_b_v, cache_b_kidx,
              cache_c_k, cache_c_v, norm_gain, final_gain, w_in_even, b_forget, w_out_even, w_in_odd,
              lambda_params, c_head_gain, w_out_odd):
    past_len = cache_a_k.shape[2]
    xp, xs = x_prompt, x_sample
    even_p, even_s, odd_p, odd_s = [], [], [], []
    for layer in range(DEPTH):
        g = norm_gain[layer]
        i = layer // 2
        if layer % 2 == 0:
            dp, st_p = even_mixer(rms_norm(xp, g), w_in_even[i], w_out_even[i], b_forget[i], 0, None)
            ds, st_s = even_mixer(rms_norm(xs, g), w_in_even[i], w_out_even[i], b_forget[i], past_len,
                                  (cache_a_k[i], cache_a_v[i], cache_a_logf[i],
                                   cache_b_k[i], cache_b_v[i], cache_b_kidx[i]))
            even_p.append(st_p)
            even_s.append(st_s)
        else:
            lam_init = 0.8 - 0.6 * math.exp(-0.3 * layer)
            dp, st_p = odd_mixer(rms_norm(xp, g), w_in_odd[i], lambda_params[i], c_head_gain[i], w_out_odd[i],
                                 lam_init, 0, None)
            ds, st_s = odd_mixer(rms_norm(xs, g), w_in_odd[i], lambda_params[i], c_head_gain[i], w_out_odd[i],
                                 lam_init, past_len, (cache_c_k[i], cache_c_v[i]))
            odd_p.append(st_p)
            odd_s.append(st_s)
        xp = xp + dp
        xs = xs + ds
    y_prompt = rms_norm(xp, final_gain)
    y_sample = rms_norm(xs, final_gain)
    a_k_p, a_v_p, a_f_p, b_k_p, b_v_p, b_i_p = [jnp.stack(t) for t in zip(*even_p)]
    a_k_s, a_v_s, a_f_s, b_k_s, b_v_s, b_i_s = [jnp.stack(t) for t in zip(*even_s)]
    c_k_p, c_v_p = [jnp.stack(t) for t in zip(*odd_p)]
    c_k_s, c_v_s = [jnp.stack(t) for t in zip(*odd_s)]
    return (y_prompt, y_sample,
            a_k_p, a_v_p, a_f_p, b_k_p, b_v_p, b_i_p, c_k_p, c_v_p,
            a_k_s, a_v_s, a_f_s, b_k_s, b_v_s, b_i_s, c_k_s, c_v_s)
```

```python
import math
from contextlib import ExitStack
import numpy as np
import concourse.bass as bass
import concourse.mybir as mybir
from concourse.bass_utils import run_bass_kernel_spmd

F32 = mybir.dt.float32
BF16 = mybir.dt.bfloat16
AF = mybir.ActivationFunctionType
ALU = mybir.AluOpType
AX = mybir.AxisListType

ENGS = ("pe", "act", "dve", "pool", "sp")
N_DMA_SLOTS = 12


class Sched:
    def __init__(self, nc, stack):
        self.nc = nc
        self.ops = []
        self.start = 0
        self.res = {}
        self.dma_count = {e: 0 for e in ENGS}
        self.dma_slot_last = {}
        self.cnt = {e: 0 for e in ENGS}
        self.prev_final = {}
        self.excl = set()
        self.sems = {e: stack.enter_context(nc.semaphore("s_" + e)) for e in ENGS}
        self.dsems = {}
        for q in ("sp", "pool"):
            for sl in range(N_DMA_SLOTS):
                self.dsems[(q, sl)] = stack.enter_context(nc.semaphore(f"d_{q}_{sl}"))

    def _deps_for(self, reads, writes):
        deps = set()
        for r in reads:
            st = self.res.get(r)
            if st:
                deps.update(st["w"].values())
        for w in writes:
            st = self.res.get(w)
            if st:
                deps.update(st["w"].values())
                deps.update(st["r"])
        return deps

    def _commit(self, opid, key, reads, writes):
        for r in reads:
            st = self.res.setdefault(r, {"w": {}, "r": []})
            st["r"].append(opid)
        for w in writes:
            self.res[w] = {"w": {key: opid}, "r": []}

    def _split(self, reads, writes):
        ex = [r for r in reads if r in self.excl]
        if ex:
            reads = [r for r in reads if r not in self.excl]
            writes = list(writes) + [r for r in ex if r not in writes]
        return reads, writes

    def add(self, eng, fn, reads=(), writes=(), extra_deps=()):
        reads, writes = self._split(reads, writes)
        deps = self._deps_for(reads, writes)
        deps.update(extra_deps)
        opid = len(self.ops)
        self.ops.append(dict(id=opid, eng=eng, fn=fn, deps=deps, dma=False, signal=False))
        self._commit(opid, eng, reads, writes)
        return opid

    def dma(self, queue, fn, reads=(), writes=(), extra_deps=()):
        reads, writes = self._split(reads, writes)
        deps = self._deps_for(reads, writes)
        deps.update(extra_deps)
        n = self.dma_count[queue]
        self.dma_count[queue] += 1
        slot = n % N_DMA_SLOTS
        prev = self.dma_slot_last.get((queue, slot))
        if prev is not None:
            deps.add(prev)
        opid = len(self.ops)
        self.ops.append(dict(id=opid, eng=queue, fn=fn, deps=deps, dma=True, slot=slot,
                             dma_val=16 * (n // N_DMA_SLOTS + 1), signal=True))
        self.dma_slot_last[(queue, slot)] = opid
        self._commit(opid, ("dma", queue, slot), reads, writes)
        return opid

    def flush(self):
        nc = self.nc
        ops = self.ops
        start = self.start
        ph = ops[start:]
        eng_ops = {e: [o for o in ph if o["eng"] == e] for e in ENGS}
        for op in ph:
            for d in op["deps"]:
                if d < start:
                    continue
                p = ops[d]
                if p["dma"]:
                    continue
                if p["eng"] == op["eng"] and p["eng"] == "pe":
                    continue
                p["signal"] = True
        for e in ENGS:
            comp = [o for o in eng_ops[e] if not o["dma"]]
            if comp:
                comp[-1]["signal"] = True
        for e in ENGS:
            for op in eng_ops[e]:
                if op["dma"]:
                    continue
                if op["signal"]:
                    self.cnt[e] += 1
                    op["sig"] = self.cnt[e]
        sems, dsems = self.sems, self.dsems
        prev_final = dict(self.prev_final)
        engobj = {"pe": "tensor", "act": "scalar", "dve": "vector", "pool": "gpsimd", "sp": "sync"}
        with nc.Block() as block:
            def make(e):
                def body(eng):
                    waited = {}
                    for k, v in prev_final.items():
                        sem = sems[k[1]] if k[0] == "c" else dsems[(k[1], k[2])]
                        eng.wait_ge(sem, v)
                        waited[k] = v
                    for op in eng_ops[e]:
                        need = {}
                        for d in op["deps"]:
                            if d < start:
                                continue
                            p = ops[d]
                            if p["dma"]:
                                k = ("d", p["eng"], p["slot"])
                                v = p["dma_val"]
                            else:
                                if p["eng"] == e and e == "pe":
                                    continue
                                k = ("c", p["eng"])
                                v = p["sig"]
                            if v > need.get(k, 0):
                                need[k] = v
                        for k, v in need.items():
                            if waited.get(k, 0) >= v:
                                continue
                            waited[k] = v
                            sem = sems[k[1]] if k[0] == "c" else dsems[(k[1], k[2])]
                            eng.wait_ge(sem, v)
                        inst = op["fn"](eng)
                        if op["dma"]:
                            inst.then_inc(dsems[(e, op["slot"])], 16)
                        elif op["signal"]:
                            inst.then_inc(sems[e], 1)
                    for (q, sl), opid in self.dma_slot_last.items():
                        if q == e and opid >= start:
                            v = ops[opid]["dma_val"]
                            if waited.get(("d", q, sl), 0) < v:
                                eng.wait_ge(dsems[(q, sl)], v)
                return body

            for e in ENGS:
                if eng_ops[e] or prev_final:
                    getattr(block, engobj[e])(make(e))
        for e in ENGS:
            if self.cnt[e]:
                self.prev_final[("c", e)] = self.cnt[e]
        for (q, sl), opid in self.dma_slot_last.items():
            self.prev_final[("d", q, sl)] = ops[opid]["dma_val"]
        self.start = len(ops)
        self.res = {}
        for op in ph:
            op["fn"] = None


def _mk(name, *a, **k):
    return lambda e: getattr(e, name)(*a, **k)


class Rot:
    def __init__(self, items):
        self.items = items
        self.i = 0

    def next(self):
        it = self.items[self.i % len(self.items)]
        self.i += 1
        return it


D = 1024
NEGM = -30000.0
LAM_INIT = {1: 0.8 - 0.6 * math.exp(-0.3 * 1), 3: 0.8 - 0.6 * math.exp(-0.3 * 3)}
EVEN_COLS = dict(aq=0, ak=512, av=1024, ag=1536, af=2048, bq=2056, bk=2568, bv=3080, bg=3592,
                 qi=4104, ki=4616, wi=4680)
NIT = 20


def build(cfg):
    NP = cfg["NP"]
    NL = cfg["NL"]
    PAST = cfg["PAST"]
    TOPK_P = cfg["TOPK_P"]
    TOPK_S = cfg["TOPK_S"]
    NE, NO = (NL + 1) // 2, NL // 2
    NTP = NP // 128
    NPT = PAST // 128
    NB = NP // 512
    NKS = PAST + 32

    nc = bass.Bass("TRN2", target_bir_lowering=False)

    def din(name, shape, dt=F32):
        return nc.dram_tensor(name, list(shape), dt, kind="ExternalInput").ap()

    def dout(name, shape, dt=F32):
        return nc.dram_tensor(name, list(shape), dt, kind="ExternalOutput").ap()

    def dscr(name, shape, dt=F32):
        return nc.dram_tensor(name, list(shape), dt, kind="Internal").ap()

    I = {}
    I["xp"] = din("xp", [NP, D])
    I["xs"] = din("xs", [128, D])
    I["ca_k"] = din("ca_k", [NE, 4, PAST, 512]); I["ca_v"] = din("ca_v", [NE, 4, PAST, 512])
    I["ca_f"] = din("ca_f", [NE, 4, PAST, 8])
    I["cb_k"] = din("cb_k", [NE, 4, PAST, 512]); I["cb_v"] = din("cb_v", [NE, 4, PAST, 512])
    I["cb_ki"] = din("cb_ki", [NE, 4, PAST, 64])
    if NO:
        I["cc_k"] = din("cc_k", [NO, 4, PAST, 1024]); I["cc_v"] = din("cc_v", [NO, 4, PAST, 1024])
        I["w_in_odd"] = din("w_in_odd", [NO, D, 4096]); I["w_out_odd"] = din("w_out_odd", [NO, D, D])
        I["lam"] = din("lam", [NO, 4, 64]); I["chg"] = din("chg", [NO, 128])
    I["w_in_even"] = din("w_in_even", [NE, D, 4688]); I["w_out_even"] = din("w_out_even", [NE, D, D])
    I["norm_gain"] = din("norm_gain", [NL, D]); I["final_gain"] = din("final_gain", [D])
    I["b_forget"] = din("b_forget", [NE, 8])
    I["ident"] = din("ident", [128, 128])
    I["cosp"] = din("cosp", [NP, 8]); I["sinp"] = din("sinp", [NP, 8])
    I["coss"] = din("coss", [128, 8]); I["sins"] = din("sins", [128, 8])
    I["cmask"] = din("cmask", [4, 128, 512]); I["ccmask"] = din("ccmask", [4, 128, 512])
    I["idxmask"] = din("idxmask", [128, 128]); I["cmask_s"] = din("cmask_s", [32, 32])
    I["ut"] = din("ut", [128, 128]); I["pow2"] = din("pow2", [128, NIT])

    O = {}
    O["y_p"] = dout("y_p", [NP, D]); O["y_s"] = dout("y_s", [128, D])
    for g, T in (("p", NP), ("s", 128)):
        O["a_k_" + g] = dout("oa_k_" + g, [NE, T, 512]); O["a_v_" + g] = dout("oa_v_" + g, [NE, T, 512])
        O["a_f_" + g] = dout("oa_f_" + g, [NE, T, 8])
        O["b_k_" + g] = dout("ob_k_" + g, [NE, T, 512]); O["b_v_" + g] = dout("ob_v_" + g, [NE, T, 512])
        O["b_ki_" + g] = dout("ob_ki_" + g, [NE, T, 64])
        if NO:
            O["c_k_" + g] = dout("oc_k_" + g, [NO, T, 1024]); O["c_v_" + g] = dout("oc_v_" + g, [NO, T, 1024])

    SC = {}
    for g, T in (("p", NP), ("s", 128)):
        NT = T // 128
        SC["resid_" + g] = dscr("resid_" + g, [T, D])
        for nm in ("qTa", "kTa", "gTa", "qTb", "kTb", "gTb", "qiT"):
            SC[nm + "_" + g] = dscr(nm + "_" + g, [512, T], BF16)
        SC["kiT_" + g] = dscr("kiT_" + g, [64, T], BF16)
        SC["va_" + g] = dscr("va_" + g, [8, 128, NT, 64], BF16)
        SC["vb_" + g] = dscr("vb_" + g, [8, 128, NT, 64], BF16)
        SC["wi_" + g] = dscr("wi_" + g, [T, 8])
        SC["cqa_" + g] = dscr("cqa_" + g, [8, 2, T], BF16)
        SC["nck_" + g] = dscr("nck_" + g, [T, 8])
        for nm in ("qTc", "kTc", "gTc"):
            SC[nm + "_" + g] = dscr(nm + "_" + g, [1024, T], BF16)
        SC["vc_" + g] = dscr("vc_" + g, [8, 128, NT, 128], BF16)

    top = ExitStack()
    S = Sched(nc, top)

    uniq = [0]

    def sb(stack, name, shape, dt):
        uniq[0] += 1
        return stack.enter_context(nc.sbuf_tensor(f"{name}_{uniq[0]}", list(shape), dt))

    def psb(stack, name, shape, dt):
        S.excl.add(name)
        uniq[0] += 1
        return stack.enter_context(nc.psum_tensor(f"{name}_{uniq[0]}", list(shape), dt))

    identf = sb(top, "identf", [128, 128], F32)
    identb = sb(top, "identb", [128, 128], BF16)
    onesb = sb(top, "onesb", [128, 128], BF16)
    onesdiv = sb(top, "onesdiv", [128, 128], BF16)
    onesf = sb(top, "onesf", [128, 128], F32)
    utf = sb(top, "utf", [128, 128], F32)
    pow2 = sb(top, "pow2s", [128, NIT], F32)
    cmask_s = sb(top, "cmask_ss", [32, 32], BF16)
    idxmask = sb(top, "idxmasks", [128, 128], F32)
    ones8 = sb(top, "ones8", [8, 512], F32)

    S.dma("sp", _mk("dma_start", out=identf[:], in_=I["ident"][:, :]), writes=["identf"])
    S.dma("sp", _mk("dma_start", out=utf[:], in_=I["ut"][:, :]), writes=["const"])
    S.dma("sp", _mk("dma_start", out=pow2[:], in_=I["pow2"][:, :]), writes=["const"])
    S.dma("sp", _mk("dma_start", out=idxmask[:], in_=I["idxmask"][:, :]), writes=["const"])
    S.dma("pool", _mk("dma_start", out=cmask_s[:], in_=I["cmask_s"][:, :]), writes=["const"])
    S.add("dve", _mk("tensor_copy", out=identb[:], in_=identf[:]), reads=["identf"], writes=["const"])
    S.add("dve", _mk("memset", onesb[:], 1.0), writes=["const"])
    S.add("dve", _mk("memset", onesdiv[:], 1.0 / 128.0), writes=["const"])
    S.add("dve", _mk("memset", onesf[:], 1.0), writes=["const"])
    S.add("dve", _mk("memset", ones8[:], 1.0), writes=["const"])
    S.flush()

    GRP = {
        "p": dict(T=NP, TB=512),
        "s": dict(T=128, TB=128),
    }

    def phase_A(L):
        even = (L % 2 == 0)
        li = L // 2
        NCOL = 4688 if even else 4096
        win = (I["w_in_even"] if even else I["w_in_odd"])[li]
        with ExitStack() as ph:
            W = sb(ph, "W", [128, 8, NCOL], BF16)
            gb = sb(ph, "gb", [128, D], F32)
            cosp = sb(ph, "cosps", [128, NTP, 8], F32); sinp = sb(ph, "sinps", [128, NTP, 8], F32)
            coss = sb(ph, "cosss", [128, 1, 8], F32); sins = sb(ph, "sinss", [128, 1, 8], F32)
            GRP["p"]["cos"], GRP["p"]["sin"] = cosp, sinp
            GRP["s"]["cos"], GRP["s"]["sin"] = coss, sins
            S.dma("sp", _mk("dma_start", out=cosp[:], in_=I["cosp"].rearrange("(j p) e -> p j e", p=128)), writes=["rope"])
            S.dma("sp", _mk("dma_start", out=sinp[:], in_=I["sinp"].rearrange("(j p) e -> p j e", p=128)), writes=["rope"])
            S.dma("sp", _mk("dma_start", out=coss[:, 0, :], in_=I["coss"][:, :]), writes=["rope"])
            S.dma("sp", _mk("dma_start", out=sins[:, 0, :], in_=I["sins"][:, :]), writes=["rope"])
            xt = Rot([(sb(ph, f"xt{k}", [128, D], F32), f"xt{k}") for k in range(2)])
            junk = sb(ph, "junkA", [128, D], BF16)
            small = Rot([(sb(ph, f"smA{k}", [128, 2], F32), f"smA{k}") for k in range(2)])
            hb = Rot([(sb(ph, f"hb{k}", [128, D], BF16), f"hb{k}") for k in range(2)])
            hT = sb(ph, "hT", [128, 8, 512], BF16)
            st = Rot([(sb(ph, f"st{k}", [128, 512], F32), f"st{k}") for k in range(3)])
            kb = Rot([(sb(ph, f"kb{k}", [128, 512], BF16), f"kb{k}") for k in range(3)])
            kTst = Rot([(sb(ph, f"kTst{k}", [128, 4, 128], BF16), f"kTst{k}") for k in range(3)])
            fmst = Rot([(sb(ph, f"fmst{k}", [128, 512], BF16), f"fmst{k}") for k in range(3)])
            rtmp = Rot([(sb(ph, f"rtmp{k}", [128, 8, 8], F32), f"rtmp{k}") for k in range(4)])
            negb = sb(ph, "negb", [8, 1], F32)
            gainp = sb(ph, "gainp", [128, 1], F32)
            lfe = sb(ph, "lfe", [8, 512], F32)
            lfT = sb(ph, "lfT", [8, 512], F32)
            cumT = sb(ph, "cumT", [8, 512], F32)
            carry = sb(ph, "carry", [8, 1], F32)
            chi = sb(ph, "chi", [8, 512], BF16)
            clo = sb(ph, "clo", [8, 512], BF16)
            chf = sb(ph, "chf", [8, 512], F32)
            lftm = sb(ph, "lftm", [128, 8, 8], F32)
            pT = psb(ph, "pT", [128, 8, 128], BF16)
            ptm = Rot([(psb(ph, f"ptm{k}", [128, 512], F32), f"ptm{k}") for k in range(2)])
            pfm = Rot([(psb(ph, f"pfm{k}", [128, 512], F32), f"pfm{k}") for k in range(2)])
            ptr = Rot([(psb(ph, f"ptr{k}", [128, 8, 128], BF16), f"ptr{k}") for k in range(2)])
            plt = psb(ph, "plt", [128, 64, 8], F32)

            wsrc = win.rearrange("(kc p) n -> p kc n", p=128)
            for c0 in range(0, NCOL, 512):
                c1 = min(NCOL, c0 + 512)
                S.dma("pool", _mk("dma_start", out=W[:, :, c0:c1], in_=wsrc[:, :, c0:c1]),
                      writes=[("W", c0 // 512)])

            def wres(c0, n):
                return [("W", k) for k in range(c0 // 512, (c0 + n - 1) // 512 + 1)]

            S.dma("sp", _mk("dma_start", out=gb[:], in_=I["norm_gain"][L].partition_broadcast(128)), writes=["gb"])
            if even:
                S.dma("sp", _mk("dma_start", out=negb[:], in_=I["b_forget"][li].unsqueeze(1)), writes=["negb"])
                S.add("dve", _mk("tensor_scalar", out=negb[:], in0=negb[:], scalar1=-1.0, scalar2=None, op0=ALU.mult),
                      reads=["negb"], writes=["negb"])
            else:
                S.dma("sp", _mk("dma_start", out=gainp[:], in_=I["chg"][li].unsqueeze(1)), writes=["gainp"])
                S.add("dve", _mk("tensor_scalar", out=gainp[:], in0=gainp[:], scalar1=1.0 - LAM_INIT[L], scalar2=None,
                                                        op0=ALU.mult), reads=["gainp"], writes=["gainp"])

            dbg = cfg.get("dbg", "")
            for g in ("p", "s"):
                if g == "s" and "nosample" in dbg:
                    continue
                G = GRP[g]
                T, TB = G["T"], G["TB"]
                nsub = TB // 128
                xsrc = (I["xp"] if g == "p" else I["xs"]) if L == 0 else SC["resid_" + g]
                if even:
                    S.add("dve", _mk("memset", carry[:], 0.0), writes=["carry"])
                    tm_chunks = [
                        dict(c0=EVEN_COLS["ak"], n=512, kind="K", rope=False, out=O["a_k_" + g][li], scr=SC["kTa_" + g], scale=1.0),
                        dict(c0=EVEN_COLS["av"], n=512, kind="V", rope=False, out=O["a_v_" + g][li], scr=SC["va_" + g], dv=64),
                        dict(c0=EVEN_COLS["bq"], n=512, kind="Q", rope=True, out=None, scr=SC["qTb_" + g], scale=0.125),
                        dict(c0=EVEN_COLS["bk"], n=512, kind="K", rope=True, out=O["b_k_" + g][li], scr=SC["kTb_" + g], scale=1.0),
                        dict(c0=EVEN_COLS["bv"], n=512, kind="V", rope=False, out=O["b_v_" + g][li], scr=SC["vb_" + g], dv=64),
                        dict(c0=EVEN_COLS["qi"], n=512, kind="Q", rope=True, out=None, scr=SC["qiT_" + g], scale=0.125),
                        dict(c0=EVEN_COLS["ki"], n=64, kind="K", rope=True, out=O["b_ki_" + g][li], scr=SC["kiT_" + g], scale=1.0),
                        dict(c0=EVEN_COLS["wi"], n=8, kind="W", rope=False, out=SC["wi_" + g], scr=None),
                    ]
                    fm_tiles = [dict(c0=EVEN_COLS["aq"] + 128 * k, n=128, kind="q", scr=SC["qTa_" + g], r0=128 * k) for k in range(4)]
                    fm_tiles += [dict(c0=EVEN_COLS["ag"] + 128 * k, n=128, kind="g", scr=SC["gTa_" + g], r0=128 * k) for k in range(4)]
                    fm_tiles += [dict(c0=EVEN_COLS["bg"] + 128 * k, n=128, kind="g", scr=SC["gTb_" + g], r0=128 * k) for k in range(4)]
                    fm_tiles += [dict(c0=EVEN_COLS["af"], n=8, kind="f")]
                else:
                    tm_chunks = []
                    for k in range(2):
                        tm_chunks.append(dict(c0=512 * k, n=512, kind="Q", rope=True, out=None, scr=SC["qTc_" + g], scale=0.125, r0=512 * k))
                    for k in range(2):
                        tm_chunks.append(dict(c0=1024 + 512 * k, n=512, kind="K", rope=True, out=O["c_k_" + g][li], scr=SC["kTc_" + g],
                                              scale=1.0, r0=512 * k, oc0=512 * k))
                    for k in range(2):
                        tm_chunks.append(dict(c0=2048 + 512 * k, n=512, kind="V", rope=False, out=O["c_v_" + g][li], scr=SC["vc_" + g],
                                              dv=128, h0=4 * k, oc0=512 * k))
                    fm_tiles = [dict(c0=3072 + 128 * k, n=128, kind="go", scr=SC["gTc_" + g], r0=128 * k) for k in range(8)]

                for tb in range(T // TB):
                    t0 = tb * TB
                    for s in range(nsub):
                        r0 = t0 + s * 128
                        x_, xn = xt.next()
                        S.dma("sp", _mk("dma_start", out=x_[:], in_=xsrc[r0:r0 + 128, :]), writes=[xn])
                        sm, smn = small.next()
                        S.add("act", _mk("activation", out=junk[:], in_=x_[:], func=AF.Square, accum_out=sm[:, 0:1]),
                              reads=[xn], writes=["junkA", smn])
                        S.add("dve", _mk("tensor_scalar", out=sm[:, 1:2], in0=sm[:, 0:1], scalar1=1.0 / D, scalar2=1e-6,
                                                                       op0=ALU.mult, op1=ALU.add), reads=[smn], writes=[smn])
                        S.add("act", _mk("activation", out=sm[:, 1:2], in_=sm[:, 1:2], func=AF.Sqrt), reads=[smn], writes=[smn])
                        S.add("dve", _mk("reciprocal", out=sm[:, 1:2], in_=sm[:, 1:2]), reads=[smn], writes=[smn])
                        h_, hn = hb.next()
                        S.add("dve", _mk("scalar_tensor_tensor", out=h_[:], in0=x_[:], scalar=sm[:, 1:2], in1=gb[:],
                                                                                         op0=ALU.mult, op1=ALU.mult),
                              reads=[xn, smn, "gb"], writes=[hn])
                        for kc in range(8):
                            S.add("pe", _mk("transpose", out=pT[:, kc, :], in_=h_[:, kc * 128:(kc + 1) * 128], identity=identb[:]),
                                  reads=[hn], writes=["pT"])
                        S.add("act", _mk("copy", out=hT[:, :, s * 128:(s + 1) * 128], in_=pT[:]), reads=["pT"], writes=[("hT", s)])
                    hTall = [("hT", s) for s in range(nsub)]

                    for s in range(nsub if "notm" not in dbg else 0):
                        r0 = t0 + s * 128
                        jt = r0 // 128
                        for ch in tm_chunks:
                            n = ch["n"]; c0 = ch["c0"]
                            ps, psn = ptm.next()
                            for kc in range(8):
                                S.add("pe", _mk("matmul",
                                    ps[:, 0:n], lhsT=hT[:, kc, s * 128:(s + 1) * 128], rhs=W[:, kc, c0:c0 + n], start=(kc == 0), stop=(kc == 7)),
                                    reads=[("hT", s)] + wres(c0, n), writes=[psn])
                            st_, stn = st.next()
                            S.add("act", _mk("copy", out=st_[:, 0:n], in_=ps[:, 0:n]), reads=[psn], writes=[stn])
                            if ch["rope"] and "norope" not in dbg:
                                nh = n // 64
                                psv = ps[:, 0:n].rearrange("p (h d) -> p h d", d=64)
                                stv = st_[:, 0:n].rearrange("p (h d) -> p h d", d=64)
                                cosb = G["cos"][:, jt, :].unsqueeze(1).broadcast_to([128, nh, 8])
                                sinb = G["sin"][:, jt, :].unsqueeze(1).broadcast_to([128, nh, 8])
                                x1 = psv[:, :, 0:8]; x2 = psv[:, :, 8:16]
                                ta, tan = rtmp.next(); tbb, tbn = rtmp.next()
                                S.add("dve", _mk("tensor_tensor", out=ta[:, 0:nh, :], in0=x1, in1=cosb, op=ALU.mult),
                                      reads=[psn, "rope"], writes=[tan])
                                S.add("dve", _mk("tensor_tensor", out=tbb[:, 0:nh, :], in0=x2, in1=sinb, op=ALU.mult),
                                      reads=[psn, "rope"], writes=[tbn])
                                S.add("dve", _mk("tensor_tensor", out=stv[:, :, 0:8], in0=ta[:, 0:nh, :], in1=tbb[:, 0:nh, :],
                                                                                                   op=ALU.subtract), reads=[tan, tbn], writes=[stn])
                                ta, tan = rtmp.next(); tbb, tbn = rtmp.next()
                                S.add("dve", _mk("tensor_tensor", out=ta[:, 0:nh, :], in0=x2, in1=cosb, op=ALU.mult),
                                      reads=[psn, "rope"], writes=[tan])
                                S.add("dve", _mk("tensor_tensor", out=tbb[:, 0:nh, :], in0=x1, in1=sinb, op=ALU.mult),
                                      reads=[psn, "rope"], writes=[tbn])
                                S.add("dve", _mk("tensor_tensor", out=stv[:, :, 8:16], in0=ta[:, 0:nh, :], in1=tbb[:, 0:nh, :],
                                                                                                   op=ALU.add), reads=[tan, tbn], writes=[stn])
                            kind = ch["kind"]
                            if ch["out"] is not None:
                                oc0 = ch.get("oc0", 0)
                                S.dma("pool", _mk("dma_start", out=ch["out"][r0:r0 + 128, oc0:oc0 + n], in_=st_[:, 0:n]),
                                      reads=[stn])
                            if kind in ("K", "Q"):
                                kb_, kbn = kb.next()
                                S.add("act", _mk("activation", out=kb_[:, 0:n], in_=st_[:, 0:n], func=AF.Copy, scale=ch["scale"]),
                                      reads=[stn], writes=[kbn])
                                ng = (n + 127) // 128
                                w_ = min(n, 128)
                                pt_, ptn = ptr.next()
                                for gi in range(ng):
                                    S.add("pe", _mk("transpose", out=pt_[0:w_, gi, :], in_=kb_[:, gi * 128:gi * 128 + w_],
                                                                                                   identity=identb[:]), reads=[kbn], writes=[ptn])
                                kt_, ktn = kTst.next()
                                S.add("dve", _mk("tensor_copy", out=kt_[0:w_, 0:ng, :], in_=pt_[0:w_, 0:ng, :]),
                                      reads=[ptn], writes=[ktn])
                                rr0 = ch.get("r0", 0)
                                if n >= 128:
                                    dst = ch["scr"][rr0:rr0 + n, r0:r0 + 128].rearrange("(g p) t -> p g t", p=128)
                                    S.dma("pool", _mk("dma_start", out=dst, in_=kt_[:, 0:ng, :]), reads=[ktn])
                                else:
                                    dst = ch["scr"][0:n, r0:r0 + 128]
                                    S.dma("pool", _mk("dma_start", out=dst, in_=kt_[0:n, 0, :]), reads=[ktn])
                            elif kind == "V":
                                kb_, kbn = kb.next()
                                S.add("act", _mk("copy", out=kb_[:, 0:n], in_=st_[:, 0:n]), reads=[stn], writes=[kbn])
                                dv = ch["dv"]; h0 = ch.get("h0", 0); nh = n // dv
                                dst = ch["scr"][h0:h0 + nh, :, jt, :].rearrange("h p d -> p h d")
                                S.dma("pool", _mk("dma_start", out=dst, in_=kb_[:, 0:n].rearrange("p (h d) -> p h d", d=dv)),
                                      reads=[kbn])

                    for ft in fm_tiles:
                        if "nofm" in dbg or ("nof" in dbg and ft["kind"] == "f"):
                            continue
                        n = ft["n"]; c0 = ft["c0"]
                        ps, psn = pfm.next()
                        for kc in range(8):
                            S.add("pe", _mk("matmul", ps[0:n, 0:TB], lhsT=W[:, kc, c0:c0 + n], rhs=hT[:, kc, 0:TB],
                                                                                  start=(kc == 0), stop=(kc == 7)),
                                  reads=hTall + wres(c0, n), writes=[psn])
                        kind = ft["kind"]
                        if kind in ("q", "g", "go"):
                            fs, fsn = fmst.next()
                            if kind == "q":
                                S.add("act", _mk("activation", out=fs[:, 0:TB], in_=ps[:, 0:TB], func=AF.Copy, scale=0.125),
                                      reads=[psn], writes=[fsn])
                            else:
                                S.add("act", _mk("activation", out=fs[:, 0:TB], in_=ps[:, 0:TB], func=AF.Silu), reads=[psn], writes=[fsn])
                                if kind == "go":
                                    S.add("dve", _mk("tensor_scalar", out=fs[:, 0:TB], in0=fs[:, 0:TB], scalar1=gainp[:, 0:1], scalar2=None,
                                                                                   op0=ALU.mult), reads=[fsn, "gainp"], writes=[fsn])
                            rr0 = ft["r0"]
                            S.dma("pool", _mk("dma_start", out=ft["scr"][rr0:rr0 + 128, t0:t0 + TB], in_=fs[:, 0:TB]),
                                  reads=[fsn])
                        else:
                            S.add("act", _mk("activation", out=lfe[:, 0:TB], in_=ps[0:8, 0:TB], func=AF.Exp, scale=-1.0, bias=negb[:, 0:1]),
                                  reads=[psn, "negb"], writes=["lfe"])
                            S.add("act", _mk("activation", out=lfe[:, 0:TB], in_=lfe[:, 0:TB], func=AF.Ln, bias=1.0, scale=1.0),
                                  reads=["lfe"], writes=["lfe"])
                            S.add("dve", _mk("tensor_scalar", out=lfT[:, 0:TB], in0=lfe[:, 0:TB], scalar1=-1.0, scalar2=None, op0=ALU.mult),
                                  reads=["lfe"], writes=["lfT"])
                            for s in range(nsub):
                                S.add("pe", _mk("transpose", out=plt[:, s, :], in_=lfT[0:8, s * 128:(s + 1) * 128], identity=identf[0:8, 0:8]),
                                      reads=["lfT"], writes=["plt"])
                            if g == "p":
                                S.add("dve", _mk("tensor_tensor_scan", out=cumT[:, 0:TB], data0=ones8[:, 0:TB], data1=lfT[:, 0:TB], initial=carry[:, 0:1],
                                                                              op0=ALU.mult, op1=ALU.add), reads=["lfT", "carry"], writes=["cumT"])
                                S.add("dve", _mk("tensor_copy", out=carry[:, 0:1], in_=cumT[:, TB - 1:TB]), reads=["cumT"], writes=["carry"])
                                S.add("dve", _mk("tensor_copy", out=chi[:, 0:TB], in_=cumT[:, 0:TB]), reads=["cumT"], writes=["chi"])
                                S.add("dve", _mk("tensor_copy", out=chf[:, 0:TB], in_=chi[:, 0:TB]), reads=["chi"], writes=["chf"])
                                S.add("dve", _mk("tensor_tensor", out=clo[:, 0:TB], in0=cumT[:, 0:TB], in1=chf[:, 0:TB], op=ALU.subtract),
                                      reads=["cumT", "chf"], writes=["clo"])
                                S.dma("pool", _mk("dma_start", out=SC["cqa_p"][:, 0, t0:t0 + TB], in_=chi[:, 0:TB]), reads=["chi"])
                                S.dma("pool", _mk("dma_start", out=SC["cqa_p"][:, 1, t0:t0 + TB], in_=clo[:, 0:TB]), reads=["clo"])
                                for s in range(nsub):
                                    S.add("pe", _mk("transpose", out=plt[:, 4 + s, :], in_=cumT[0:8, s * 128:(s + 1) * 128], identity=identf[0:8, 0:8]),
                                          reads=["cumT"], writes=["plt"])
                            S.add("dve", _mk("tensor_copy", out=lftm[:, 0:nsub, :], in_=plt[:, 0:nsub, :]), reads=["plt"], writes=["lftm"])
                            S.dma("pool", _mk("dma_start", out=O["a_f_" + g][li][t0:t0 + TB, :].rearrange("(s p) h -> p s h", p=128),
                                                                         in_=lftm[:, 0:nsub, :]), reads=["lftm"])
                            if g == "p":
                                S.add("dve", _mk("tensor_scalar", out=lftm[:, 4:4 + nsub, :], in0=plt[:, 4:4 + nsub, :], scalar1=-1.0, scalar2=None, op0=ALU.mult),
                                      reads=["plt"], writes=["lftm2"])
                                S.dma("pool", _mk("dma_start", out=SC["nck_p"][t0:t0 + TB, :].rearrange("(s p) h -> p s h", p=128),
                                                                        in_=lftm[:, 4:4 + nsub, :]), reads=["lftm2"])
                            else:
                                S.dma("pool", _mk("dma_start", out=SC["nck_s"][0:128, :], in_=lftm[:, 0, :]), reads=["lftm"])
            S.flush()

    def phase_B(L):
        even = (L % 2 == 0)
        li = L // 2
        last = (L == NL - 1)
        wout = (I["w_out_even"] if even else I["w_out_odd"])[li]
        NKMAX = max(NP, ((NKS + 127) // 128) * 128)
        with ExitStack() as ph:
            wo = sb(ph, "wo", [128, 8, D], BF16)
            gbf = sb(ph, "gbf", [128, D], F32)
            cmask = sb(ph, "cmasks", [128, 4, 512], BF16)
            pbuf = Rot([(sb(ph, f"pbuf{k}", [128, 512], BF16), f"pbuf{k}") for k in range(3)])
            qTt = Rot([(sb(ph, f"qTt{k}", [66, 512], BF16), f"qTt{k}") for k in range(2)])
            gt = Rot([(sb(ph, f"gt{k}", [128, 512], BF16), f"gt{k}") for k in range(2)])
            ftmp = Rot([(sb(ph, f"ftmp{k}", [128, 512], F32), f"ftmp{k}") for k in range(4)])
            sqb = sb(ph, "sqb", [128, 512], BF16)
            mix = sb(ph, "mix", [128, 8, 512], BF16)
            xo = Rot([(sb(ph, f"xo{k}", [128, D], F32), f"xo{k}") for k in range(1)])
            xn_ = Rot([(sb(ph, f"xn{k}", [128, D], F32), f"xn{k}") for k in range(1)])
            junkB = sb(ph, "junkB", [128, D], BF16)
            smallB = Rot([(sb(ph, f"smB{k}", [128, 2], F32), f"smB{k}") for k in range(2)])
            nck = sb(ph, "nck", [128, max(NTP, NPT + 1), 8], F32)
            neglam = sb(ph, "neglam", [128, 1], F32)
            lp = sb(ph, "lp", [128, 4, 64], F32)
            lpt = sb(ph, "lpt", [128, 64], F32)
            lps = sb(ph, "lps", [128, 2], F32)
            if even:
                qit = Rot([(sb(ph, f"qit{k}", [64, 8, 128], BF16), f"qit{k}") for k in range(2)])
                wit = Rot([(sb(ph, f"wit{k}", [128, 8], F32), f"wit{k}") for k in range(2)])
                diagw = sb(ph, "diagw", [128, 8, 128], BF16)
                rbuf = Rot([(sb(ph, f"rbuf{k}", [128, 512], BF16), f"rbuf{k}") for k in range(4)])
                bis = sb(ph, "bis", [128, 8], F32)
                hwtab = sb(ph, "hwtab", [128, NIT], F32)
            B = {}
            sbank = Rot([(psb(ph, f"sbank{k}", [128, 512], F32), f"sbank{k}") for k in range(2)])
            acc = [(psb(ph, f"acc{k}", [128, 512], F32), f"acc{k}") for k in range(4)]
            ptrB = psb(ph, "ptrB", [128, 8, 128], BF16)
            pmisc = psb(ph, "pmisc", [128, 512], F32)

            S.dma("pool", _mk("dma_start", out=wo[:], in_=wout.rearrange("(kc p) n -> p kc n", p=128)), writes=["wo"])
            if last:
                S.dma("sp", _mk("dma_start", out=gbf[:], in_=I["final_gain"].partition_broadcast(128)), writes=["gbf"])
            S.dma("pool", _mk("dma_start", out=cmask[:], in_=(I["cmask"] if even else I["ccmask"]).rearrange("j p q -> p j q")), writes=["const"])
            if even:
                S.dma("sp", _mk("dma_start", out=nck[:, 0:NTP, :], in_=SC["nck_p"].rearrange("(j p) h -> p j h", p=128)), writes=["nck"])
            else:
                S.dma("sp", _mk("dma_start", out=lp[:].rearrange("p a b -> p (a b)"),
                                                  in_=I["lam"][li].rearrange("a b -> (a b)").partition_broadcast(128)), writes=["lp"])
                for k in range(2):
                    S.add("dve", _mk("tensor_tensor", out=lpt[:], in0=lp[:, 2 * k, :], in1=lp[:, 2 * k + 1, :], op=ALU.mult),
                          reads=["lp"], writes=["lpt"])
                    S.add("dve", _mk("tensor_reduce", out=lps[:, k:k + 1], in_=lpt[:], axis=AX.X, op=ALU.add), reads=["lpt"], writes=["lps"])
                S.add("act", _mk("activation", out=lps[:], in_=lps[:], func=AF.Exp), reads=["lps"], writes=["lps"])
                S.add("dve", _mk("tensor_tensor", out=neglam[:], in0=lps[:, 1:2], in1=lps[:, 0:1], op=ALU.subtract), reads=["lps"], writes=["neglam"])
                S.add("dve", _mk("tensor_scalar", out=neglam[:], in0=neglam[:], scalar1=-LAM_INIT[L], scalar2=None, op0=ALU.add),
                      reads=["neglam"], writes=["neglam"])

            def attend(qT_ap, qres, QW, chunks, dv, O_ap, SM_ap, ores):
                if isinstance(chunks, list) and chunks and isinstance(chunks[0], dict):
                    chunks = [(len(chunks), (lambda t=chunks: t))]
                n = sum(c[0] for c in chunks)
                idx = 0
                pending = chunks[0][1]()
                for ci in range(len(chunks)):
                    nxt = chunks[ci + 1][1]() if ci + 1 < len(chunks) else None
                    for kt in pending:
                        attend_tile(qT_ap, qres, QW, kt, dv, O_ap, SM_ap, ores, idx, n)
                        idx += 1
                    pending = nxt

            def attend_tile(qT_ap, qres, QW, kt, dv, O_ap, SM_ap, ores, idx, n):
                if True:
                    kn = kt["kn"]
                    ps, psn = sbank.next()
                    am = kt.get("addmask")
                    S.add("pe", _mk("matmul", ps[0:kn, 0:QW], lhsT=kt["kT"], rhs=qT_ap, start=True, stop=(am is None)),
                          reads=[kt["kres"], qres], writes=[psn])
                    if am is not None:
                        S.add("pe", _mk("matmul", ps[0:kn, 0:QW], lhsT=identb[0:kn, 0:kn], rhs=am, start=False, stop=True),
                              reads=["const"], writes=[psn])
                    pt, ptn = pbuf.next()
                    bias = kt.get("bias")
                    if bias is not None:
                        S.add("act", _mk("activation", out=pt[0:kn, 0:QW], in_=ps[0:kn, 0:QW], func=AF.Exp, bias=bias),
                              reads=[psn, kt["bres"]], writes=[ptn])
                    else:
                        S.add("act", _mk("activation", out=pt[0:kn, 0:QW], in_=ps[0:kn, 0:QW], func=AF.Exp),
                              reads=[psn], writes=[ptn])
                    mm = kt.get("mulmask")
                    if mm is not None:
                        S.add("pool", _mk("tensor_tensor", out=pt[0:kn, 0:QW], in0=pt[0:kn, 0:QW], in1=mm, op=ALU.mult),
                              reads=[ptn, kt["mres"]], writes=[ptn])
                    S.add("pe", _mk("matmul", O_ap, lhsT=kt["v"], rhs=pt[0:kn, 0:QW], start=(idx == 0), stop=(idx == n - 1)),
                          reads=[ptn, kt["vres"]], writes=[ores[0]])
                    S.add("pe", _mk("matmul", SM_ap, lhsT=onesb[0:kn, 0:dv], rhs=pt[0:kn, 0:QW], start=(idx == 0), stop=(idx == n - 1)),
                          reads=[ptn], writes=[ores[1]])

            def prompt_tiles(kscr, krow0, Kc, vscr, vh, vd0, dv, KE, bias_h=None, masks=None, j_diag0=None, mul=None):
                chunks = []
                for c in range((KE + 15) // 16):
                    j0 = c * 16
                    nt = min(16, KE - j0)

                    def thunk(j0=j0, nt=nt):
                        tiles = []
                        kb_, kbn = kTb_.next()
                        vv, vvn = vb_.next()
                        S.dma("sp", _mk("dma_start", out=kb_[0:64, 0:nt * 128], in_=kscr[krow0:krow0 + 64, j0 * 128:(j0 + nt) * 128]),
                              writes=[kbn])
                        S.dma("sp", _mk("dma_start", out=vv[:, 0:nt, 0:dv], in_=vscr[vh, :, j0:j0 + nt, vd0:vd0 + dv]), writes=[vvn])
                        for jj in range(nt):
                            j = j0 + jj
                            t = dict(kn=128, kT=kb_[0:Kc, jj * 128:(jj + 1) * 128], kres=kbn, v=vv[:, jj, 0:dv], vres=vvn)
                            if bias_h is not None:
                                t["bias"] = nck[:, j, bias_h:bias_h + 1]; t["bres"] = "nck"
                            if masks is not None and j >= j_diag0:
                                t["addmask"] = masks[:, j - j_diag0, :]
                            if mul is not None:
                                t["mulmask"] = mul[:, j, :]; t["mres"] = "selT"
                            tiles.append(t)
                        return tiles
                    chunks.append((nt, thunk))
                return chunks

            def fin_pair(O_, SM_, ores, gscr, grow0, qcol0, QW, ct, mcol0):
                g_, gn = gt.next()
                S.dma("sp", _mk("dma_start", out=g_[:, 0:QW], in_=gscr[grow0:grow0 + 128, qcol0:qcol0 + QW]), writes=[gn])
                r_, rn = ftmp.next()
                S.add("dve", _mk("reciprocal", out=r_[:, 0:QW], in_=SM_[:, 0:QW]), reads=[ores[1]], writes=[rn])
                S.add("dve", _mk("tensor_tensor", out=r_[:, 0:QW], in0=O_[:, 0:QW], in1=r_[:, 0:QW], op=ALU.mult), reads=[ores[0], rn], writes=[rn])
                S.add("dve", _mk("tensor_tensor", out=mix[:, ct, mcol0:mcol0 + QW], in0=r_[:, 0:QW], in1=g_[:, 0:QW], op=ALU.mult),
                      reads=[rn, gn], writes=[("mix", ct)])

            def fin_diff(gscr, grow0, qcol0, QW, ct, mcol0):
                g_, gn = gt.next()
                S.dma("sp", _mk("dma_start", out=g_[:, 0:QW], in_=gscr[grow0:grow0 + 128, qcol0:qcol0 + QW]), writes=[gn])
                r0_, r0n = ftmp.next(); r1_, r1n = ftmp.next()
                (O0, o0n), (S0, s0n), (O1, o1n), (S1, s1n) = acc
                S.add("dve", _mk("reciprocal", out=r0_[:, 0:QW], in_=S0[:, 0:QW]), reads=[s0n], writes=[r0n])
                S.add("dve", _mk("tensor_tensor", out=r0_[:, 0:QW], in0=O0[:, 0:QW], in1=r0_[:, 0:QW], op=ALU.mult), reads=[o0n, r0n], writes=[r0n])
                S.add("dve", _mk("reciprocal", out=r1_[:, 0:QW], in_=S1[:, 0:QW]), reads=[s1n], writes=[r1n])
                S.add("dve", _mk("tensor_tensor", out=r1_[:, 0:QW], in0=O1[:, 0:QW], in1=r1_[:, 0:QW], op=ALU.mult), reads=[o1n, r1n], writes=[r1n])
                S.add("dve", _mk("scalar_tensor_tensor", out=r0_[:, 0:QW], in0=r1_[:, 0:QW], scalar=neglam[:, 0:1], in1=r0_[:, 0:QW],
                                                               op0=ALU.mult, op1=ALU.add), reads=[r0n, r1n, "neglam"], writes=[r0n])
                S.add("act", _mk("activation", out=sqb[:, 0:QW], in_=r0_[:, 0:QW], func=AF.Square), reads=[r0n], writes=["sqb"])
                S.add("pe", _mk("matmul", pmisc[:, 0:QW], lhsT=onesdiv[:], rhs=sqb[:, 0:QW], start=True, stop=True), reads=["sqb"], writes=["pmisc"])
                S.add("dve", _mk("tensor_scalar", out=r1_[:, 0:QW], in0=pmisc[:, 0:QW], scalar1=1e-6, scalar2=None, op0=ALU.add),
                      reads=["pmisc"], writes=[r1n])
                S.add("act", _mk("activation", out=r1_[:, 0:QW], in_=r1_[:, 0:QW], func=AF.Sqrt), reads=[r1n], writes=[r1n])
                S.add("dve", _mk("reciprocal", out=r1_[:, 0:QW], in_=r1_[:, 0:QW]), reads=[r1n], writes=[r1n])
                S.add("dve", _mk("tensor_tensor", out=r0_[:, 0:QW], in0=r0_[:, 0:QW], in1=r1_[:, 0:QW], op=ALU.mult), reads=[r0n, r1n], writes=[r0n])
                S.add("dve", _mk("tensor_tensor", out=mix[:, ct, mcol0:mcol0 + QW], in0=r0_[:, 0:QW], in1=g_[:, 0:QW], op=ALU.mult),
                      reads=[r0n, gn], writes=[("mix", ct)])

            def out_proj(g, t0, nsub):
                xsrc = (I["xp"] if g == "p" else I["xs"]) if L == 0 else SC["resid_" + g]
                for s in range(nsub):
                    r0 = t0 + s * 128
                    x_, xn = xo.next()
                    S.dma("sp", _mk("dma_start", out=x_[:], in_=xsrc[r0:r0 + 128, :]), writes=[xn])
                    y_, yn = xn_.next()
                    for half in range(2):
                        pa, pan = acc[half]
                        for ct in range(8):
                            S.add("pe", _mk("matmul", pa[:, :], lhsT=mix[:, ct, s * 128:(s + 1) * 128],
                                                                                      rhs=wo[:, ct, half * 512:(half + 1) * 512], start=(ct == 0), stop=(ct == 7)),
                                  reads=[("mix", ct), "wo"], writes=[pan])
                        S.add("dve", _mk("tensor_tensor", out=y_[:, half * 512:(half + 1) * 512], in0=pa[:, :],
                                                                                            in1=x_[:, half * 512:(half + 1) * 512], op=ALU.add),
                              reads=[pan, xn], writes=[yn])
                    if not last:
                        S.dma("pool", _mk("dma_start", out=SC["resid_" + g][r0:r0 + 128, :], in_=y_[:]), reads=[yn])
                    else:
                        sm, smn = smallB.next()
                        S.add("act", _mk("activation", out=junkB[:], in_=y_[:], func=AF.Square, accum_out=sm[:, 0:1]),
                              reads=[yn], writes=["junkB", smn])
                        S.add("dve", _mk("tensor_scalar", out=sm[:, 1:2], in0=sm[:, 0:1], scalar1=1.0 / D, scalar2=1e-6, op0=ALU.mult, op1=ALU.add),
                              reads=[smn], writes=[smn])
                        S.add("act", _mk("activation", out=sm[:, 1:2], in_=sm[:, 1:2], func=AF.Sqrt), reads=[smn], writes=[smn])
                        S.add("dve", _mk("reciprocal", out=sm[:, 1:2], in_=sm[:, 1:2]), reads=[smn], writes=[smn])
                        S.add("dve", _mk("scalar_tensor_tensor", out=y_[:], in0=y_[:], scalar=sm[:, 1:2], in1=gbf[:], op0=ALU.mult, op1=ALU.mult),
                              reads=[yn, smn, "gbf"], writes=[yn])
                        S.dma("pool", _mk("dma_start", out=O["y_" + g][r0:r0 + 128, :], in_=y_[:]), reads=[yn])

            def indexer(qiT_src, qc0, nq, wi_src, wr0, kiT_of, NV, diag_j, ul, topk):
                q_, qn = qit.next()
                S.dma("sp", _mk("dma_start", out=q_[:, :, 0:nq], in_=qiT_src.rearrange("(h d) t -> d h t", d=64)[:, :, qc0:qc0 + nq]), writes=[qn])
                w_, wn = wit.next()
                S.dma("sp", _mk("dma_start", out=w_[0:nq, :], in_=wi_src[wr0:wr0 + nq, :]), writes=[wn])
                for hh in range(8):
                    S.add("dve", _mk("tensor_scalar", out=diagw[0:nq, hh, 0:nq], in0=identb[0:nq, 0:nq], scalar1=w_[0:nq, hh:hh + 1], scalar2=None,
                                                                          op0=ALU.mult), reads=[wn, "const"], writes=["diagw"])
                sc_ps, scn = acc[0]
                for kg in range((NV + 511) // 512):
                    k0 = kg * 512
                    n = min(512, NV - k0)
                    kap, kres = kiT_of(k0, n)
                    for hh in range(8):
                        ps, psn = sbank.next()
                        S.add("pe", _mk("matmul", ps[0:nq, 0:n], lhsT=q_[:, hh, 0:nq], rhs=kap, start=True, stop=True),
                              reads=[qn, kres], writes=[psn])
                        r_, rn = rbuf.next()
                        if hh % 2 == 0:
                            S.add("act", _mk("activation", out=r_[0:nq, 0:n], in_=ps[0:nq, 0:n], func=AF.Relu), reads=[psn], writes=[rn])
                        else:
                            S.add("dve", _mk("tensor_scalar", out=r_[0:nq, 0:n], in0=ps[0:nq, 0:n], scalar1=0.0, scalar2=None, op0=ALU.max),
                                  reads=[psn], writes=[rn])
                        S.add("pe", _mk("matmul", sc_ps[0:nq, 0:n], lhsT=diagw[0:nq, hh, 0:nq], rhs=r_[0:nq, 0:n], start=(hh == 0), stop=(hh == 7)),
                              reads=[rn, "diagw"], writes=[scn])
                    S.add("act", _mk("copy", out=B["scores"][0:nq, k0:k0 + n], in_=sc_ps[0:nq, 0:n]), reads=[scn], writes=["scores"])
                S.add("dve", _mk("tensor_reduce", out=bis[0:nq, 0:1], in_=B["scores"][0:nq, 0:NV], axis=AX.X, op=ALU.min), reads=["scores"], writes=["bis"])
                if diag_j is not None:
                    S.add("dve", _mk("tensor_tensor", out=B["scores"][0:nq, diag_j * 128:(diag_j + 1) * 128], in0=B["scores"][0:nq, diag_j * 128:(diag_j + 1) * 128],
                                                           in1=idxmask[0:nq, :], op=ALU.add), reads=["scores", "const"], writes=["scores"])
                S.add("dve", _mk("tensor_reduce", out=bis[0:nq, 1:2], in_=B["scores"][0:nq, 0:NV], axis=AX.X, op=ALU.max), reads=["scores"], writes=["bis"])
                S.add("dve", _mk("scalar_tensor_tensor", out=bis[0:nq, 1:2], in0=bis[0:nq, 1:2], scalar=1.0, in1=bis[0:nq, 0:1], op0=ALU.add, op1=ALU.subtract),
                      reads=["bis"], writes=["bis"])
                S.add("dve", _mk("tensor_scalar", out=hwtab[0:nq, :], in0=pow2[0:nq, :], scalar1=bis[0:nq, 1:2], scalar2=None, op0=ALU.mult),
                      reads=["bis", "const"], writes=["hwtab"])
                for k in range(NIT):
                    S.add("dve", _mk("tensor_tensor", out=bis[0:nq, 2:3], in0=bis[0:nq, 0:1], in1=hwtab[0:nq, k:k + 1], op=ALU.add),
                          reads=["bis", "hwtab"], writes=["bis"])
                    S.add("dve", _mk("tensor_scalar", out=B["sel"][0:nq, ul, 0:NV], in0=B["scores"][0:nq, 0:NV], scalar1=bis[0:nq, 2:3], scalar2=0.0,
                                                           op0=ALU.is_ge, op1=ALU.add, accum_out=bis[0:nq, 3:4]), reads=["scores", "bis"], writes=[("sel", ul), "bis"])
                    S.add("dve", _mk("scalar_tensor_tensor", out=bis[0:nq, 4:5], in0=bis[0:nq, 3:4], scalar=topk - 0.5, in1=hwtab[0:nq, k:k + 1],
                                                                       op0=ALU.is_ge, op1=ALU.mult), reads=["bis", "hwtab"], writes=["bis"])
                    S.add("dve", _mk("tensor_tensor", out=bis[0:nq, 0:1], in0=bis[0:nq, 0:1], in1=bis[0:nq, 4:5], op=ALU.add), reads=["bis"], writes=["bis"])
                S.add("dve", _mk("tensor_scalar", out=B["sel"][0:nq, ul, 0:NV], in0=B["scores"][0:nq, 0:NV], scalar1=bis[0:nq, 0:1], scalar2=None, op0=ALU.is_ge),
                      reads=["scores", "bis"], writes=[("sel", ul)])

            with ExitStack() as sec:
                kTb_ = Rot([(sb(sec, f"kTbuf{k}", [66, 2048], BF16), f"kTbuf{k}") for k in range(2)])
                vb_ = Rot([(sb(sec, f"vbuf{k}", [128, 16, 128], BF16), f"vbuf{k}") for k in range(2)])
                for k in range(2):
                    t_, tn = kTb_.next()
                    S.add("pool", _mk("memset", t_[64:66, :], 1.0), writes=[tn])
                if even:
                    B["scores"] = sb(sec, "scores", [128, NP], F32)
                    B["sel"] = sb(sec, "sel", [128, 2, NP], BF16)
                    B["selT"] = sb(sec, "selT", [128, NTP, 256], BF16)
                    kic = Rot([(sb(sec, f"kic{k}", [64, 512], BF16), f"kic{k}") for k in range(2)])
                for m in range(NB):
                    Q0 = 512 * m
                    KE = 4 * (m + 1)
                    if even:
                        def fox_heads(h_list):
                            for h in h_list:
                                q_, qn = qTt.next()
                                S.dma("sp", _mk("dma_start", out=q_[0:64, :], in_=SC["qTa_p"][h * 64:(h + 1) * 64, Q0:Q0 + 512]), writes=[qn])
                                S.dma("sp", _mk("dma_start", out=q_[64:66, :], in_=SC["cqa_p"][h, :, Q0:Q0 + 512]), writes=[qn])
                                tiles = prompt_tiles(SC["kTa_p"], h * 64, 66, SC["va_p"], h, 0, 64, KE, bias_h=h, masks=cmask, j_diag0=4 * m)
                                pb = (h % 2) * 64
                                (Oa, on), (Sa, sn) = acc[0], acc[1]
                                attend(q_[0:66, 0:512], qn, 512, tiles, 64, Oa[pb:pb + 64, :], Sa[pb:pb + 64, :], (on, sn))
                                if h % 2 == 1:
                                    fin_pair(Oa, Sa, (on, sn), SC["gTa_p"], (h // 2) * 128, Q0, 512, h // 2, 0)

                        def dsa_sub(u2):
                            q0 = Q0 + 256 * u2
                            KE2 = 4 * m + 2 * u2 + 2
                            for ul in range(2):
                                i = 4 * m + 2 * u2 + ul
                                NV = 128 * (i + 1)

                                def kiT_of(k0, n):
                                    kc_, kcn = kic.next()
                                    S.dma("sp", _mk("dma_start", out=kc_[:, 0:n], in_=SC["kiT_p"][:, k0:k0 + n]), writes=[kcn])
                                    return kc_[:, 0:n], kcn
                                indexer(SC["qiT_p"], i * 128, 128, SC["wi_p"], i * 128, kiT_of, NV, i, ul, TOPK_P)
                                if NV < KE2 * 128:
                                    S.add("pool", _mk("memset", B["sel"][:, ul, NV:KE2 * 128], 0.0), writes=[("sel", ul)])
                            return q0, KE2

                        def dsa_attn(q0, KE2, u2):
                            for j in range(KE2):
                                for ul in range(2):
                                    S.add("pe", _mk("transpose", out=ptrB[:, ul, :], in_=B["sel"][:, ul, j * 128:(j + 1) * 128], identity=identb[:]),
                                          reads=[("sel", ul)], writes=["ptrB"])
                                S.add("act", _mk("copy", out=B["selT"][:, j, :], in_=ptrB[:, 0:2, :].rearrange("p a b -> p (a b)")), reads=["ptrB"], writes=["selT"])
                            for h in range(8):
                                q_, qn = qTt.next()
                                S.dma("sp", _mk("dma_start", out=q_[0:64, 0:256], in_=SC["qTb_p"][h * 64:(h + 1) * 64, q0:q0 + 256]), writes=[qn])
                                tiles = prompt_tiles(SC["kTb_p"], h * 64, 64, SC["vb_p"], h, 0, 64, KE2, mul=B["selT"])
                                pb = (h % 2) * 64
                                (Oa, on), (Sa, sn) = acc[2], acc[3]
                                attend(q_[0:64, 0:256], qn, 256, tiles, 64, Oa[pb:pb + 64, 0:256], Sa[pb:pb + 64, 0:256], (on, sn))
                                if h % 2 == 1:
                                    fin_pair(Oa, Sa, (on, sn), SC["gTb_p"], (h // 2) * 128, q0, 256, 4 + h // 2, 256 * u2)

                        q0a, KEa = dsa_sub(0)
                        fox_heads(range(0, 4))
                        dsa_attn(q0a, KEa, 0)
                        q0b, KEb = dsa_sub(1)
                        fox_heads(range(4, 8))
                        dsa_attn(q0b, KEb, 1)
                    else:
                        for h in range(8):
                            for c in range(2):
                                hs = 2 * h + c
                                q_, qn = qTt.next()
                                S.dma("sp", _mk("dma_start", out=q_[0:64, :], in_=SC["qTc_p"][hs * 64:(hs + 1) * 64, Q0:Q0 + 512]), writes=[qn])
                                tiles = prompt_tiles(SC["kTc_p"], hs * 64, 64, SC["vc_p"], h, 0, 128, KE, masks=cmask, j_diag0=4 * m)
                                (Oa, on), (Sa, sn) = acc[2 * c], acc[2 * c + 1]
                                attend(q_[0:64, 0:512], qn, 512, tiles, 128, Oa[:, :], Sa[:, :], (on, sn))
                            fin_diff(SC["gTc_p"], h * 128, Q0, 512, h, 0)
                    out_proj("p", Q0, 4)

                S.flush()

            with ExitStack() as sec:
                NKSP = ((NKS + 127) // 128) * 128
                if even:
                    B["scores"] = sb(sec, "scores_s", [128, NKSP], F32)
                    B["sel"] = sb(sec, "sel_s", [128, 1, NKSP], BF16)
                    B["selT"] = sb(sec, "selT_s", [128, NPT + 1, 32], BF16)
                    B["kTs"] = sb(sec, "kTs", [64, 8, NKS], BF16)
                    B["vs"] = sb(sec, "vs", [128, NPT + 1, 512], BF16)
                    B["cst"] = sb(sec, "cst", [128, NPT, 512], BF16)
                    B["lfs"] = sb(sec, "lfs", [128, NPT + 1, 8], F32)
                    B["kis"] = sb(sec, "kis", [64, NKS], BF16)
                    B["csk"] = sb(sec, "csk", [128, NPT, 64], BF16)
                else:
                    B["kTs"] = sb(sec, "kTs", [64, 16, NKS], BF16)
                    B["vs"] = sb(sec, "vs", [128, NPT + 1, 1024], BF16)
                    B["cst"] = sb(sec, "cst", [128, NPT, 1024], BF16)
                for b in range(4):
                    qc0 = 32 * b
                    ntile = NPT + 1

                    def load_kT(cache_ap, ncols, nsh, new_scr, dst):
                        S.dma("pool", _mk("dma_start", out=B["cst"][:, :, 0:ncols], in_=cache_ap.rearrange("(j p) c -> p j c", p=128)), writes=["cst"])
                        for sh in range(nsh):
                            for j4 in range(0, NPT, 4):
                                nj = min(4, NPT - j4)
                                for jj in range(nj):
                                    S.add("pe", _mk("transpose", out=ptrB[0:64, jj, :], in_=B["cst"][:, j4 + jj, sh * 64:(sh + 1) * 64], identity=identb[:]),
                                          reads=["cst"], writes=["ptrB"])
                                S.add("dve", _mk("tensor_copy", out=dst[0:64, sh, j4 * 128:(j4 + nj) * 128].rearrange("p (j t) -> p j t", t=128),
                                                                                           in_=ptrB[0:64, 0:nj, :]), reads=["ptrB"], writes=["kTs"])
                        S.dma("sp", _mk("dma_start", out=dst[0:64, 0:nsh, PAST:PAST + 32], in_=new_scr.rearrange("(h d) t -> d h t", d=64)[:, :, qc0:qc0 + 32]),
                              writes=["kTs"])

                    def load_v(cache_ap, ncols, new_scr, dv):
                        S.dma("pool", _mk("dma_start", out=B["vs"][:, 0:NPT, 0:ncols], in_=cache_ap.rearrange("(j p) c -> p j c", p=128)), writes=["vs"])
                        S.dma("sp", _mk("dma_start", out=B["vs"][0:32, NPT, 0:ncols].rearrange("p (h d) -> p h d", d=dv),
                                                          in_=new_scr[:, qc0:qc0 + 32, 0, :].rearrange("h p d -> p h d")), writes=["vs"])

                    def sample_tiles(sh, h, dv, bias_h=None, mask_new=None, mul=None):
                        tiles = []
                        for j in range(ntile):
                            kn = 128 if j < NPT else 32
                            t = dict(kn=kn, kT=B["kTs"][0:64, sh, j * 128:j * 128 + kn], kres="kTs", v=B["vs"][0:kn, j, h * dv:(h + 1) * dv], vres="vs")
                            if bias_h is not None:
                                t["bias"] = nck[0:kn, j, bias_h:bias_h + 1]; t["bres"] = "nck"
                            if mask_new is not None and j == NPT:
                                t["addmask"] = mask_new
                            if mul is not None:
                                t["mulmask"] = mul[0:kn, j, 0:32]; t["mres"] = "selT"
                            tiles.append(t)
                        return tiles

                    if even:
                        load_kT(I["ca_k"][li, b], 512, 8, SC["kTa_s"], B["kTs"])
                        load_v(I["ca_v"][li, b], 512, SC["va_s"], 64)
                        S.add("dve", _mk("memset", B["lfs"][:, NPT, :], 0.0), writes=["lfs"])
                        S.dma("sp", _mk("dma_start", out=B["lfs"][:, 0:NPT, :], in_=I["ca_f"][li, b].rearrange("(j p) h -> p j h", p=128)), writes=["lfs"])
                        S.dma("sp", _mk("dma_start", out=B["lfs"][0:32, NPT, :], in_=SC["nck_s"][qc0:qc0 + 32, :]), writes=["lfs"])
                        cps, cpn = acc[0]
                        tps, tpn = acc[1]
                        for j in range(ntile):
                            S.add("pe", _mk("matmul", cps[:, j * 8:(j + 1) * 8], lhsT=utf[:], rhs=B["lfs"][:, j, :], start=True, stop=(j == 0)),
                                  reads=["lfs", "const"], writes=[cpn])
                            for j2 in range(j):
                                S.add("pe", _mk("matmul", cps[:, j * 8:(j + 1) * 8], lhsT=onesf[:], rhs=B["lfs"][:, j2, :], start=False, stop=(j2 == j - 1)),
                                      reads=["lfs", "const"], writes=[cpn])
                        for j in range(ntile):
                            S.add("pe", _mk("matmul", tps[:, 0:8], lhsT=onesf[:], rhs=B["lfs"][:, j, :], start=(j == 0), stop=(j == ntile - 1)),
                                  reads=["lfs", "const"], writes=[tpn])
                        S.add("act", _mk("copy", out=B["lfs"][:, 0, :], in_=tps[:, 0:8]), reads=[tpn], writes=["lfs"])
                        S.add("dve", _mk("tensor_tensor", out=nck[:, 0:ntile, :], in0=B["lfs"][:, 0, :].unsqueeze(1).broadcast_to([128, ntile, 8]),
                                                               in1=cps[:, 0:ntile * 8].rearrange("p (j h) -> p j h", h=8), op=ALU.subtract),
                              reads=[cpn, "lfs"], writes=["nck"])
                        for h in range(8):
                            q_, qn = qTt.next()
                            S.dma("sp", _mk("dma_start", out=q_[0:64, 0:32], in_=SC["qTa_s"][h * 64:(h + 1) * 64, qc0:qc0 + 32]), writes=[qn])
                            tiles = sample_tiles(h, h, 64, bias_h=h, mask_new=cmask_s[0:32, 0:32])
                            pb = (h % 2) * 64
                            (Oa, on), (Sa, sn) = acc[2], acc[3]
                            attend(q_[0:64, 0:32], qn, 32, tiles, 64, Oa[pb:pb + 64, 0:32], Sa[pb:pb + 64, 0:32], (on, sn))
                            if h % 2 == 1:
                                fin_pair(Oa, Sa, (on, sn), SC["gTa_s"], (h // 2) * 128, qc0, 32, h // 2, qc0)
                        S.dma("pool", _mk("dma_start", out=B["csk"][:], in_=I["cb_ki"][li, b].rearrange("(j p) c -> p j c", p=128)), writes=["csk"])
                        for j4 in range(0, NPT, 4):
                            nj = min(4, NPT - j4)
                            for jj in range(nj):
                                S.add("pe", _mk("transpose", out=ptrB[0:64, jj, :], in_=B["csk"][:, j4 + jj, :], identity=identb[:]), reads=["csk"], writes=["ptrB"])
                            S.add("dve", _mk("tensor_copy", out=B["kis"][0:64, j4 * 128:(j4 + nj) * 128].rearrange("p (j t) -> p j t", t=128), in_=ptrB[0:64, 0:nj, :]),
                                  reads=["ptrB"], writes=["kis"])
                        S.dma("sp", _mk("dma_start", out=B["kis"][0:64, PAST:PAST + 32], in_=SC["kiT_s"][:, qc0:qc0 + 32]), writes=["kis"])
                        indexer(SC["qiT_s"], qc0, 32, SC["wi_s"], qc0, lambda k0, n: (B["kis"][0:64, k0:k0 + n], "kis"), NKS, None, 0, TOPK_S)
                        for j in range(ntile):
                            kn = 128 if j < NPT else 32
                            S.add("pe", _mk("transpose", out=ptrB[0:kn, 0, 0:32], in_=B["sel"][0:32, 0, j * 128:j * 128 + kn], identity=identb[0:32, 0:32]),
                                  reads=[("sel", 0)], writes=["ptrB"])
                            S.add("act", _mk("copy", out=B["selT"][0:kn, j, 0:32], in_=ptrB[0:kn, 0, 0:32]), reads=["ptrB"], writes=["selT"])
                        load_kT(I["cb_k"][li, b], 512, 8, SC["kTb_s"], B["kTs"])
                        load_v(I["cb_v"][li, b], 512, SC["vb_s"], 64)
                        for h in range(8):
                            q_, qn = qTt.next()
                            S.dma("sp", _mk("dma_start", out=q_[0:64, 0:32], in_=SC["qTb_s"][h * 64:(h + 1) * 64, qc0:qc0 + 32]), writes=[qn])
                            tiles = sample_tiles(h, h, 64, mul=B["selT"])
                            pb = (h % 2) * 64
                            (Oa, on), (Sa, sn) = acc[2], acc[3]
                            attend(q_[0:64, 0:32], qn, 32, tiles, 64, Oa[pb:pb + 64, 0:32], Sa[pb:pb + 64, 0:32], (on, sn))
                            if h % 2 == 1:
                                fin_pair(Oa, Sa, (on, sn), SC["gTb_s"], (h // 2) * 128, qc0, 32, 4 + h // 2, qc0)
                    else:
                        load_kT(I["cc_k"][li, b], 1024, 16, SC["kTc_s"], B["kTs"])
                        load_v(I["cc_v"][li, b], 1024, SC["vc_s"], 128)
                        for h in range(8):
                            for c in range(2):
                                hs = 2 * h + c
                                q_, qn = qTt.next()
                                S.dma("sp", _mk("dma_start", out=q_[0:64, 0:32], in_=SC["qTc_s"][hs * 64:(hs + 1) * 64, qc0:qc0 + 32]), writes=[qn])
                                tiles = sample_tiles(hs, h, 128)
                                (Oa, on), (Sa, sn) = acc[2 * c], acc[2 * c + 1]
                                attend(q_[0:64, 0:32], qn, 32, tiles, 128, Oa[:, 0:32], Sa[:, 0:32], (on, sn))
                            fin_diff(SC["gTc_s"], h * 128, qc0, 32, h, qc0)
                out_proj("s", 0, 1)
                S.flush()

    only = cfg.get("only")
    for L in range(NL):
        if only is None or f"A{L}" in only:
            phase_A(L)
        if only is None or f"B{L}" in only:
            phase_B(L)

    top.close()
    return nc, None


def rope_tables(pos):
    half = 8
    inv = (500000.0 ** (-np.arange(half, dtype=np.float32) * 2.0 / 16)).astype(np.float32)
    ang = pos.astype(np.float32)[:, None] * inv[None, :]
    return np.cos(ang).astype(np.float32), np.sin(ang).astype(np.float32)


def make_consts(NP, PAST):
    c = {}
    c["ident"] = np.eye(128, dtype=np.float32)
    cp, sp = rope_tables(np.arange(NP))
    c["cosp"], c["sinp"] = cp, sp
    pos_s = PAST + (np.arange(128) % 32)
    cs, ss = rope_tables(pos_s)
    c["coss"], c["sins"] = cs, ss
    k = np.arange(128)[:, None]
    q = np.arange(512)[None, :]
    cm = np.zeros((4, 128, 512), np.float32)
    ccm = np.zeros((4, 128, 512), np.float32)
    for jj in range(4):
        kp = 128 * jj + k
        cm[jj] = np.where(kp <= q, 0.0, NEGM)
        ccm[jj] = np.where(kp // 64 <= q // 64, 0.0, NEGM)
    c["cmask"], c["ccmask"] = cm, ccm
    qq = np.arange(128)[:, None]
    kk = np.arange(128)[None, :]
    c["idxmask"] = np.where(kk // 64 <= qq // 64, 0.0, -1e30).astype(np.float32)
    k32 = np.arange(32)[:, None]
    q32 = np.arange(32)[None, :]
    c["cmask_s"] = np.where(k32 <= q32, 0.0, NEGM).astype(np.float32)
    c["ut"] = np.triu(np.ones((128, 128), np.float32))
    c["pow2"] = np.tile((0.5 ** (np.arange(NIT) + 1)).astype(np.float32)[None, :], (128, 1))
    return c


_CACHE = {}


def run(inputs, cfg, n_cores, prompt_of_core, sample_of_core):
    key = tuple(sorted((k, str(v)) for k, v in cfg.items()))
    if key not in _CACHE:
        _CACHE[key] = build(cfg)
    nc, _ = _CACHE[key]
    NP, PAST = cfg["NP"], cfg["PAST"]
    consts = make_consts(NP, PAST)
    f = lambda a: np.ascontiguousarray(np.asarray(a, dtype=np.float32))
    in_maps = []
    NO = cfg["NL"] // 2
    for c in range(n_cores):
        pb = prompt_of_core[c]
        sbs = sample_of_core[c]
        m = dict(consts)
        m["xp"] = f(inputs["x_prompt"][pb])
        m["xs"] = f(np.asarray(inputs["x_sample"])[sbs].reshape(128, D))
        m["ca_k"] = f(np.asarray(inputs["cache_a_k"])[:, sbs].reshape(-1, 4, PAST, 512))
        m["ca_v"] = f(np.asarray(inputs["cache_a_v"])[:, sbs].reshape(-1, 4, PAST, 512))
        m["ca_f"] = f(np.asarray(inputs["cache_a_logf"])[:, sbs])
        m["cb_k"] = f(np.asarray(inputs["cache_b_k"])[:, sbs].reshape(-1, 4, PAST, 512))
        m["cb_v"] = f(np.asarray(inputs["cache_b_v"])[:, sbs].reshape(-1, 4, PAST, 512))
        m["cb_ki"] = f(np.asarray(inputs["cache_b_kidx"])[:, sbs])
        if NO:
            m["cc_k"] = f(np.asarray(inputs["cache_c_k"])[:, sbs].reshape(-1, 4, PAST, 1024))
            m["cc_v"] = f(np.asarray(inputs["cache_c_v"])[:, sbs].reshape(-1, 4, PAST, 1024))
            m["w_in_odd"] = f(inputs["w_in_odd"]); m["w_out_odd"] = f(inputs["w_out_odd"])
            m["lam"] = f(inputs["lambda_params"]); m["chg"] = f(inputs["c_head_gain"])
        m["w_in_even"] = f(inputs["w_in_even"]); m["w_out_even"] = f(inputs["w_out_even"])
        m["norm_gain"] = f(inputs["norm_gain"]); m["final_gain"] = f(inputs["final_gain"])
        m["b_forget"] = f(inputs["b_forget"])
        in_maps.append(m)
    res = run_bass_kernel_spmd(nc, in_maps, core_ids=list(range(n_cores)))
    return res.results


def assemble(results, cfg, n_prompt, n_sample_total, prompt_of_core, sample_of_core):
    NP, NL = cfg["NP"], cfg["NL"]
    NE, NO = (NL + 1) // 2, NL // 2
    first_core = {}
    for c, pb in enumerate(prompt_of_core):
        first_core.setdefault(pb, c)

    def P(name, shape_tail, lead=None):
        arrs = [np.asarray(results[first_core[pb]][name]) for pb in range(n_prompt)]
        if lead is None:
            return np.stack(arrs, 0).reshape((n_prompt, NP) + shape_tail)
        return np.stack(arrs, 1).reshape((lead, n_prompt, NP) + shape_tail)

    def Sm(name, shape_tail, lead=None):
        if lead is None:
            out = np.zeros((n_sample_total, 32) + shape_tail, np.float32)
            for c, sbs in enumerate(sample_of_core):
                out[sbs] = np.asarray(results[c][name]).reshape((4, 32) + shape_tail)
            return out
        out = np.zeros((lead, n_sample_total, 32) + shape_tail, np.float32)
        for c, sbs in enumerate(sample_of_core):
            out[:, sbs] = np.asarray(results[c][name]).reshape((lead, 4, 32) + shape_tail)
        return out

    outs = [P("y_p", (D,)), Sm("y_s", (D,))]
    outs += [P("oa_k_p", (8, 64), NE), P("oa_v_p", (8, 64), NE), P("oa_f_p", (8,), NE),
             P("ob_k_p", (8, 64), NE), P("ob_v_p", (8, 64), NE), P("ob_ki_p", (64,), NE)]
    if NO:
        outs += [P("oc_k_p", (8, 128), NO), P("oc_v_p", (8, 128), NO)]
    outs += [Sm("oa_k_s", (8, 64), NE), Sm("oa_v_s", (8, 64), NE), Sm("oa_f_s", (8,), NE),
             Sm("ob_k_s", (8, 64), NE), Sm("ob_v_s", (8, 64), NE), Sm("ob_ki_s", (64,), NE)]
    if NO:
        outs += [Sm("oc_k_s", (8, 128), NO), Sm("oc_v_s", (8, 128), NO)]
    return tuple(np.ascontiguousarray(o, dtype=np.float32) for o in outs)


def kernel(**inputs):
    cfg = dict(NP=8192, NL=4, PAST=1024, TOPK_P=256, TOPK_S=256)
    n_cores = 8
    prompt_of_core = [c % 2 for c in range(n_cores)]
    sample_of_core = [list(range(4 * c, 4 * c + 4)) for c in range(n_cores)]
    results = run(inputs, cfg, n_cores, prompt_of_core, sample_of_core)
    return assemble(results, cfg, 2, 32, prompt_of_core, sample_of_core)
```

```python
import math
from contextlib import ExitStack
import numpy as np
import concourse.bass as bass
import concourse.mybir as mybir
from concourse.bass_utils import run_bass_kernel_spmd

F32 = mybir.dt.float32
BF16 = mybir.dt.bfloat16
AF = mybir.ActivationFunctionType
ALU = mybir.AluOpType
AX = mybir.AxisListType

ENGS = ("pe", "act", "dve", "pool", "sp")
N_DMA_SLOTS = 12


class Sched:
    def __init__(self, nc, stack):
        self.nc = nc
        self.ops = []
        self.start = 0
        self.res = {}
        self.dma_count = {e: 0 for e in ENGS}
        self.dma_slot_last = {}
        self.cnt = {e: 0 for e in ENGS}
        self.prev_final = {}
        self.excl = set()
        self.sems = {e: stack.enter_context(nc.semaphore("s_" + e)) for e in ENGS}
        self.dsems = {}
        for q in ("sp", "pool"):
            for sl in range(N_DMA_SLOTS):
                self.dsems[(q, sl)] = stack.enter_context(nc.semaphore(f"d_{q}_{sl}"))

    def _deps_for(self, reads, writes):
        deps = set()
        for r in reads:
            st = self.res.get(r)
            if st:
                deps.update(st["w"].values())
        for w in writes:
            st = self.res.get(w)
            if st:
                deps.update(st["w"].values())
                deps.update(st["r"])
        return deps

    def _commit(self, opid, key, reads, writes):
        for r in reads:
            st = self.res.setdefault(r, {"w": {}, "r": []})
            st["r"].append(opid)
        for w in writes:
            self.res[w] = {"w": {key: opid}, "r": []}

    def _split(self, reads, writes):
        ex = [r for r in reads if r in self.excl]
        if ex:
            reads = [r for r in reads if r not in self.excl]
            writes = list(writes) + [r for r in ex if r not in writes]
        return reads, writes

    def add(self, eng, fn, reads=(), writes=(), extra_deps=()):
        reads, writes = self._split(reads, writes)
        deps = self._deps_for(reads, writes)
        deps.update(extra_deps)
        opid = len(self.ops)
        self.ops.append(dict(id=opid, eng=eng, fn=fn, deps=deps, dma=False, signal=False))
        self._commit(opid, eng, reads, writes)
        return opid

    def dma(self, queue, fn, reads=(), writes=(), extra_deps=()):
        reads, writes = self._split(reads, writes)
        deps = self._deps_for(reads, writes)
        deps.update(extra_deps)
        n = self.dma_count[queue]
        self.dma_count[queue] += 1
        slot = n % N_DMA_SLOTS
        prev = self.dma_slot_last.get((queue, slot))
        if prev is not None:
            deps.add(prev)
        opid = len(self.ops)
        self.ops.append(dict(id=opid, eng=queue, fn=fn, deps=deps, dma=True, slot=slot,
                             dma_val=16 * (n // N_DMA_SLOTS + 1), signal=True))
        self.dma_slot_last[(queue, slot)] = opid
        self._commit(opid, ("dma", queue, slot), reads, writes)
        return opid

    def flush(self):
        nc = self.nc
        ops = self.ops
        start = self.start
        ph = ops[start:]
        eng_ops = {e: [o for o in ph if o["eng"] == e] for e in ENGS}
        for op in ph:
            for d in op["deps"]:
                if d < start:
                    continue
                p = ops[d]
                if p["dma"]:
                    continue
                if p["eng"] == op["eng"] and p["eng"] == "pe":
                    continue
                p["signal"] = True
        for e in ENGS:
            comp = [o for o in eng_ops[e] if not o["dma"]]
            if comp:
                comp[-1]["signal"] = True
        for e in ENGS:
            for op in eng_ops[e]:
                if op["dma"]:
                    continue
                if op["signal"]:
                    self.cnt[e] += 1
                    op["sig"] = self.cnt[e]
        sems, dsems = self.sems, self.dsems
        prev_final = dict(self.prev_final)
        engobj = {"pe": "tensor", "act": "scalar", "dve": "vector", "pool": "gpsimd", "sp": "sync"}
        with nc.Block() as block:
            def make(e):
                def body(eng):
                    waited = {}
                    for k, v in prev_final.items():
                        sem = sems[k[1]] if k[0] == "c" else dsems[(k[1], k[2])]
                        eng.wait_ge(sem, v)
                        waited[k] = v
                    for op in eng_ops[e]:
                        need = {}
                        for d in op["deps"]:
                            if d < start:
                                continue
                            p = ops[d]
                            if p["dma"]:
                                k = ("d", p["eng"], p["slot"])
                                v = p["dma_val"]
                            else:
                                if p["eng"] == e and e == "pe":
                                    continue
                                k = ("c", p["eng"])
                                v = p["sig"]
                            if v > need.get(k, 0):
                                need[k] = v
                        for k, v in need.items():
                            if waited.get(k, 0) >= v:
                                continue
                            waited[k] = v
                            sem = sems[k[1]] if k[0] == "c" else dsems[(k[1], k[2])]
                            eng.wait_ge(sem, v)
                        inst = op["fn"](eng)
                        if op["dma"]:
                            inst.then_inc(dsems[(e, op["slot"])], 16)
                        elif op["signal"]:
                            inst.then_inc(sems[e], 1)
                    for (q, sl), opid in self.dma_slot_last.items():
                        if q == e and opid >= start:
                            v = ops[opid]["dma_val"]
                            if waited.get(("d", q, sl), 0) < v:
                                eng.wait_ge(dsems[(q, sl)], v)
                return body

            for e in ENGS:
                if eng_ops[e] or prev_final:
                    getattr(block, engobj[e])(make(e))
        for e in ENGS:
            if self.cnt[e]:
                self.prev_final[("c", e)] = self.cnt[e]
        for (q, sl), opid in self.dma_slot_last.items():
            self.prev_final[("d", q, sl)] = ops[opid]["dma_val"]
        self.start = len(ops)
        self.res = {}
        for op in ph:
            op["fn"] = None


def _mk(name, *a, **k):
    return lambda e: getattr(e, name)(*a, **k)


class Rot:
    def __init__(self, items):
        self.items = items
        self.i = 0

    def next(self):
        it = self.items[self.i % len(self.items)]
        self.i += 1
        return it


D = 1024
NEGM = -30000.0
LAM_INIT = {1: 0.8 - 0.6 * math.exp(-0.3 * 1), 3: 0.8 - 0.6 * math.exp(-0.3 * 3)}
EVEN_COLS = dict(aq=0, ak=512, av=1024, ag=1536, af=2048, bq=2056, bk=2568, bv=3080, bg=3592,
                 qi=4104, ki=4616, wi=4680)
NIT = 18


def build(cfg):
    NP = cfg["NP"]
    NL = cfg["NL"]
    PAST = cfg["PAST"]
    TOPK_P = cfg["TOPK_P"]
    TOPK_S = cfg["TOPK_S"]
    NE, NO = (NL + 1) // 2, NL // 2
    NTP = NP // 128
    NPT = PAST // 128
    NB = NP // 512
    NKS = PAST + 32

    nc = bass.Bass("TRN2", target_bir_lowering=False)

    def din(name, shape, dt=F32):
        return nc.dram_tensor(name, list(shape), dt, kind="ExternalInput").ap()

    def dout(name, shape, dt=F32):
        return nc.dram_tensor(name, list(shape), dt, kind="ExternalOutput").ap()

    def dscr(name, shape, dt=F32):
        return nc.dram_tensor(name, list(shape), dt, kind="Internal").ap()

    I = {}
    I["xp"] = din("xp", [NP, D])
    I["xs"] = din("xs", [128, D])
    I["ca_k"] = din("ca_k", [NE, 4, PAST, 512]); I["ca_v"] = din("ca_v", [NE, 4, PAST, 512])
    I["ca_f"] = din("ca_f", [NE, 4, PAST, 8])
    I["cb_k"] = din("cb_k", [NE, 4, PAST, 512]); I["cb_v"] = din("cb_v", [NE, 4, PAST, 512])
    I["cb_ki"] = din("cb_ki", [NE, 4, PAST, 64])
    if NO:
        I["cc_k"] = din("cc_k", [NO, 4, PAST, 1024]); I["cc_v"] = din("cc_v", [NO, 4, PAST, 1024])
        I["w_in_odd"] = din("w_in_odd", [NO, D, 4096]); I["w_out_odd"] = din("w_out_odd", [NO, D, D])
        I["lam"] = din("lam", [NO, 4, 64]); I["chg"] = din("chg", [NO, 128])
    I["w_in_even"] = din("w_in_even", [NE, D, 4688]); I["w_out_even"] = din("w_out_even", [NE, D, D])
    I["norm_gain"] = din("norm_gain", [NL, D]); I["final_gain"] = din("final_gain", [D])
    I["b_forget"] = din("b_forget", [NE, 8])
    I["ident"] = din("ident", [128, 128])
    I["cosp"] = din("cosp", [NP, 8]); I["sinp"] = din("sinp", [NP, 8])
    I["coss"] = din("coss", [128, 8]); I["sins"] = din("sins", [128, 8])
    I["cmask"] = din("cmask", [4, 128, 512]); I["ccmask"] = din("ccmask", [4, 128, 512])
    I["idxmask"] = din("idxmask", [128, 128]); I["cmask_s"] = din("cmask_s", [32, 32])
    I["ut"] = din("ut", [128, 128]); I["pow2"] = din("pow2", [128, NIT])

    O = {}
    O["y_p"] = dout("y_p", [NP, D]); O["y_s"] = dout("y_s", [128, D])
    for g, T in (("p", NP), ("s", 128)):
        O["a_k_" + g] = dout("oa_k_" + g, [NE, T, 512]); O["a_v_" + g] = dout("oa_v_" + g, [NE, T, 512])
        O["a_f_" + g] = dout("oa_f_" + g, [NE, T, 8])
        O["b_k_" + g] = dout("ob_k_" + g, [NE, T, 512]); O["b_v_" + g] = dout("ob_v_" + g, [NE, T, 512])
        O["b_ki_" + g] = dout("ob_ki_" + g, [NE, T, 64])
        if NO:
            O["c_k_" + g] = dout("oc_k_" + g, [NO, T, 1024]); O["c_v_" + g] = dout("oc_v_" + g, [NO, T, 1024])

    SC = {}
    for g, T in (("p", NP), ("s", 128)):
        NT = T // 128
        SC["resid_" + g] = dscr("resid_" + g, [T, D])
        for nm in ("qTa", "kTa", "gTa", "qTb", "kTb", "gTb", "qiT"):
            SC[nm + "_" + g] = dscr(nm + "_" + g, [512, T], BF16)
        SC["kiT_" + g] = dscr("kiT_" + g, [64, T], BF16)
        SC["va_" + g] = dscr("va_" + g, [8, 128, NT, 64], BF16)
        SC["vb_" + g] = dscr("vb_" + g, [8, 128, NT, 64], BF16)
        SC["wi_" + g] = dscr("wi_" + g, [T, 8])
        SC["cqa_" + g] = dscr("cqa_" + g, [8, 2, T], BF16)
        SC["nck_" + g] = dscr("nck_" + g, [T, 8])
        for nm in ("qTc", "kTc", "gTc"):
            SC[nm + "_" + g] = dscr(nm + "_" + g, [1024, T], BF16)
        SC["vc_" + g] = dscr("vc_" + g, [8, 128, NT, 128], BF16)

    top = ExitStack()
    S = Sched(nc, top)

    uniq = [0]

    def sb(stack, name, shape, dt):
        uniq[0] += 1
        return stack.enter_context(nc.sbuf_tensor(f"{name}_{uniq[0]}", list(shape), dt))

    def psb(stack, name, shape, dt):
        S.excl.add(name)
        uniq[0] += 1
        return stack.enter_context(nc.psum_tensor(f"{name}_{uniq[0]}", list(shape), dt))

    identf = sb(top, "identf", [128, 128], F32)
    identb = sb(top, "identb", [128, 128], BF16)
    onesb = sb(top, "onesb", [128, 128], BF16)
    onesdiv = sb(top, "onesdiv", [128, 128], BF16)
    onesf = sb(top, "onesf", [128, 128], F32)
    utf = sb(top, "utf", [128, 128], F32)
    pow2 = sb(top, "pow2s", [128, NIT], F32)
    cmask_s = sb(top, "cmask_ss", [32, 32], BF16)
    idxmask = sb(top, "idxmasks", [128, 128], F32)
    ones8 = sb(top, "ones8", [8, 512], F32)

    S.dma("sp", _mk("dma_start", out=identf[:], in_=I["ident"][:, :]), writes=["identf"])
    S.dma("sp", _mk("dma_start", out=utf[:], in_=I["ut"][:, :]), writes=["const"])
    S.dma("sp", _mk("dma_start", out=pow2[:], in_=I["pow2"][:, :]), writes=["const"])
    S.dma("sp", _mk("dma_start", out=idxmask[:], in_=I["idxmask"][:, :]), writes=["const"])
    S.dma("pool", _mk("dma_start", out=cmask_s[:], in_=I["cmask_s"][:, :]), writes=["const"])
    S.add("dve", _mk("tensor_copy", out=identb[:], in_=identf[:]), reads=["identf"], writes=["const"])
    S.add("dve", _mk("memset", onesb[:], 1.0), writes=["const"])
    S.add("dve", _mk("memset", onesdiv[:], 1.0 / 128.0), writes=["const"])
    S.add("dve", _mk("memset", onesf[:], 1.0), writes=["const"])
    S.add("dve", _mk("memset", ones8[:], 1.0), writes=["const"])
    S.flush()

    GRP = {
        "p": dict(T=NP, TB=512),
        "s": dict(T=128, TB=128),
    }

    def phase_A(L):
        even = (L % 2 == 0)
        li = L // 2
        NCOL = 4688 if even else 4096
        win = (I["w_in_even"] if even else I["w_in_odd"])[li]
        with ExitStack() as ph:
            W = sb(ph, "W", [128, 8, NCOL], BF16)
            gb = sb(ph, "gb", [128, D], F32)
            cosp = sb(ph, "cosps", [128, NTP, 8], F32); sinp = sb(ph, "sinps", [128, NTP, 8], F32)
            coss = sb(ph, "cosss", [128, 1, 8], F32); sins = sb(ph, "sinss", [128, 1, 8], F32)
            GRP["p"]["cos"], GRP["p"]["sin"] = cosp, sinp
            GRP["s"]["cos"], GRP["s"]["sin"] = coss, sins
            S.dma("sp", _mk("dma_start", out=cosp[:], in_=I["cosp"].rearrange("(j p) e -> p j e", p=128)), writes=["rope"])
            S.dma("sp", _mk("dma_start", out=sinp[:], in_=I["sinp"].rearrange("(j p) e -> p j e", p=128)), writes=["rope"])
            S.dma("sp", _mk("dma_start", out=coss[:, 0, :], in_=I["coss"][:, :]), writes=["rope"])
            S.dma("sp", _mk("dma_start", out=sins[:, 0, :], in_=I["sins"][:, :]), writes=["rope"])
            xt = Rot([(sb(ph, f"xt{k}", [128, D], F32), f"xt{k}") for k in range(2)])
            junk = sb(ph, "junkA", [128, D], BF16)
            small = Rot([(sb(ph, f"smA{k}", [128, 2], F32), f"smA{k}") for k in range(2)])
            hb = Rot([(sb(ph, f"hb{k}", [128, D], BF16), f"hb{k}") for k in range(2)])
            hT = sb(ph, "hT", [128, 8, 512], BF16)
            st = Rot([(sb(ph, f"st{k}", [128, 512], F32), f"st{k}") for k in range(3)])
            kb = Rot([(sb(ph, f"kb{k}", [128, 512], BF16), f"kb{k}") for k in range(3)])
            kTst = Rot([(sb(ph, f"kTst{k}", [128, 4, 128], BF16), f"kTst{k}") for k in range(3)])
            fmst = Rot([(sb(ph, f"fmst{k}", [128, 512], BF16), f"fmst{k}") for k in range(3)])
            rtmp = Rot([(sb(ph, f"rtmp{k}", [128, 8, 8], F32), f"rtmp{k}") for k in range(4)])
            negb = sb(ph, "negb", [8, 1], F32)
            gainp = sb(ph, "gainp", [128, 1], F32)
            lfe = sb(ph, "lfe", [8, 512], F32)
            lfT = sb(ph, "lfT", [8, 512], F32)
            cumT = sb(ph, "cumT", [8, 512], F32)
            carry = sb(ph, "carry", [8, 1], F32)
            chi = sb(ph, "chi", [8, 512], BF16)
            clo = sb(ph, "clo", [8, 512], BF16)
            chf = sb(ph, "chf", [8, 512], F32)
            lftm = sb(ph, "lftm", [128, 8, 8], F32)
            pT = psb(ph, "pT", [128, 8, 128], BF16)
            ptm = Rot([(psb(ph, f"ptm{k}", [128, 512], F32), f"ptm{k}") for k in range(2)])
            pfm = Rot([(psb(ph, f"pfm{k}", [128, 512], F32), f"pfm{k}") for k in range(2)])
            ptr = Rot([(psb(ph, f"ptr{k}", [128, 8, 128], BF16), f"ptr{k}") for k in range(2)])
            plt = psb(ph, "plt", [128, 64, 8], F32)

            wsrc = win.rearrange("(kc p) n -> p kc n", p=128)
            for c0 in range(0, NCOL, 512):
                c1 = min(NCOL, c0 + 512)
                S.dma("pool", _mk("dma_start", out=W[:, :, c0:c1], in_=wsrc[:, :, c0:c1]),
                      writes=[("W", c0 // 512)])

            def wres(c0, n):
                return [("W", k) for k in range(c0 // 512, (c0 + n - 1) // 512 + 1)]

            S.dma("sp", _mk("dma_start", out=gb[:], in_=I["norm_gain"][L].partition_broadcast(128)), writes=["gb"])
            if even:
                S.dma("sp", _mk("dma_start", out=negb[:], in_=I["b_forget"][li].unsqueeze(1)), writes=["negb"])
                S.add("dve", _mk("tensor_scalar", out=negb[:], in0=negb[:], scalar1=-1.0, scalar2=None, op0=ALU.mult),
                      reads=["negb"], writes=["negb"])
            else:
                S.dma("sp", _mk("dma_start", out=gainp[:], in_=I["chg"][li].unsqueeze(1)), writes=["gainp"])
                S.add("dve", _mk("tensor_scalar", out=gainp[:], in0=gainp[:], scalar1=1.0 - LAM_INIT[L], scalar2=None,
                                                        op0=ALU.mult), reads=["gainp"], writes=["gainp"])

            dbg = cfg.get("dbg", "")
            for g in ("p", "s"):
                if g == "s" and "nosample" in dbg:
                    continue
                G = GRP[g]
                T, TB = G["T"], G["TB"]
                nsub = TB // 128
                xsrc = (I["xp"] if g == "p" else I["xs"]) if L == 0 else SC["resid_" + g]
                if even:
                    S.add("dve", _mk("memset", carry[:], 0.0), writes=["carry"])
                    tm_chunks = [
                        dict(c0=EVEN_COLS["ak"], n=512, kind="K", rope=False, out=O["a_k_" + g][li], scr=SC["kTa_" + g], scale=1.0),
                        dict(c0=EVEN_COLS["av"], n=512, kind="V", rope=False, out=O["a_v_" + g][li], scr=SC["va_" + g], dv=64),
                        dict(c0=EVEN_COLS["bq"], n=512, kind="Q", rope=True, out=None, scr=SC["qTb_" + g], scale=0.125),
                        dict(c0=EVEN_COLS["bk"], n=512, kind="K", rope=True, out=O["b_k_" + g][li], scr=SC["kTb_" + g], scale=1.0),
                        dict(c0=EVEN_COLS["bv"], n=512, kind="V", rope=False, out=O["b_v_" + g][li], scr=SC["vb_" + g], dv=64),
                        dict(c0=EVEN_COLS["qi"], n=512, kind="Q", rope=True, out=None, scr=SC["qiT_" + g], scale=0.125),
                        dict(c0=EVEN_COLS["ki"], n=64, kind="K", rope=True, out=O["b_ki_" + g][li], scr=SC["kiT_" + g], scale=1.0),
                        dict(c0=EVEN_COLS["wi"], n=8, kind="W", rope=False, out=SC["wi_" + g], scr=None),
                    ]
                    fm_tiles = [dict(c0=EVEN_COLS["aq"] + 128 * k, n=128, kind="q", scr=SC["qTa_" + g], r0=128 * k) for k in range(4)]
                    fm_tiles += [dict(c0=EVEN_COLS["ag"] + 128 * k, n=128, kind="g", scr=SC["gTa_" + g], r0=128 * k) for k in range(4)]
                    fm_tiles += [dict(c0=EVEN_COLS["bg"] + 128 * k, n=128, kind="g", scr=SC["gTb_" + g], r0=128 * k) for k in range(4)]
                    fm_tiles += [dict(c0=EVEN_COLS["af"], n=8, kind="f")]
                else:
                    tm_chunks = []
                    for k in range(2):
                        tm_chunks.append(dict(c0=512 * k, n=512, kind="Q", rope=True, out=None, scr=SC["qTc_" + g], scale=0.125, r0=512 * k))
                    for k in range(2):
                        tm_chunks.append(dict(c0=1024 + 512 * k, n=512, kind="K", rope=True, out=O["c_k_" + g][li], scr=SC["kTc_" + g],
                                              scale=1.0, r0=512 * k, oc0=512 * k))
                    for k in range(2):
                        tm_chunks.append(dict(c0=2048 + 512 * k, n=512, kind="V", rope=False, out=O["c_v_" + g][li], scr=SC["vc_" + g],
                                              dv=128, h0=4 * k, oc0=512 * k))
                    fm_tiles = [dict(c0=3072 + 128 * k, n=128, kind="go", scr=SC["gTc_" + g], r0=128 * k) for k in range(8)]

                for tb in range(T // TB):
                    t0 = tb * TB
                    for s in range(nsub):
                        r0 = t0 + s * 128
                        x_, xn = xt.next()
                        S.dma("sp", _mk("dma_start", out=x_[:], in_=xsrc[r0:r0 + 128, :]), writes=[xn])
                        sm, smn = small.next()
                        S.add("act", _mk("activation", out=junk[:], in_=x_[:], func=AF.Square, accum_out=sm[:, 0:1]),
                              reads=[xn], writes=["junkA", smn])
                        S.add("dve", _mk("tensor_scalar", out=sm[:, 1:2], in0=sm[:, 0:1], scalar1=1.0 / D, scalar2=1e-6,
                                                                       op0=ALU.mult, op1=ALU.add), reads=[smn], writes=[smn])
                        S.add("act", _mk("activation", out=sm[:, 1:2], in_=sm[:, 1:2], func=AF.Sqrt), reads=[smn], writes=[smn])
                        S.add("dve", _mk("reciprocal", out=sm[:, 1:2], in_=sm[:, 1:2]), reads=[smn], writes=[smn])
                        h_, hn = hb.next()
                        S.add("dve", _mk("scalar_tensor_tensor", out=h_[:], in0=x_[:], scalar=sm[:, 1:2], in1=gb[:],
                                                                                         op0=ALU.mult, op1=ALU.mult),
                              reads=[xn, smn, "gb"], writes=[hn])
                        for kc in range(8):
                            S.add("pe", _mk("transpose", out=pT[:, kc, :], in_=h_[:, kc * 128:(kc + 1) * 128], identity=identb[:]),
                                  reads=[hn], writes=["pT"])
                        S.add("act", _mk("copy", out=hT[:, :, s * 128:(s + 1) * 128], in_=pT[:]), reads=["pT"], writes=[("hT", s)])
                    hTall = [("hT", s) for s in range(nsub)]

                    for s in range(nsub if "notm" not in dbg else 0):
                        r0 = t0 + s * 128
                        jt = r0 // 128
                        for ch in tm_chunks:
                            n = ch["n"]; c0 = ch["c0"]
                            ps, psn = ptm.next()
                            for kc in range(8):
                                S.add("pe", _mk("matmul",
                                    ps[:, 0:n], lhsT=hT[:, kc, s * 128:(s + 1) * 128], rhs=W[:, kc, c0:c0 + n], start=(kc == 0), stop=(kc == 7)),
                                    reads=[("hT", s)] + wres(c0, n), writes=[psn])
                            st_, stn = st.next()
                            S.add("act", _mk("copy", out=st_[:, 0:n], in_=ps[:, 0:n]), reads=[psn], writes=[stn])
                            if ch["rope"] and "norope" not in dbg:
                                nh = n // 64
                                psv = ps[:, 0:n].rearrange("p (h d) -> p h d", d=64)
                                stv = st_[:, 0:n].rearrange("p (h d) -> p h d", d=64)
                                cosb = G["cos"][:, jt, :].unsqueeze(1).broadcast_to([128, nh, 8])
                                sinb = G["sin"][:, jt, :].unsqueeze(1).broadcast_to([128, nh, 8])
                                x1 = psv[:, :, 0:8]; x2 = psv[:, :, 8:16]
                                ta, tan = rtmp.next(); tbb, tbn = rtmp.next()
                                S.add("dve", _mk("tensor_tensor", out=ta[:, 0:nh, :], in0=x1, in1=cosb, op=ALU.mult),
                                      reads=[psn, "rope"], writes=[tan])
                                S.add("dve", _mk("tensor_tensor", out=tbb[:, 0:nh, :], in0=x2, in1=sinb, op=ALU.mult),
                                      reads=[psn, "rope"], writes=[tbn])
                                S.add("dve", _mk("tensor_tensor", out=stv[:, :, 0:8], in0=ta[:, 0:nh, :], in1=tbb[:, 0:nh, :],
                                                                                                   op=ALU.subtract), reads=[tan, tbn], writes=[stn])
                                ta, tan = rtmp.next(); tbb, tbn = rtmp.next()
                                S.add("dve", _mk("tensor_tensor", out=ta[:, 0:nh, :], in0=x2, in1=cosb, op=ALU.mult),
                                      reads=[psn, "rope"], writes=[tan])
                                S.add("dve", _mk("tensor_tensor", out=tbb[:, 0:nh, :], in0=x1, in1=sinb, op=ALU.mult),
                                      reads=[psn, "rope"], writes=[tbn])
                                S.add("dve", _mk("tensor_tensor", out=stv[:, :, 8:16], in0=ta[:, 0:nh, :], in1=tbb[:, 0:nh, :],
                                                                                                   op=ALU.add), reads=[tan, tbn], writes=[stn])
                            kind = ch["kind"]
                            if ch["out"] is not None:
                                oc0 = ch.get("oc0", 0)
                                S.dma("pool", _mk("dma_start", out=ch["out"][r0:r0 + 128, oc0:oc0 + n], in_=st_[:, 0:n]),
                                      reads=[stn])
                            if kind in ("K", "Q"):
                                kb_, kbn = kb.next()
                                S.add("act", _mk("activation", out=kb_[:, 0:n], in_=st_[:, 0:n], func=AF.Copy, scale=ch["scale"]),
                                      reads=[stn], writes=[kbn])
                                ng = (n + 127) // 128
                                w_ = min(n, 128)
                                pt_, ptn = ptr.next()
                                for gi in range(ng):
                                    S.add("pe", _mk("transpose", out=pt_[0:w_, gi, :], in_=kb_[:, gi * 128:gi * 128 + w_],
                                                                                                   identity=identb[:]), reads=[kbn], writes=[ptn])
                                kt_, ktn = kTst.next()
                                S.add("dve", _mk("tensor_copy", out=kt_[0:w_, 0:ng, :], in_=pt_[0:w_, 0:ng, :]),
                                      reads=[ptn], writes=[ktn])
                                rr0 = ch.get("r0", 0)
                                if n >= 128:
                                    dst = ch["scr"][rr0:rr0 + n, r0:r0 + 128].rearrange("(g p) t -> p g t", p=128)
                                    S.dma("pool", _mk("dma_start", out=dst, in_=kt_[:, 0:ng, :]), reads=[ktn])
                                else:
                                    dst = ch["scr"][0:n, r0:r0 + 128]
                                    S.dma("pool", _mk("dma_start", out=dst, in_=kt_[0:n, 0, :]), reads=[ktn])
                            elif kind == "V":
                                kb_, kbn = kb.next()
                                S.add("act", _mk("copy", out=kb_[:, 0:n], in_=st_[:, 0:n]), reads=[stn], writes=[kbn])
                                dv = ch["dv"]; h0 = ch.get("h0", 0); nh = n // dv
                                dst = ch["scr"][h0:h0 + nh, :, jt, :].rearrange("h p d -> p h d")
                                S.dma("pool", _mk("dma_start", out=dst, in_=kb_[:, 0:n].rearrange("p (h d) -> p h d", d=dv)),
                                      reads=[kbn])

                    for ft in fm_tiles:
                        if "nofm" in dbg or ("nof" in dbg and ft["kind"] == "f"):
                            continue
                        n = ft["n"]; c0 = ft["c0"]
                        ps, psn = pfm.next()
                        for kc in range(8):
                            S.add("pe", _mk("matmul", ps[0:n, 0:TB], lhsT=W[:, kc, c0:c0 + n], rhs=hT[:, kc, 0:TB],
                                                                                  start=(kc == 0), stop=(kc == 7)),
                                  reads=hTall + wres(c0, n), writes=[psn])
                        kind = ft["kind"]
                        if kind in ("q", "g", "go"):
                            fs, fsn = fmst.next()
                            if kind == "q":
                                S.add("act", _mk("activation", out=fs[:, 0:TB], in_=ps[:, 0:TB], func=AF.Copy, scale=0.125),
                                      reads=[psn], writes=[fsn])
                            else:
                                S.add("act", _mk("activation", out=fs[:, 0:TB], in_=ps[:, 0:TB], func=AF.Silu), reads=[psn], writes=[fsn])
                                if kind == "go":
                                    S.add("dve", _mk("tensor_scalar", out=fs[:, 0:TB], in0=fs[:, 0:TB], scalar1=gainp[:, 0:1], scalar2=None,
                                                                                   op0=ALU.mult), reads=[fsn, "gainp"], writes=[fsn])
                            rr0 = ft["r0"]
                            S.dma("pool", _mk("dma_start", out=ft["scr"][rr0:rr0 + 128, t0:t0 + TB], in_=fs[:, 0:TB]),
                                  reads=[fsn])
                        else:
                            S.add("act", _mk("activation", out=lfe[:, 0:TB], in_=ps[0:8, 0:TB], func=AF.Exp, scale=-1.0, bias=negb[:, 0:1]),
                                  reads=[psn, "negb"], writes=["lfe"])
                            S.add("act", _mk("activation", out=lfe[:, 0:TB], in_=lfe[:, 0:TB], func=AF.Ln, bias=1.0, scale=1.0),
                                  reads=["lfe"], writes=["lfe"])
                            S.add("dve", _mk("tensor_scalar", out=lfT[:, 0:TB], in0=lfe[:, 0:TB], scalar1=-1.0, scalar2=None, op0=ALU.mult),
                                  reads=["lfe"], writes=["lfT"])
                            for s in range(nsub):
                                S.add("pe", _mk("transpose", out=plt[:, s, :], in_=lfT[0:8, s * 128:(s + 1) * 128], identity=identf[0:8, 0:8]),
                                      reads=["lfT"], writes=["plt"])
                            if g == "p":
                                S.add("dve", _mk("tensor_tensor_scan", out=cumT[:, 0:TB], data0=ones8[:, 0:TB], data1=lfT[:, 0:TB], initial=carry[:, 0:1],
                                                                              op0=ALU.mult, op1=ALU.add), reads=["lfT", "carry"], writes=["cumT"])
                                S.add("dve", _mk("tensor_copy", out=carry[:, 0:1], in_=cumT[:, TB - 1:TB]), reads=["cumT"], writes=["carry"])
                                S.add("dve", _mk("tensor_copy", out=chi[:, 0:TB], in_=cumT[:, 0:TB]), reads=["cumT"], writes=["chi"])
                                S.add("dve", _mk("tensor_copy", out=chf[:, 0:TB], in_=chi[:, 0:TB]), reads=["chi"], writes=["chf"])
                                S.add("dve", _mk("tensor_tensor", out=clo[:, 0:TB], in0=cumT[:, 0:TB], in1=chf[:, 0:TB], op=ALU.subtract),
                                      reads=["cumT", "chf"], writes=["clo"])
                                S.dma("pool", _mk("dma_start", out=SC["cqa_p"][:, 0, t0:t0 + TB], in_=chi[:, 0:TB]), reads=["chi"])
                                S.dma("pool", _mk("dma_start", out=SC["cqa_p"][:, 1, t0:t0 + TB], in_=clo[:, 0:TB]), reads=["clo"])
                                for s in range(nsub):
                                    S.add("pe", _mk("transpose", out=plt[:, 4 + s, :], in_=cumT[0:8, s * 128:(s + 1) * 128], identity=identf[0:8, 0:8]),
                                          reads=["cumT"], writes=["plt"])
                            S.add("dve", _mk("tensor_copy", out=lftm[:, 0:nsub, :], in_=plt[:, 0:nsub, :]), reads=["plt"], writes=["lftm"])
                            S.dma("pool", _mk("dma_start", out=O["a_f_" + g][li][t0:t0 + TB, :].rearrange("(s p) h -> p s h", p=128),
                                                                         in_=lftm[:, 0:nsub, :]), reads=["lftm"])
                            if g == "p":
                                S.add("dve", _mk("tensor_scalar", out=lftm[:, 4:4 + nsub, :], in0=plt[:, 4:4 + nsub, :], scalar1=-1.0, scalar2=None, op0=ALU.mult),
                                      reads=["plt"], writes=["lftm2"])
                                S.dma("pool", _mk("dma_start", out=SC["nck_p"][t0:t0 + TB, :].rearrange("(s p) h -> p s h", p=128),
                                                                        in_=lftm[:, 4:4 + nsub, :]), reads=["lftm2"])
                            else:
                                S.dma("pool", _mk("dma_start", out=SC["nck_s"][0:128, :], in_=lftm[:, 0, :]), reads=["lftm"])
            S.flush()

    def phase_B(L):
        even = (L % 2 == 0)
        li = L // 2
        last = (L == NL - 1)
        wout = (I["w_out_even"] if even else I["w_out_odd"])[li]
        dbgB = cfg.get("dbg", "")
        NKMAX = max(NP, ((NKS + 127) // 128) * 128)
        with ExitStack() as ph:
            wo = sb(ph, "wo", [128, 8, D], BF16)
            gbf = sb(ph, "gbf", [128, D], F32)
            cmask = sb(ph, "cmasks", [128, 4, 512], BF16)
            pbuf = Rot([(sb(ph, f"pbuf{k}", [128, 512], BF16), f"pbuf{k}") for k in range(3)])
            qTt = Rot([(sb(ph, f"qTt{k}", [66, 512], BF16), f"qTt{k}") for k in range(2)])
            gt = Rot([(sb(ph, f"gt{k}", [128, 512], BF16), f"gt{k}") for k in range(2)])
            ftmp = Rot([(sb(ph, f"ftmp{k}", [128, 512], F32), f"ftmp{k}") for k in range(4)])
            sqb = sb(ph, "sqb", [128, 512], BF16)
            mix = sb(ph, "mix", [128, 8, 512], BF16)
            xo = Rot([(sb(ph, f"xo{k}", [128, D], F32), f"xo{k}") for k in range(1)])
            xn_ = Rot([(sb(ph, f"xn{k}", [128, D], F32), f"xn{k}") for k in range(1)])
            junkB = sb(ph, "junkB", [128, D], BF16)
            smallB = Rot([(sb(ph, f"smB{k}", [128, 2], F32), f"smB{k}") for k in range(2)])
            nck = sb(ph, "nck", [128, max(NTP, NPT + 1), 8], F32)
            neglam = sb(ph, "neglam", [128, 1], F32)
            lp = sb(ph, "lp", [128, 4, 64], F32)
            lpt = sb(ph, "lpt", [128, 64], F32)
            lps = sb(ph, "lps", [128, 2], F32)
            if even:
                qit = Rot([(sb(ph, f"qit{k}", [64, 8, 128], BF16), f"qit{k}") for k in range(2)])
                wit = Rot([(sb(ph, f"wit{k}", [128, 8], F32), f"wit{k}") for k in range(2)])
                diagw = sb(ph, "diagw", [128, 8, 128], BF16)
                rbuf = Rot([(sb(ph, f"rbuf{k}", [128, 512], BF16), f"rbuf{k}") for k in range(4)])
                bis = sb(ph, "bis", [128, 8], F32)
                hwtab = sb(ph, "hwtab", [128, NIT], F32)
            B = {}
            sbank = Rot([(psb(ph, f"sbank{k}", [128, 512], F32), f"sbank{k}") for k in range(2)])
            acc = [(psb(ph, f"acc{k}", [128, 512], F32), f"acc{k}") for k in range(4)]
            ptrB = psb(ph, "ptrB", [128, 8, 128], BF16)
            pmisc = psb(ph, "pmisc", [128, 512], F32)

            S.dma("pool", _mk("dma_start", out=wo[:], in_=wout.rearrange("(kc p) n -> p kc n", p=128)), writes=["wo"])
            if last:
                S.dma("sp", _mk("dma_start", out=gbf[:], in_=I["final_gain"].partition_broadcast(128)), writes=["gbf"])
            S.dma("pool", _mk("dma_start", out=cmask[:], in_=(I["cmask"] if even else I["ccmask"]).rearrange("j p q -> p j q")), writes=["const"])
            if even:
                S.dma("sp", _mk("dma_start", out=nck[:, 0:NTP, :], in_=SC["nck_p"].rearrange("(j p) h -> p j h", p=128)), writes=["nck"])
            else:
                S.dma("sp", _mk("dma_start", out=lp[:].rearrange("p a b -> p (a b)"),
                                                  in_=I["lam"][li].rearrange("a b -> (a b)").partition_broadcast(128)), writes=["lp"])
                for k in range(2):
                    S.add("dve", _mk("tensor_tensor", out=lpt[:], in0=lp[:, 2 * k, :], in1=lp[:, 2 * k + 1, :], op=ALU.mult),
                          reads=["lp"], writes=["lpt"])
                    S.add("dve", _mk("tensor_reduce", out=lps[:, k:k + 1], in_=lpt[:], axis=AX.X, op=ALU.add), reads=["lpt"], writes=["lps"])
                S.add("act", _mk("activation", out=lps[:], in_=lps[:], func=AF.Exp), reads=["lps"], writes=["lps"])
                S.add("dve", _mk("tensor_tensor", out=neglam[:], in0=lps[:, 1:2], in1=lps[:, 0:1], op=ALU.subtract), reads=["lps"], writes=["neglam"])
                S.add("dve", _mk("tensor_scalar", out=neglam[:], in0=neglam[:], scalar1=-LAM_INIT[L], scalar2=None, op0=ALU.add),
                      reads=["neglam"], writes=["neglam"])

            def attend(qT_ap, qres, QW, chunks, dv, O_ap, SM_ap, ores):
                if isinstance(chunks, list) and chunks and isinstance(chunks[0], dict):
                    chunks = [(len(chunks), (lambda t=chunks: t))]
                n = sum(c[0] for c in chunks)
                starts = []
                acc_ = 0
                for c in chunks:
                    starts.append(acc_)
                    acc_ += c[0]
                loaded = []
                state = {"next": 0}

                def load_next():
                    if state["next"] < len(chunks):
                        loaded.extend(chunks[state["next"]][1]())
                        state["next"] += 1

                def stage_a(kt):
                    kn = kt["kn"]
                    ps, psn = sbank.next()
                    am = kt.get("addmask")
                    S.add("pe", _mk("matmul", ps[0:kn, 0:QW], lhsT=kt["kT"], rhs=qT_ap, start=True, stop=(am is None)),
                          reads=[kt["kres"], qres], writes=[psn])
                    if am is not None:
                        S.add("pe", _mk("matmul", ps[0:kn, 0:QW], lhsT=identb[0:kn, 0:kn], rhs=am, start=False, stop=True),
                              reads=["const"] + ([kt["mres"]] if "mres" in kt else []), writes=[psn])
                    pt, ptn = pbuf.next()
                    bias = kt.get("bias")
                    if bias is not None:
                        S.add("act", _mk("activation", out=pt[0:kn, 0:QW], in_=ps[0:kn, 0:QW], func=AF.Exp, bias=bias),
                              reads=[psn, kt["bres"]], writes=[ptn])
                    else:
                        S.add("act", _mk("activation", out=pt[0:kn, 0:QW], in_=ps[0:kn, 0:QW], func=AF.Exp),
                              reads=[psn], writes=[ptn])
                    return pt, ptn

                def stage_b(kt, pt, ptn, idx):
                    kn = kt["kn"]
                    S.add("pe", _mk("matmul", O_ap, lhsT=kt["v"], rhs=pt[0:kn, 0:QW], start=(idx == 0), stop=(idx == n - 1)),
                          reads=[ptn, kt["vres"]], writes=[ores[0]])
                    S.add("pe", _mk("matmul", SM_ap, lhsT=onesb[0:kn, 0:dv], rhs=pt[0:kn, 0:QW], start=(idx == 0), stop=(idx == n - 1)),
                          reads=[ptn], writes=[ores[1]])

                load_next()
                cur = stage_a(loaded[0])
                for t in range(n):
                    if t in starts:
                        load_next()
                    nxt = None
                    if t + 1 < n:
                        while len(loaded) < t + 2:
                            load_next()
                        nxt = stage_a(loaded[t + 1])
                    stage_b(loaded[t], cur[0], cur[1], t)
                    cur = nxt

            def prompt_tiles(kscr, krow0, Kc, vscr, vh, vd0, dv, KE, bias_h=None, masks=None, j_diag0=None, mul=None, mres=None):
                chunks = []
                for c in range((KE + 15) // 16):
                    j0 = c * 16
                    nt = min(16, KE - j0)

                    def thunk(j0=j0, nt=nt):
                        tiles = []
                        kb_, kbn = kTb_.next()
                        vv, vvn = vb_.next()
                        S.dma("sp", _mk("dma_start", out=kb_[0:64, 0:nt * 128], in_=kscr[krow0:krow0 + 64, j0 * 128:(j0 + nt) * 128]),
                              writes=[kbn])
                        S.dma("sp", _mk("dma_start", out=vv[:, 0:nt, 0:dv], in_=vscr[vh, :, j0:j0 + nt, vd0:vd0 + dv]), writes=[vvn])
                        for jj in range(nt):
                            j = j0 + jj
                            t = dict(kn=128, kT=kb_[0:Kc, jj * 128:(jj + 1) * 128], kres=kbn, v=vv[:, jj, 0:dv], vres=vvn)
                            if bias_h is not None:
                                t["bias"] = nck[:, j, bias_h:bias_h + 1]; t["bres"] = "nck"
                            if masks is not None and j >= j_diag0:
                                t["addmask"] = masks[:, j - j_diag0, :]
                                if mres is not None:
                                    t["mres"] = mres
                            if mul is not None:
                                t["mulmask"] = mul[:, j, :]; t["mres"] = "selT"
                            tiles.append(t)
                        return tiles
                    chunks.append((nt, thunk))
                return chunks

            def fin_pair(O_, SM_, ores, gscr, grow0, qcol0, QW, ct, mcol0):
                g_, gn = gt.next()
                S.dma("sp", _mk("dma_start", out=g_[:, 0:QW], in_=gscr[grow0:grow0 + 128, qcol0:qcol0 + QW]), writes=[gn])
                r_, rn = ftmp.next()
                S.add("dve", _mk("reciprocal", out=r_[:, 0:QW], in_=SM_[:, 0:QW]), reads=[ores[1]], writes=[rn])
                S.add("dve", _mk("tensor_tensor", out=r_[:, 0:QW], in0=O_[:, 0:QW], in1=r_[:, 0:QW], op=ALU.mult), reads=[ores[0], rn], writes=[rn])
                S.add("dve", _mk("tensor_tensor", out=mix[:, ct, mcol0:mcol0 + QW], in0=r_[:, 0:QW], in1=g_[:, 0:QW], op=ALU.mult),
                      reads=[rn, gn], writes=[("mix", ct)])

            def fin_diff(gscr, grow0, qcol0, QW, ct, mcol0):
                g_, gn = gt.next()
                S.dma("sp", _mk("dma_start", out=g_[:, 0:QW], in_=gscr[grow0:grow0 + 128, qcol0:qcol0 + QW]), writes=[gn])
                r0_, r0n = ftmp.next(); r1_, r1n = ftmp.next()
                (O0, o0n), (S0, s0n), (O1, o1n), (S1, s1n) = acc
                S.add("dve", _mk("reciprocal", out=r0_[:, 0:QW], in_=S0[:, 0:QW]), reads=[s0n], writes=[r0n])
                S.add("dve", _mk("tensor_tensor", out=r0_[:, 0:QW], in0=O0[:, 0:QW], in1=r0_[:, 0:QW], op=ALU.mult), reads=[o0n, r0n], writes=[r0n])
                S.add("dve", _mk("reciprocal", out=r1_[:, 0:QW], in_=S1[:, 0:QW]), reads=[s1n], writes=[r1n])
                S.add("dve", _mk("tensor_tensor", out=r1_[:, 0:QW], in0=O1[:, 0:QW], in1=r1_[:, 0:QW], op=ALU.mult), reads=[o1n, r1n], writes=[r1n])
                S.add("dve", _mk("scalar_tensor_tensor", out=r0_[:, 0:QW], in0=r1_[:, 0:QW], scalar=neglam[:, 0:1], in1=r0_[:, 0:QW],
                                                               op0=ALU.mult, op1=ALU.add), reads=[r0n, r1n, "neglam"], writes=[r0n])
                S.add("act", _mk("activation", out=sqb[:, 0:QW], in_=r0_[:, 0:QW], func=AF.Square), reads=[r0n], writes=["sqb"])
                S.add("pe", _mk("matmul", pmisc[:, 0:QW], lhsT=onesdiv[:], rhs=sqb[:, 0:QW], start=True, stop=True), reads=["sqb"], writes=["pmisc"])
                S.add("dve", _mk("tensor_scalar", out=r1_[:, 0:QW], in0=pmisc[:, 0:QW], scalar1=1e-6, scalar2=None, op0=ALU.add),
                      reads=["pmisc"], writes=[r1n])
                S.add("act", _mk("activation", out=r1_[:, 0:QW], in_=r1_[:, 0:QW], func=AF.Sqrt), reads=[r1n], writes=[r1n])
                S.add("dve", _mk("reciprocal", out=r1_[:, 0:QW], in_=r1_[:, 0:QW]), reads=[r1n], writes=[r1n])
                S.add("dve", _mk("tensor_tensor", out=r0_[:, 0:QW], in0=r0_[:, 0:QW], in1=r1_[:, 0:QW], op=ALU.mult), reads=[r0n, r1n], writes=[r0n])
                S.add("dve", _mk("tensor_tensor", out=mix[:, ct, mcol0:mcol0 + QW], in0=r0_[:, 0:QW], in1=g_[:, 0:QW], op=ALU.mult),
                      reads=[r0n, gn], writes=[("mix", ct)])

            def out_proj(g, t0, nsub):
                xsrc = (I["xp"] if g == "p" else I["xs"]) if L == 0 else SC["resid_" + g]
                for s in range(nsub):
                    r0 = t0 + s * 128
                    x_, xn = xo.next()
                    S.dma("sp", _mk("dma_start", out=x_[:], in_=xsrc[r0:r0 + 128, :]), writes=[xn])
                    y_, yn = xn_.next()
                    for half in range(2):
                        pa, pan = acc[half]
                        for ct in range(8):
                            S.add("pe", _mk("matmul", pa[:, :], lhsT=mix[:, ct, s * 128:(s + 1) * 128],
                                                                                      rhs=wo[:, ct, half * 512:(half + 1) * 512], start=(ct == 0), stop=(ct == 7)),
                                  reads=[("mix", ct), "wo"], writes=[pan])
                        S.add("dve", _mk("tensor_tensor", out=y_[:, half * 512:(half + 1) * 512], in0=pa[:, :],
                                                                                            in1=x_[:, half * 512:(half + 1) * 512], op=ALU.add),
                              reads=[pan, xn], writes=[yn])
                    if not last:
                        S.dma("pool", _mk("dma_start", out=SC["resid_" + g][r0:r0 + 128, :], in_=y_[:]), reads=[yn])
                    else:
                        sm, smn = smallB.next()
                        S.add("act", _mk("activation", out=junkB[:], in_=y_[:], func=AF.Square, accum_out=sm[:, 0:1]),
                              reads=[yn], writes=["junkB", smn])
                        S.add("dve", _mk("tensor_scalar", out=sm[:, 1:2], in0=sm[:, 0:1], scalar1=1.0 / D, scalar2=1e-6, op0=ALU.mult, op1=ALU.add),
                              reads=[smn], writes=[smn])
                        S.add("act", _mk("activation", out=sm[:, 1:2], in_=sm[:, 1:2], func=AF.Sqrt), reads=[smn], writes=[smn])
                        S.add("dve", _mk("reciprocal", out=sm[:, 1:2], in_=sm[:, 1:2]), reads=[smn], writes=[smn])
                        S.add("dve", _mk("scalar_tensor_tensor", out=y_[:], in0=y_[:], scalar=sm[:, 1:2], in1=gbf[:], op0=ALU.mult, op1=ALU.mult),
                              reads=[yn, smn, "gbf"], writes=[yn])
                        S.dma("pool", _mk("dma_start", out=O["y_" + g][r0:r0 + 128, :], in_=y_[:]), reads=[yn])

            def indexer(qiT_src, qc0, nq, wi_src, wr0, kiT_of, NV, diag_j, ul, topk, tail_to=None):
                q_, qn = qit.next()
                S.dma("sp", _mk("dma_start", out=q_[:, :, 0:nq], in_=qiT_src.rearrange("(h d) t -> d h t", d=64)[:, :, qc0:qc0 + nq]), writes=[qn])
                w_, wn = wit.next()
                S.dma("sp", _mk("dma_start", out=w_[0:nq, :], in_=wi_src[wr0:wr0 + nq, :]), writes=[wn])
                for hh in range(8):
                    S.add("dve", _mk("tensor_scalar", out=diagw[0:nq, hh, 0:nq], in0=identb[0:nq, 0:nq], scalar1=w_[0:nq, hh:hh + 1], scalar2=None,
                                     op0=ALU.mult), reads=[wn, "const"], writes=["diagw"])
                sc_ps, scn = acc[0]
                for kg in range((NV + 511) // 512):
                    k0 = kg * 512
                    n = min(512, NV - k0)
                    kap, kres = kiT_of(k0, n)

                    def dots(hh):
                        ps, psn = sbank.next()
                        S.add("pe", _mk("matmul", ps[0:nq, 0:n], lhsT=q_[:, hh, 0:nq], rhs=kap, start=True, stop=True),
                              reads=[qn, kres], writes=[psn])
                        r_, rn = rbuf.next()
                        if hh % 2 == 0:
                            S.add("act", _mk("activation", out=r_[0:nq, 0:n], in_=ps[0:nq, 0:n], func=AF.Relu), reads=[psn], writes=[rn])
                        else:
                            S.add("dve", _mk("tensor_scalar", out=r_[0:nq, 0:n], in0=ps[0:nq, 0:n], scalar1=0.0, scalar2=None, op0=ALU.max),
                                  reads=[psn], writes=[rn])
                        return r_, rn
                    cur = dots(0)
                    for hh in range(8):
                        nxt = dots(hh + 1) if hh + 1 < 8 else None
                        S.add("pe", _mk("matmul", sc_ps[0:nq, 0:n], lhsT=diagw[0:nq, hh, 0:nq], rhs=cur[0][0:nq, 0:n], start=(hh == 0), stop=(hh == 7)),
                              reads=[cur[1], "diagw"], writes=[scn])
                        cur = nxt
                    S.add("act", _mk("copy", out=B["scores"][0:nq, k0:k0 + n], in_=sc_ps[0:nq, 0:n]), reads=[scn], writes=["scores"])
                steps = []
                sc = B["scores"]; sl = B["sel"]

                def prep():
                    S.add("dve", _mk("tensor_reduce", out=bis[0:nq, 0:1], in_=sc[0:nq, 0:NV], axis=AX.X, op=ALU.min), reads=["scores"], writes=["bis"])
                    if diag_j is not None:
                        S.add("dve", _mk("tensor_tensor", out=sc[0:nq, diag_j * 128:(diag_j + 1) * 128], in0=sc[0:nq, diag_j * 128:(diag_j + 1) * 128],
                                         in1=idxmask[0:nq, :], op=ALU.add), reads=["scores", "const"], writes=["scores"])
                    S.add("dve", _mk("tensor_reduce", out=bis[0:nq, 1:2], in_=sc[0:nq, 0:NV], axis=AX.X, op=ALU.max), reads=["scores"], writes=["bis"])
                    S.add("dve", _mk("scalar_tensor_tensor", out=bis[0:nq, 1:2], in0=bis[0:nq, 1:2], scalar=1.0, in1=bis[0:nq, 0:1], op0=ALU.add, op1=ALU.subtract),
                          reads=["bis"], writes=["bis"])
                    S.add("dve", _mk("tensor_scalar", out=hwtab[0:nq, :], in0=pow2[0:nq, :], scalar1=bis[0:nq, 1:2], scalar2=None, op0=ALU.mult),
                          reads=["bis", "const"], writes=["hwtab"])
                steps.append(prep)

                def it(k):
                    S.add("dve", _mk("tensor_tensor", out=bis[0:nq, 2:3], in0=bis[0:nq, 0:1], in1=hwtab[0:nq, k:k + 1], op=ALU.add),
                          reads=["bis", "hwtab"], writes=["bis"])
                    S.add("dve", _mk("tensor_scalar", out=sl[0:nq, ul, 0:NV], in0=sc[0:nq, 0:NV], scalar1=bis[0:nq, 2:3], scalar2=0.0,
                                     op0=ALU.is_ge, op1=ALU.add, accum_out=bis[0:nq, 3:4]), reads=["scores", "bis"], writes=[("sel", ul), "bis"])
                    S.add("dve", _mk("scalar_tensor_tensor", out=bis[0:nq, 4:5], in0=bis[0:nq, 3:4], scalar=topk - 0.5, in1=hwtab[0:nq, k:k + 1],
                                     op0=ALU.is_ge, op1=ALU.mult), reads=["bis", "hwtab"], writes=["bis"])
                    S.add("dve", _mk("tensor_tensor", out=bis[0:nq, 0:1], in0=bis[0:nq, 0:1], in1=bis[0:nq, 4:5], op=ALU.add), reads=["bis"], writes=["bis"])
                for k in range(NIT if "nobis" not in dbgB else 0):
                    steps.append(lambda k=k: it(k))

                def fin():
                    S.add("dve", _mk("tensor_scalar", out=sl[0:nq, ul, 0:NV], in0=sc[0:nq, 0:NV], scalar1=bis[0:nq, 0:1], scalar2=None, op0=ALU.is_ge),
                          reads=["scores", "bis"], writes=[("sel", ul)])
                    if tail_to is not None and NV < tail_to:
                        S.add("pool", _mk("memset", sl[:, ul, NV:tail_to], 0.0), writes=[("sel", ul)])
                steps.append(fin)
                return steps

            with ExitStack() as sec:
                kTb_ = Rot([(sb(sec, f"kTbuf{k}", [66, 2048], BF16), f"kTbuf{k}") for k in range(2)])
                vb_ = Rot([(sb(sec, f"vbuf{k}", [128, 16, 128], BF16), f"vbuf{k}") for k in range(2)])
                for k in range(2):
                    t_, tn = kTb_.next()
                    S.add("pool", _mk("memset", t_[64:66, :], 1.0), writes=[tn])
                if even:
                    B["scores"] = sb(sec, "scores", [128, NP], F32)
                    B["sel"] = sb(sec, "sel", [128, 2, NP], BF16)
                    B["selT"] = sb(sec, "selT", [128, NTP, 256], BF16)
                    kic = Rot([(sb(sec, f"kic{k}", [64, 512], BF16), f"kic{k}") for k in range(2)])
                if even:
                    def fox_head(m, h):
                        Q0 = 512 * m
                        KE = 4 * (m + 1)
                        q_, qn = qTt.next()
                        S.dma("sp", _mk("dma_start", out=q_[0:64, :], in_=SC["qTa_p"][h * 64:(h + 1) * 64, Q0:Q0 + 512]), writes=[qn])
                        S.dma("sp", _mk("dma_start", out=q_[64:66, :], in_=SC["cqa_p"][h, :, Q0:Q0 + 512]), writes=[qn])
                        tiles = prompt_tiles(SC["kTa_p"], h * 64, 66, SC["va_p"], h, 0, 64, KE, bias_h=h, masks=cmask, j_diag0=4 * m)
                        pb = (h % 2) * 64
                        (Oa, on), (Sa, sn) = acc[0], acc[1]
                        attend(q_[0:66, 0:512], qn, 512, tiles, 64, Oa[pb:pb + 64, :], Sa[pb:pb + 64, :], (on, sn))
                        if h % 2 == 1:
                            fin_pair(Oa, Sa, (on, sn), SC["gTa_p"], (h // 2) * 128, Q0, 512, h // 2, 0)

                    def dsa_head(m, u2, h):
                        q0 = 512 * m + 256 * u2
                        KE2 = 4 * m + 2 * u2 + 2
                        q_, qn = qTt.next()
                        S.dma("sp", _mk("dma_start", out=q_[0:64, 0:256], in_=SC["qTb_p"][h * 64:(h + 1) * 64, q0:q0 + 256]), writes=[qn])
                        tiles = prompt_tiles(SC["kTb_p"], h * 64, 64, SC["vb_p"], h, 0, 64, KE2, masks=B["selT"], j_diag0=0, mres="selT")
                        pb = (h % 2) * 64
                        (Oa, on), (Sa, sn) = acc[2], acc[3]
                        attend(q_[0:64, 0:256], qn, 256, tiles, 64, Oa[pb:pb + 64, 0:256], Sa[pb:pb + 64, 0:256], (on, sn))
                        if h % 2 == 1:
                            fin_pair(Oa, Sa, (on, sn), SC["gTb_p"], (h // 2) * 128, q0, 256, 4 + h // 2, 256 * u2)

                    def idx_part(m, u2, ul):
                        i = 4 * m + 2 * u2 + ul
                        NV = 128 * (i + 1)
                        KE2 = 4 * m + 2 * u2 + 2

                        def kiT_of(k0, n):
                            kc_, kcn = kic.next()
                            S.dma("sp", _mk("dma_start", out=kc_[:, 0:n], in_=SC["kiT_p"][:, k0:k0 + n]), writes=[kcn])
                            return kc_[:, 0:n], kcn
                        if "noidx" in dbgB:
                            return []
                        return indexer(SC["qiT_p"], i * 128, 128, SC["wi_p"], i * 128, kiT_of, NV, i, ul, TOPK_P, tail_to=KE2 * 128)

                    def sel_transposes(m, u2):
                        KE2 = 4 * m + 2 * u2 + 2
                        for j in range(KE2):
                            for ul in range(2):
                                S.add("pe", _mk("transpose", out=ptrB[:, ul, :], in_=B["sel"][:, ul, j * 128:(j + 1) * 128], identity=identb[:]),
                                      reads=[("sel", ul)], writes=["ptrB"])
                            S.add("act", _mk("activation", out=B["selT"][:, j, :], in_=ptrB[:, 0:2, :].rearrange("p a b -> p (a b)"), func=AF.Identity,
                                             scale=-NEGM, bias=NEGM), reads=["ptrB"], writes=["selT"])

                    subs = [(m, u2) for m in range(NB) for u2 in range(2)]
                    for ul in range(2):
                        for st_ in idx_part(subs[0][0], subs[0][1], ul):
                            st_()
                    for si, (m, u2) in enumerate(subs):
                        if "nodsaattn" not in dbgB:
                            sel_transposes(m, u2)
                        nxt = subs[si + 1] if si + 1 < len(subs) else None
                        heads = []
                        if "nofox" not in dbgB:
                            heads += [("fox", h) for h in (range(0, 4) if u2 == 0 else range(4, 8))]
                        if "nodsaattn" not in dbgB:
                            heads += [("dsa", h) for h in range(8)]
                        half = (len(heads) + 1) // 2
                        for part in range(2):
                            pend = idx_part(nxt[0], nxt[1], part) if nxt is not None else []
                            hs_ = heads[:half] if part == 0 else heads[half:]
                            per = (len(pend) + max(1, len(hs_)) - 1) // max(1, len(hs_))
                            for kind, h in hs_:
                                if kind == "fox":
                                    fox_head(m, h)
                                else:
                                    dsa_head(m, u2, h)
                                for _ in range(min(per, len(pend))):
                                    pend.pop(0)()
                            while pend:
                                pend.pop(0)()
                        if u2 == 1:
                            out_proj("p", 512 * m, 4)
                for m in range(NB if not even else 0):
                    Q0 = 512 * m
                    KE = 4 * (m + 1)
                    if even:
                        pass
                    else:
                        for h in range(8):
                            for c in range(2):
                                hs = 2 * h + c
                                q_, qn = qTt.next()
                                S.dma("sp", _mk("dma_start", out=q_[0:64, :], in_=SC["qTc_p"][hs * 64:(hs + 1) * 64, Q0:Q0 + 512]), writes=[qn])
                                tiles = prompt_tiles(SC["kTc_p"], hs * 64, 64, SC["vc_p"], h, 0, 128, KE, masks=cmask, j_diag0=4 * m)
                                (Oa, on), (Sa, sn) = acc[2 * c], acc[2 * c + 1]
                                attend(q_[0:64, 0:512], qn, 512, tiles, 128, Oa[:, :], Sa[:, :], (on, sn))
                            fin_diff(SC["gTc_p"], h * 128, Q0, 512, h, 0)
                    out_proj("p", Q0, 4)

                S.flush()

            with ExitStack() as sec:
                NKSP = ((NKS + 127) // 128) * 128
                if even:
                    B["scores"] = sb(sec, "scores_s", [128, NKSP], F32)
                    B["sel"] = sb(sec, "sel_s", [128, 1, NKSP], BF16)
                    B["selT"] = sb(sec, "selT_s", [128, NPT + 1, 32], BF16)
                    B["kTs"] = sb(sec, "kTs", [64, 8, NKS], BF16)
                    B["vs"] = sb(sec, "vs", [128, NPT + 1, 512], BF16)
                    B["cst"] = sb(sec, "cst", [128, NPT, 512], BF16)
                    B["lfs"] = sb(sec, "lfs", [128, NPT + 1, 8], F32)
                    B["kis"] = sb(sec, "kis", [64, NKS], BF16)
                    B["csk"] = sb(sec, "csk", [128, NPT, 64], BF16)
                else:
                    B["kTs"] = sb(sec, "kTs", [64, 16, NKS], BF16)
                    B["vs"] = sb(sec, "vs", [128, NPT + 1, 1024], BF16)
                    B["cst"] = sb(sec, "cst", [128, NPT, 1024], BF16)
                for b in range(4 if "nosampleB" not in dbgB else 0):
                    qc0 = 32 * b
                    ntile = NPT + 1

                    def load_kT(cache_ap, ncols, nsh, new_scr, dst):
                        S.dma("pool", _mk("dma_start", out=B["cst"][:, :, 0:ncols], in_=cache_ap.rearrange("(j p) c -> p j c", p=128)), writes=["cst"])
                        for sh in range(nsh):
                            for j4 in range(0, NPT, 4):
                                nj = min(4, NPT - j4)
                                for jj in range(nj):
                                    S.add("pe", _mk("transpose", out=ptrB[0:64, jj, :], in_=B["cst"][:, j4 + jj, sh * 64:(sh + 1) * 64], identity=identb[:]),
                                          reads=["cst"], writes=["ptrB"])
                                S.add("dve", _mk("tensor_copy", out=dst[0:64, sh, j4 * 128:(j4 + nj) * 128].rearrange("p (j t) -> p j t", t=128),
                                                                                           in_=ptrB[0:64, 0:nj, :]), reads=["ptrB"], writes=["kTs"])
                        S.dma("sp", _mk("dma_start", out=dst[0:64, 0:nsh, PAST:PAST + 32], in_=new_scr.rearrange("(h d) t -> d h t", d=64)[:, :, qc0:qc0 + 32]),
                              writes=["kTs"])

                    def load_v(cache_ap, ncols, new_scr, dv):
                        S.dma("pool", _mk("dma_start", out=B["vs"][:, 0:NPT, 0:ncols], in_=cache_ap.rearrange("(j p) c -> p j c", p=128)), writes=["vs"])
                        S.dma("sp", _mk("dma_start", out=B["vs"][0:32, NPT, 0:ncols].rearrange("p (h d) -> p h d", d=dv),
                                                          in_=new_scr[:, qc0:qc0 + 32, 0, :].rearrange("h p d -> p h d")), writes=["vs"])

                    def sample_tiles(sh, h, dv, bias_h=None, mask_new=None, mul=None):
                        tiles = []
                        for j in range(ntile):
                            kn = 128 if j < NPT else 32
                            t = dict(kn=kn, kT=B["kTs"][0:64, sh, j * 128:j * 128 + kn], kres="kTs", v=B["vs"][0:kn, j, h * dv:(h + 1) * dv], vres="vs")
                            if bias_h is not None:
                                t["bias"] = nck[0:kn, j, bias_h:bias_h + 1]; t["bres"] = "nck"
                            if mask_new is not None and j == NPT:
                                t["addmask"] = mask_new
                            if mul is not None:
                                t["addmask"] = mul[0:kn, j, 0:32]; t["mres"] = "selT"
                            tiles.append(t)
                        return tiles

                    if even:
                        load_kT(I["ca_k"][li, b], 512, 8, SC["kTa_s"], B["kTs"])
                        load_v(I["ca_v"][li, b], 512, SC["va_s"], 64)
                        S.add("dve", _mk("memset", B["lfs"][:, NPT, :], 0.0), writes=["lfs"])
                        S.dma("sp", _mk("dma_start", out=B["lfs"][:, 0:NPT, :], in_=I["ca_f"][li, b].rearrange("(j p) h -> p j h", p=128)), writes=["lfs"])
                        S.dma("sp", _mk("dma_start", out=B["lfs"][0:32, NPT, :], in_=SC["nck_s"][qc0:qc0 + 32, :]), writes=["lfs"])
                        cps, cpn = acc[0]
                        tps, tpn = acc[1]
                        for j in range(ntile):
                            S.add("pe", _mk("matmul", cps[:, j * 8:(j + 1) * 8], lhsT=utf[:], rhs=B["lfs"][:, j, :], start=True, stop=(j == 0)),
                                  reads=["lfs", "const"], writes=[cpn])
                            for j2 in range(j):
                                S.add("pe", _mk("matmul", cps[:, j * 8:(j + 1) * 8], lhsT=onesf[:], rhs=B["lfs"][:, j2, :], start=False, stop=(j2 == j - 1)),
                                      reads=["lfs", "const"], writes=[cpn])
                        for j in range(ntile):
                            S.add("pe", _mk("matmul", tps[:, 0:8], lhsT=onesf[:], rhs=B["lfs"][:, j, :], start=(j == 0), stop=(j == ntile - 1)),
                                  reads=["lfs", "const"], writes=[tpn])
                        S.add("act", _mk("copy", out=B["lfs"][:, 0, :], in_=tps[:, 0:8]), reads=[tpn], writes=["lfs"])
                        S.add("dve", _mk("tensor_tensor", out=nck[:, 0:ntile, :], in0=B["lfs"][:, 0, :].unsqueeze(1).broadcast_to([128, ntile, 8]),
                                                               in1=cps[:, 0:ntile * 8].rearrange("p (j h) -> p j h", h=8), op=ALU.subtract),
                              reads=[cpn, "lfs"], writes=["nck"])
                        for h in range(8):
                            q_, qn = qTt.next()
                            S.dma("sp", _mk("dma_start", out=q_[0:64, 0:32], in_=SC["qTa_s"][h * 64:(h + 1) * 64, qc0:qc0 + 32]), writes=[qn])
                            tiles = sample_tiles(h, h, 64, bias_h=h, mask_new=cmask_s[0:32, 0:32])
                            pb = (h % 2) * 64
                            (Oa, on), (Sa, sn) = acc[2], acc[3]
                            attend(q_[0:64, 0:32], qn, 32, tiles, 64, Oa[pb:pb + 64, 0:32], Sa[pb:pb + 64, 0:32], (on, sn))
                            if h % 2 == 1:
                                fin_pair(Oa, Sa, (on, sn), SC["gTa_s"], (h // 2) * 128, qc0, 32, h // 2, qc0)
                        S.dma("pool", _mk("dma_start", out=B["csk"][:], in_=I["cb_ki"][li, b].rearrange("(j p) c -> p j c", p=128)), writes=["csk"])
                        for j4 in range(0, NPT, 4):
                            nj = min(4, NPT - j4)
                            for jj in range(nj):
                                S.add("pe", _mk("transpose", out=ptrB[0:64, jj, :], in_=B["csk"][:, j4 + jj, :], identity=identb[:]), reads=["csk"], writes=["ptrB"])
                            S.add("dve", _mk("tensor_copy", out=B["kis"][0:64, j4 * 128:(j4 + nj) * 128].rearrange("p (j t) -> p j t", t=128), in_=ptrB[0:64, 0:nj, :]),
                                  reads=["ptrB"], writes=["kis"])
                        S.dma("sp", _mk("dma_start", out=B["kis"][0:64, PAST:PAST + 32], in_=SC["kiT_s"][:, qc0:qc0 + 32]), writes=["kis"])
                        for st_ in indexer(SC["qiT_s"], qc0, 32, SC["wi_s"], qc0, lambda k0, n: (B["kis"][0:64, k0:k0 + n], "kis"), NKS, None, 0, TOPK_S):
                            st_()
                        for j in range(ntile):
                            kn = 128 if j < NPT else 32
                            S.add("pe", _mk("transpose", out=ptrB[0:kn, 0, 0:32], in_=B["sel"][0:32, 0, j * 128:j * 128 + kn], identity=identb[0:32, 0:32]),
                                  reads=[("sel", 0)], writes=["ptrB"])
                            S.add("act", _mk("activation", out=B["selT"][0:kn, j, 0:32], in_=ptrB[0:kn, 0, 0:32], func=AF.Identity, scale=-NEGM, bias=NEGM), reads=["ptrB"], writes=["selT"])
                        load_kT(I["cb_k"][li, b], 512, 8, SC["kTb_s"], B["kTs"])
                        load_v(I["cb_v"][li, b], 512, SC["vb_s"], 64)
                        for h in range(8):
                            q_, qn = qTt.next()
                            S.dma("sp", _mk("dma_start", out=q_[0:64, 0:32], in_=SC["qTb_s"][h * 64:(h + 1) * 64, qc0:qc0 + 32]), writes=[qn])
                            tiles = sample_tiles(h, h, 64, mul=B["selT"])
                            pb = (h % 2) * 64
                            (Oa, on), (Sa, sn) = acc[2], acc[3]
                            attend(q_[0:64, 0:32], qn, 32, tiles, 64, Oa[pb:pb + 64, 0:32], Sa[pb:pb + 64, 0:32], (on, sn))
                            if h % 2 == 1:
                                fin_pair(Oa, Sa, (on, sn), SC["gTb_s"], (h // 2) * 128, qc0, 32, 4 + h // 2, qc0)
                    else:
                        load_kT(I["cc_k"][li, b], 1024, 16, SC["kTc_s"], B["kTs"])
                        load_v(I["cc_v"][li, b], 1024, SC["vc_s"], 128)
                        for h in range(8):
                            for c in range(2):
                                hs = 2 * h + c
                                q_, qn = qTt.next()
                                S.dma("sp", _mk("dma_start", out=q_[0:64, 0:32], in_=SC["qTc_s"][hs * 64:(hs + 1) * 64, qc0:qc0 + 32]), writes=[qn])
                                tiles = sample_tiles(hs, h, 128)
                                (Oa, on), (Sa, sn) = acc[2 * c], acc[2 * c + 1]
                                attend(q_[0:64, 0:32], qn, 32, tiles, 128, Oa[:, 0:32], Sa[:, 0:32], (on, sn))
                            fin_diff(SC["gTc_s"], h * 128, qc0, 32, h, qc0)
                out_proj("s", 0, 1)
                S.flush()

    only = cfg.get("only")
    for L in range(NL):
        if only is None or f"A{L}" in only:
            phase_A(L)
        if only is None or f"B{L}" in only:
            phase_B(L)

    top.close()
    return nc, None


def rope_tables(pos):
    half = 8
    inv = (500000.0 ** (-np.arange(half, dtype=np.float32) * 2.0 / 16)).astype(np.float32)
    ang = pos.astype(np.float32)[:, None] * inv[None, :]
    return np.cos(ang).astype(np.float32), np.sin(ang).astype(np.float32)


def make_consts(NP, PAST):
    c = {}
    c["ident"] = np.eye(128, dtype=np.float32)
    cp, sp = rope_tables(np.arange(NP))
    c["cosp"], c["sinp"] = cp, sp
    pos_s = PAST + (np.arange(128) % 32)
    cs, ss = rope_tables(pos_s)
    c["coss"], c["sins"] = cs, ss
    k = np.arange(128)[:, None]
    q = np.arange(512)[None, :]
    cm = np.zeros((4, 128, 512), np.float32)
    ccm = np.zeros((4, 128, 512), np.float32)
    for jj in range(4):
        kp = 128 * jj + k
        cm[jj] = np.where(kp <= q, 0.0, NEGM)
        ccm[jj] = np.where(kp // 64 <= q // 64, 0.0, NEGM)
    c["cmask"], c["ccmask"] = cm, ccm
    qq = np.arange(128)[:, None]
    kk = np.arange(128)[None, :]
    c["idxmask"] = np.where(kk // 64 <= qq // 64, 0.0, -1e30).astype(np.float32)
    k32 = np.arange(32)[:, None]
    q32 = np.arange(32)[None, :]
    c["cmask_s"] = np.where(k32 <= q32, 0.0, NEGM).astype(np.float32)
    c["ut"] = np.triu(np.ones((128, 128), np.float32))
    c["pow2"] = np.tile((0.5 ** (np.arange(NIT) + 1)).astype(np.float32)[None, :], (128, 1))
    return c


_CACHE = {}


def run(inputs, cfg, n_cores, prompt_of_core, sample_of_core):
    key = tuple(sorted((k, str(v)) for k, v in cfg.items()))
    if key not in _CACHE:
        _CACHE[key] = build(cfg)
    nc, _ = _CACHE[key]
    NP, PAST = cfg["NP"], cfg["PAST"]
    consts = make_consts(NP, PAST)
    f = lambda a: np.ascontiguousarray(np.asarray(a, dtype=np.float32))
    in_maps = []
    NO = cfg["NL"] // 2
    for c in range(n_cores):
        pb = prompt_of_core[c]
        sbs = sample_of_core[c]
        m = dict(consts)
        m["xp"] = f(inputs["x_prompt"][pb])
        m["xs"] = f(np.asarray(inputs["x_sample"])[sbs].reshape(128, D))
        m["ca_k"] = f(np.asarray(inputs["cache_a_k"])[:, sbs].reshape(-1, 4, PAST, 512))
        m["ca_v"] = f(np.asarray(inputs["cache_a_v"])[:, sbs].reshape(-1, 4, PAST, 512))
        m["ca_f"] = f(np.asarray(inputs["cache_a_logf"])[:, sbs])
        m["cb_k"] = f(np.asarray(inputs["cache_b_k"])[:, sbs].reshape(-1, 4, PAST, 512))
        m["cb_v"] = f(np.asarray(inputs["cache_b_v"])[:, sbs].reshape(-1, 4, PAST, 512))
        m["cb_ki"] = f(np.asarray(inputs["cache_b_kidx"])[:, sbs])
        if NO:
            m["cc_k"] = f(np.asarray(inputs["cache_c_k"])[:, sbs].reshape(-1, 4, PAST, 1024))
            m["cc_v"] = f(np.asarray(inputs["cache_c_v"])[:, sbs].reshape(-1, 4, PAST, 1024))
            m["w_in_odd"] = f(inputs["w_in_odd"]); m["w_out_odd"] = f(inputs["w_out_odd"])
            m["lam"] = f(inputs["lambda_params"]); m["chg"] = f(inputs["c_head_gain"])
        m["w_in_even"] = f(inputs["w_in_even"]); m["w_out_even"] = f(inputs["w_out_even"])
        m["norm_gain"] = f(inputs["norm_gain"]); m["final_gain"] = f(inputs["final_gain"])
        m["b_forget"] = f(inputs["b_forget"])
        in_maps.append(m)
    res = run_bass_kernel_spmd(nc, in_maps, core_ids=list(range(n_cores)))
    return res.results


def assemble(results, cfg, n_prompt, n_sample_total, prompt_of_core, sample_of_core):
    NP, NL = cfg["NP"], cfg["NL"]
    NE, NO = (NL + 1) // 2, NL // 2
    first_core = {}
    for c, pb in enumerate(prompt_of_core):
        first_core.setdefault(pb, c)

    def P(name, shape_tail, lead=None):
        arrs = [np.asarray(results[first_core[pb]][name]) for pb in range(n_prompt)]
        if lead is None:
            return np.stack(arrs, 0).reshape((n_prompt, NP) + shape_tail)
        return np.stack(arrs, 1).reshape((lead, n_prompt, NP) + shape_tail)

    def Sm(name, shape_tail, lead=None):
        if lead is None:
            out = np.zeros((n_sample_total, 32) + shape_tail, np.float32)
            for c, sbs in enumerate(sample_of_core):
                out[sbs] = np.asarray(results[c][name]).reshape((4, 32) + shape_tail)
            return out
        out = np.zeros((lead, n_sample_total, 32) + shape_tail, np.float32)
        for c, sbs in enumerate(sample_of_core):
            out[:, sbs] = np.asarray(results[c][name]).reshape((lead, 4, 32) + shape_tail)
        return out

    outs = [P("y_p", (D,)), Sm("y_s", (D,))]
    outs += [P("oa_k_p", (8, 64), NE), P("oa_v_p", (8, 64), NE), P("oa_f_p", (8,), NE),
             P("ob_k_p", (8, 64), NE), P("ob_v_p", (8, 64), NE), P("ob_ki_p", (64,), NE)]
    if NO:
        outs += [P("oc_k_p", (8, 128), NO), P("oc_v_p", (8, 128), NO)]
    outs += [Sm("oa_k_s", (8, 64), NE), Sm("oa_v_s", (8, 64), NE), Sm("oa_f_s", (8,), NE),
             Sm("ob_k_s", (8, 64), NE), Sm("ob_v_s", (8, 64), NE), Sm("ob_ki_s", (64,), NE)]
    if NO:
        outs += [Sm("oc_k_s", (8, 128), NO), Sm("oc_v_s", (8, 128), NO)]
    return tuple(np.ascontiguousarray(o, dtype=np.float32) for o in outs)


def kernel(**inputs):
    cfg = dict(NP=8192, NL=4, PAST=1024, TOPK_P=256, TOPK_S=256)
    n_cores = 8
    prompt_of_core = [c % 2 for c in range(n_cores)]
    sample_of_core = [list(range(4 * c, 4 * c + 4)) for c in range(n_cores)]
    results = run(inputs, cfg, n_cores, prompt_of_core, sample_of_core)
    return assemble(results, cfg, 2, 32, prompt_of_core, sample_of_core)
```

```python
import math
from contextlib import ExitStack
import numpy as np
import concourse.bass as bass
import concourse.mybir as mybir
from concourse.bass_utils import run_bass_kernel_spmd

F32 = mybir.dt.float32
BF16 = mybir.dt.bfloat16
AF = mybir.ActivationFunctionType
ALU = mybir.AluOpType
AX = mybir.AxisListType

ENGS = ("pe", "act", "dve", "pool", "sp")
N_DMA_SLOTS = 12


class Sched:
    def __init__(self, nc, stack):
        self.nc = nc
        self.ops = []
        self.start = 0
        self.res = {}
        self.dma_count = {e: 0 for e in ENGS}
        self.dma_slot_last = {}
        self.cnt = {e: 0 for e in ENGS}
        self.prev_final = {}
        self.excl = set()
        self.sems = {e: stack.enter_context(nc.semaphore("s_" + e)) for e in ENGS}
        self.dsems = {}
        for q in ("sp", "pool"):
            for sl in range(N_DMA_SLOTS):
                self.dsems[(q, sl)] = stack.enter_context(nc.semaphore(f"d_{q}_{sl}"))

    def _deps_for(self, reads, writes):
        deps = set()
        for r in reads:
            st = self.res.get(r)
            if st:
                deps.update(st["w"].values())
        for w in writes:
            st = self.res.get(w)
            if st:
                deps.update(st["w"].values())
                deps.update(st["r"])
        return deps

    def _commit(self, opid, key, reads, writes):
        for r in reads:
            st = self.res.setdefault(r, {"w": {}, "r": []})
            st["r"].append(opid)
        for w in writes:
            self.res[w] = {"w": {key: opid}, "r": []}

    def _split(self, reads, writes):
        ex = [r for r in reads if r in self.excl]
        if ex:
            reads = [r for r in reads if r not in self.excl]
            writes = list(writes) + [r for r in ex if r not in writes]
        return reads, writes

    def add(self, eng, fn, reads=(), writes=(), extra_deps=()):
        reads, writes = self._split(reads, writes)
        deps = self._deps_for(reads, writes)
        deps.update(extra_deps)
        opid = len(self.ops)
        self.ops.append(dict(id=opid, eng=eng, fn=fn, deps=deps, dma=False, signal=False))
        self._commit(opid, eng, reads, writes)
        return opid

    def dma(self, queue, fn, reads=(), writes=(), extra_deps=()):
        reads, writes = self._split(reads, writes)
        deps = self._deps_for(reads, writes)
        deps.update(extra_deps)
        n = self.dma_count[queue]
        self.dma_count[queue] += 1
        slot = n % N_DMA_SLOTS
        prev = self.dma_slot_last.get((queue, slot))
        if prev is not None:
            deps.add(prev)
        opid = len(self.ops)
        self.ops.append(dict(id=opid, eng=queue, fn=fn, deps=deps, dma=True, slot=slot,
                             dma_val=16 * (n // N_DMA_SLOTS + 1), signal=True))
        self.dma_slot_last[(queue, slot)] = opid
        self._commit(opid, ("dma", queue, slot), reads, writes)
        return opid

    def flush(self):
        nc = self.nc
        ops = self.ops
        start = self.start
        ph = ops[start:]
        eng_ops = {e: [o for o in ph if o["eng"] == e] for e in ENGS}
        for op in ph:
            for d in op["deps"]:
                if d < start:
                    continue
                p = ops[d]
                if p["dma"]:
                    continue
                if p["eng"] == op["eng"] and p["eng"] == "pe":
                    continue
                p["signal"] = True
        for e in ENGS:
            comp = [o for o in eng_ops[e] if not o["dma"]]
            if comp:
                comp[-1]["signal"] = True
        for e in ENGS:
            for op in eng_ops[e]:
                if op["dma"]:
                    continue
                if op["signal"]:
                    self.cnt[e] += 1
                    op["sig"] = self.cnt[e]
        sems, dsems = self.sems, self.dsems
        prev_final = dict(self.prev_final)
        engobj = {"pe": "tensor", "act": "scalar", "dve": "vector", "pool": "gpsimd", "sp": "sync"}
        with nc.Block() as block:
            def make(e):
                def body(eng):
                    waited = {}
                    for k, v in prev_final.items():
                        sem = sems[k[1]] if k[0] == "c" else dsems[(k[1], k[2])]
                        eng.wait_ge(sem, v)
                        waited[k] = v
                    for op in eng_ops[e]:
                        need = {}
                        for d in op["deps"]:
                            if d < start:
                                continue
                            p = ops[d]
                            if p["dma"]:
                                k = ("d", p["eng"], p["slot"])
                                v = p["dma_val"]
                            else:
                                if p["eng"] == e and e == "pe":
                                    continue
                                k = ("c", p["eng"])
                                v = p["sig"]
                            if v > need.get(k, 0):
                                need[k] = v
                        for k, v in need.items():
                            if waited.get(k, 0) >= v:
                                continue
                            waited[k] = v
                            sem = sems[k[1]] if k[0] == "c" else dsems[(k[1], k[2])]
                            eng.wait_ge(sem, v)
                        inst = op["fn"](eng)
                        if op["dma"]:
                            inst.then_inc(dsems[(e, op["slot"])], 16)
                        elif op["signal"]:
                            inst.then_inc(sems[e], 1)
                    for (q, sl), opid in self.dma_slot_last.items():
                        if q == e and opid >= start:
                            v = ops[opid]["dma_val"]
                            if waited.get(("d", q, sl), 0) < v:
                                eng.wait_ge(dsems[(q, sl)], v)
                return body

            for e in ENGS:
                if eng_ops[e] or prev_final:
                    getattr(block, engobj[e])(make(e))
        for e in ENGS:
            if self.cnt[e]:
                self.prev_final[("c", e)] = self.cnt[e]
        for (q, sl), opid in self.dma_slot_last.items():
            self.prev_final[("d", q, sl)] = ops[opid]["dma_val"]
        self.start = len(ops)
        self.res = {}
        for op in ph:
            op["fn"] = None


def _mk(name, *a, **k):
    return lambda e: getattr(e, name)(*a, **k)


class Rot:
    def __init__(self, items):
        self.items = items
        self.i = 0

    def next(self):
        it = self.items[self.i % len(self.items)]
        self.i += 1
        return it


D = 1024
NEGM = -30000.0
LAM_INIT = {1: 0.8 - 0.6 * math.exp(-0.3 * 1), 3: 0.8 - 0.6 * math.exp(-0.3 * 3)}
EVEN_COLS = dict(aq=0, ak=512, av=1024, ag=1536, af=2048, bq=2056, bk=2568, bv=3080, bg=3592,
                 qi=4104, ki=4616, wi=4680)
NIT = 18


def build(cfg):
    NP = cfg["NP"]
    NL = cfg["NL"]
    PAST = cfg["PAST"]
    TOPK_P = cfg["TOPK_P"]
    TOPK_S = cfg["TOPK_S"]
    NE, NO = (NL + 1) // 2, NL // 2
    NTP = NP // 128
    NPT = PAST // 128
    NB = NP // 512
    NKS = PAST + 32

    nc = bass.Bass("TRN2", target_bir_lowering=False)

    def din(name, shape, dt=F32):
        return nc.dram_tensor(name, list(shape), dt, kind="ExternalInput").ap()

    def dout(name, shape, dt=F32):
        return nc.dram_tensor(name, list(shape), dt, kind="ExternalOutput").ap()

    def dscr(name, shape, dt=F32):
        return nc.dram_tensor(name, list(shape), dt, kind="Internal").ap()

    I = {}
    I["xp"] = din("xp", [NP, D])
    I["xs"] = din("xs", [128, D])
    I["ca_k"] = din("ca_k", [NE, 4, PAST, 512]); I["ca_v"] = din("ca_v", [NE, 4, PAST, 512])
    I["ca_f"] = din("ca_f", [NE, 4, PAST, 8])
    I["cb_k"] = din("cb_k", [NE, 4, PAST, 512]); I["cb_v"] = din("cb_v", [NE, 4, PAST, 512])
    I["cb_ki"] = din("cb_ki", [NE, 4, PAST, 64])
    if NO:
        I["cc_k"] = din("cc_k", [NO, 4, PAST, 1024]); I["cc_v"] = din("cc_v", [NO, 4, PAST, 1024])
        I["w_in_odd"] = din("w_in_odd", [NO, D, 4096]); I["w_out_odd"] = din("w_out_odd", [NO, D, D])
        I["lam"] = din("lam", [NO, 4, 64]); I["chg"] = din("chg", [NO, 128])
    I["w_in_even"] = din("w_in_even", [NE, D, 4688]); I["w_out_even"] = din("w_out_even", [NE, D, D])
    I["norm_gain"] = din("norm_gain", [NL, D]); I["final_gain"] = din("final_gain", [D])
    I["b_forget"] = din("b_forget", [NE, 8])
    I["ident"] = din("ident", [128, 128])
    I["cosp"] = din("cosp", [NP, 8]); I["sinp"] = din("sinp", [NP, 8])
    I["coss"] = din("coss", [128, 8]); I["sins"] = din("sins", [128, 8])
    I["cmask"] = din("cmask", [4, 128, 512]); I["ccmask"] = din("ccmask", [4, 128, 512])
    I["idxmask"] = din("idxmask", [128, 128]); I["cmask_s"] = din("cmask_s", [32, 32])
    I["ut"] = din("ut", [128, 128]); I["pow2"] = din("pow2", [128, NIT])

    O = {}
    O["y_p"] = dout("y_p", [NP, D]); O["y_s"] = dout("y_s", [128, D])
    for g, T in (("p", NP), ("s", 128)):
        O["a_k_" + g] = dout("oa_k_" + g, [NE, T, 512]); O["a_v_" + g] = dout("oa_v_" + g, [NE, T, 512])
        O["a_f_" + g] = dout("oa_f_" + g, [NE, T, 8])
        O["b_k_" + g] = dout("ob_k_" + g, [NE, T, 512]); O["b_v_" + g] = dout("ob_v_" + g, [NE, T, 512])
        O["b_ki_" + g] = dout("ob_ki_" + g, [NE, T, 64])
        if NO:
            O["c_k_" + g] = dout("oc_k_" + g, [NO, T, 1024]); O["c_v_" + g] = dout("oc_v_" + g, [NO, T, 1024])

    SC = {}
    for g, T in (("p", NP), ("s", 128)):
        NT = T // 128
        SC["resid_" + g] = dscr("resid_" + g, [T, D])
        for nm in ("qTa", "kTa", "gTa", "qTb", "kTb", "gTb", "qiT"):
            SC[nm + "_" + g] = dscr(nm + "_" + g, [512, T], BF16)
        SC["kiT_" + g] = dscr("kiT_" + g, [64, T], BF16)
        SC["va_" + g] = dscr("va_" + g, [8, 128, NT, 64], BF16)
        SC["vb_" + g] = dscr("vb_" + g, [8, 128, NT, 64], BF16)
        SC["wi_" + g] = dscr("wi_" + g, [T, 8])
        SC["cqa_" + g] = dscr("cqa_" + g, [8, 2, T], BF16)
        SC["nck_" + g] = dscr("nck_" + g, [T, 8])
        for nm in ("qTc", "kTc", "gTc"):
            SC[nm + "_" + g] = dscr(nm + "_" + g, [1024, T], BF16)
        SC["vc_" + g] = dscr("vc_" + g, [8, 128, NT, 128], BF16)

    top = ExitStack()
    S = Sched(nc, top)

    uniq = [0]

    def sb(stack, name, shape, dt):
        uniq[0] += 1
        return stack.enter_context(nc.sbuf_tensor(f"{name}_{uniq[0]}", list(shape), dt))

    def psb(stack, name, shape, dt):
        S.excl.add(name)
        uniq[0] += 1
        return stack.enter_context(nc.psum_tensor(f"{name}_{uniq[0]}", list(shape), dt))

    identf = sb(top, "identf", [128, 128], F32)
    identb = sb(top, "identb", [128, 128], BF16)
    onesb = sb(top, "onesb", [128, 128], BF16)
    onesdiv = sb(top, "onesdiv", [128, 128], BF16)
    onesf = sb(top, "onesf", [128, 128], F32)
    utf = sb(top, "utf", [128, 128], F32)
    pow2 = sb(top, "pow2s", [128, NIT], F32)
    cmask_s = sb(top, "cmask_ss", [32, 32], BF16)
    idxmask = sb(top, "idxmasks", [128, 128], F32)
    ones8 = sb(top, "ones8", [8, 512], F32)

    S.dma("sp", _mk("dma_start", out=identf[:], in_=I["ident"][:, :]), writes=["identf"])
    S.dma("sp", _mk("dma_start", out=utf[:], in_=I["ut"][:, :]), writes=["const"])
    S.dma("sp", _mk("dma_start", out=pow2[:], in_=I["pow2"][:, :]), writes=["const"])
    S.dma("sp", _mk("dma_start", out=idxmask[:], in_=I["idxmask"][:, :]), writes=["const"])
    S.dma("pool", _mk("dma_start", out=cmask_s[:], in_=I["cmask_s"][:, :]), writes=["const"])
    S.add("dve", _mk("tensor_copy", out=identb[:], in_=identf[:]), reads=["identf"], writes=["const"])
    S.add("dve", _mk("memset", onesb[:], 1.0), writes=["const"])
    S.add("dve", _mk("memset", onesdiv[:], 1.0 / 128.0), writes=["const"])
    S.add("dve", _mk("memset", onesf[:], 1.0), writes=["const"])
    S.add("dve", _mk("memset", ones8[:], 1.0), writes=["const"])
    S.flush()

    GRP = {
        "p": dict(T=NP, TB=512),
        "s": dict(T=128, TB=128),
    }

    def phase_A(L):
        even = (L % 2 == 0)
        li = L // 2
        NCOL = 4688 if even else 4096
        win = (I["w_in_even"] if even else I["w_in_odd"])[li]
        with ExitStack() as ph:
            W = sb(ph, "W", [128, 8, NCOL], BF16)
            gb = sb(ph, "gb", [128, D], F32)
            cosp = sb(ph, "cosps", [128, NTP, 8], F32); sinp = sb(ph, "sinps", [128, NTP, 8], F32)
            coss = sb(ph, "cosss", [128, 1, 8], F32); sins = sb(ph, "sinss", [128, 1, 8], F32)
            GRP["p"]["cos"], GRP["p"]["sin"] = cosp, sinp
            GRP["s"]["cos"], GRP["s"]["sin"] = coss, sins
            S.dma("sp", _mk("dma_start", out=cosp[:], in_=I["cosp"].rearrange("(j p) e -> p j e", p=128)), writes=["rope"])
            S.dma("sp", _mk("dma_start", out=sinp[:], in_=I["sinp"].rearrange("(j p) e -> p j e", p=128)), writes=["rope"])
            S.dma("sp", _mk("dma_start", out=coss[:, 0, :], in_=I["coss"][:, :]), writes=["rope"])
            S.dma("sp", _mk("dma_start", out=sins[:, 0, :], in_=I["sins"][:, :]), writes=["rope"])
            xt = Rot([(sb(ph, f"xt{k}", [128, D], F32), f"xt{k}") for k in range(2)])
            junk = sb(ph, "junkA", [128, D], BF16)
            small = Rot([(sb(ph, f"smA{k}", [128, 2], F32), f"smA{k}") for k in range(2)])
            hb = Rot([(sb(ph, f"hb{k}", [128, D], BF16), f"hb{k}") for k in range(2)])
            hT = sb(ph, "hT", [128, 8, 512], BF16)
            st = Rot([(sb(ph, f"st{k}", [128, 512], F32), f"st{k}") for k in range(3)])
            kb = Rot([(sb(ph, f"kb{k}", [128, 512], BF16), f"kb{k}") for k in range(3)])
            kTst = Rot([(sb(ph, f"kTst{k}", [128, 4, 128], BF16), f"kTst{k}") for k in range(3)])
            fmst = Rot([(sb(ph, f"fmst{k}", [128, 512], BF16), f"fmst{k}") for k in range(3)])
            rtmp = Rot([(sb(ph, f"rtmp{k}", [128, 8, 8], F32), f"rtmp{k}") for k in range(4)])
            negb = sb(ph, "negb", [8, 1], F32)
            gainp = sb(ph, "gainp", [128, 1], F32)
            lfe = sb(ph, "lfe", [8, 512], F32)
            lfT = sb(ph, "lfT", [8, 512], F32)
            cumT = sb(ph, "cumT", [8, 512], F32)
            carry = sb(ph, "carry", [8, 1], F32)
            chi = sb(ph, "chi", [8, 512], BF16)
            clo = sb(ph, "clo", [8, 512], BF16)
            chf = sb(ph, "chf", [8, 512], F32)
            lftm = sb(ph, "lftm", [128, 8, 8], F32)
            pT = psb(ph, "pT", [128, 8, 128], BF16)
            ptm = Rot([(psb(ph, f"ptm{k}", [128, 512], F32), f"ptm{k}") for k in range(2)])
            pfm = Rot([(psb(ph, f"pfm{k}", [128, 512], F32), f"pfm{k}") for k in range(2)])
            ptr = Rot([(psb(ph, f"ptr{k}", [128, 8, 128], BF16), f"ptr{k}") for k in range(2)])
            plt = psb(ph, "plt", [128, 64, 8], F32)

            wsrc = win.rearrange("(kc p) n -> p kc n", p=128)
            for c0 in range(0, NCOL, 512):
                c1 = min(NCOL, c0 + 512)
                S.dma("pool", _mk("dma_start", out=W[:, :, c0:c1], in_=wsrc[:, :, c0:c1]),
                      writes=[("W", c0 // 512)])

            def wres(c0, n):
                return [("W", k) for k in range(c0 // 512, (c0 + n - 1) // 512 + 1)]

            S.dma("sp", _mk("dma_start", out=gb[:], in_=I["norm_gain"][L].partition_broadcast(128)), writes=["gb"])
            if even:
                S.dma("sp", _mk("dma_start", out=negb[:], in_=I["b_forget"][li].unsqueeze(1)), writes=["negb"])
                S.add("dve", _mk("tensor_scalar", out=negb[:], in0=negb[:], scalar1=-1.0, scalar2=None, op0=ALU.mult),
                      reads=["negb"], writes=["negb"])
            else:
                S.dma("sp", _mk("dma_start", out=gainp[:], in_=I["chg"][li].unsqueeze(1)), writes=["gainp"])
                S.add("dve", _mk("tensor_scalar", out=gainp[:], in0=gainp[:], scalar1=1.0 - LAM_INIT[L], scalar2=None,
                                                        op0=ALU.mult), reads=["gainp"], writes=["gainp"])

            dbg = cfg.get("dbg", "")
            for g in ("p", "s"):
                if g == "s" and "nosample" in dbg:
                    continue
                G = GRP[g]
                T, TB = G["T"], G["TB"]
                nsub = TB // 128
                xsrc = (I["xp"] if g == "p" else I["xs"]) if L == 0 else SC["resid_" + g]
                if even:
                    S.add("dve", _mk("memset", carry[:], 0.0), writes=["carry"])
                    tm_chunks = [
                        dict(c0=EVEN_COLS["ak"], n=512, kind="K", rope=False, out=O["a_k_" + g][li], scr=SC["kTa_" + g], scale=1.0),
                        dict(c0=EVEN_COLS["av"], n=512, kind="V", rope=False, out=O["a_v_" + g][li], scr=SC["va_" + g], dv=64),
                        dict(c0=EVEN_COLS["bq"], n=512, kind="Q", rope=True, out=None, scr=SC["qTb_" + g], scale=0.125),
                        dict(c0=EVEN_COLS["bk"], n=512, kind="K", rope=True, out=O["b_k_" + g][li], scr=SC["kTb_" + g], scale=1.0),
                        dict(c0=EVEN_COLS["bv"], n=512, kind="V", rope=False, out=O["b_v_" + g][li], scr=SC["vb_" + g], dv=64),
                        dict(c0=EVEN_COLS["qi"], n=512, kind="Q", rope=True, out=None, scr=SC["qiT_" + g], scale=0.125),
                        dict(c0=EVEN_COLS["ki"], n=64, kind="K", rope=True, out=O["b_ki_" + g][li], scr=SC["kiT_" + g], scale=1.0),
                        dict(c0=EVEN_COLS["wi"], n=8, kind="W", rope=False, out=SC["wi_" + g], scr=None),
                    ]
                    fm_tiles = [dict(c0=EVEN_COLS["aq"] + 128 * k, n=128, kind="q", scr=SC["qTa_" + g], r0=128 * k) for k in range(4)]
                    fm_tiles += [dict(c0=EVEN_COLS["ag"] + 128 * k, n=128, kind="g", scr=SC["gTa_" + g], r0=128 * k) for k in range(4)]
                    fm_tiles += [dict(c0=EVEN_COLS["bg"] + 128 * k, n=128, kind="g", scr=SC["gTb_" + g], r0=128 * k) for k in range(4)]
                    fm_tiles += [dict(c0=EVEN_COLS["af"], n=8, kind="f")]
                else:
                    tm_chunks = []
                    for k in range(2):
                        tm_chunks.append(dict(c0=512 * k, n=512, kind="Q", rope=True, out=None, scr=SC["qTc_" + g], scale=0.125, r0=512 * k))
                    for k in range(2):
                        tm_chunks.append(dict(c0=1024 + 512 * k, n=512, kind="K", rope=True, out=O["c_k_" + g][li], scr=SC["kTc_" + g],
                                              scale=1.0, r0=512 * k, oc0=512 * k))
                    for k in range(2):
                        tm_chunks.append(dict(c0=2048 + 512 * k, n=512, kind="V", rope=False, out=O["c_v_" + g][li], scr=SC["vc_" + g],
                                              dv=128, h0=4 * k, oc0=512 * k))
                    fm_tiles = [dict(c0=3072 + 128 * k, n=128, kind="go", scr=SC["gTc_" + g], r0=128 * k) for k in range(8)]

                for tb in range(T // TB):
                    t0 = tb * TB
                    for s in range(nsub):
                        r0 = t0 + s * 128
                        x_, xn = xt.next()
                        S.dma("sp", _mk("dma_start", out=x_[:], in_=xsrc[r0:r0 + 128, :]), writes=[xn])
                        sm, smn = small.next()
                        S.add("act", _mk("activation", out=junk[:], in_=x_[:], func=AF.Square, accum_out=sm[:, 0:1]),
                              reads=[xn], writes=["junkA", smn])
                        S.add("dve", _mk("tensor_scalar", out=sm[:, 1:2], in0=sm[:, 0:1], scalar1=1.0 / D, scalar2=1e-6,
                                                                       op0=ALU.mult, op1=ALU.add), reads=[smn], writes=[smn])
                        S.add("act", _mk("activation", out=sm[:, 1:2], in_=sm[:, 1:2], func=AF.Sqrt), reads=[smn], writes=[smn])
                        S.add("dve", _mk("reciprocal", out=sm[:, 1:2], in_=sm[:, 1:2]), reads=[smn], writes=[smn])
                        h_, hn = hb.next()
                        S.add("dve", _mk("scalar_tensor_tensor", out=h_[:], in0=x_[:], scalar=sm[:, 1:2], in1=gb[:],
                                                                                         op0=ALU.mult, op1=ALU.mult),
                              reads=[xn, smn, "gb"], writes=[hn])
                        for kc in range(8):
                            S.add("pe", _mk("transpose", out=pT[:, kc, :], in_=h_[:, kc * 128:(kc + 1) * 128], identity=identb[:]),
                                  reads=[hn], writes=["pT"])
                        S.add("act", _mk("copy", out=hT[:, :, s * 128:(s + 1) * 128], in_=pT[:]), reads=["pT"], writes=[("hT", s)])
                    hTall = [("hT", s) for s in range(nsub)]

                    for s in range(nsub if "notm" not in dbg else 0):
                        r0 = t0 + s * 128
                        jt = r0 // 128
                        for ch in tm_chunks:
                            n = ch["n"]; c0 = ch["c0"]
                            ps, psn = ptm.next()
                            for kc in range(8):
                                S.add("pe", _mk("matmul",
                                    ps[:, 0:n], lhsT=hT[:, kc, s * 128:(s + 1) * 128], rhs=W[:, kc, c0:c0 + n], start=(kc == 0), stop=(kc == 7)),
                                    reads=[("hT", s)] + wres(c0, n), writes=[psn])
                            st_, stn = st.next()
                            S.add("act", _mk("copy", out=st_[:, 0:n], in_=ps[:, 0:n]), reads=[psn], writes=[stn])
                            if ch["rope"] and "norope" not in dbg:
                                nh = n // 64
                                psv = ps[:, 0:n].rearrange("p (h d) -> p h d", d=64)
                                stv = st_[:, 0:n].rearrange("p (h d) -> p h d", d=64)
                                cosb = G["cos"][:, jt, :].unsqueeze(1).broadcast_to([128, nh, 8])
                                sinb = G["sin"][:, jt, :].unsqueeze(1).broadcast_to([128, nh, 8])
                                x1 = stv[:, :, 0:8]; x2 = stv[:, :, 8:16]
                                ta, tan = rtmp.next(); tbb, tbn = rtmp.next(); tc, tcn = rtmp.next(); td, tdn = rtmp.next()
                                S.add("dve", _mk("tensor_tensor", out=ta[:, 0:nh, :], in0=x1, in1=cosb, op=ALU.mult), reads=[stn, "rope"], writes=[tan])
                                S.add("dve", _mk("tensor_tensor", out=tbb[:, 0:nh, :], in0=x2, in1=sinb, op=ALU.mult), reads=[stn, "rope"], writes=[tbn])
                                S.add("dve", _mk("tensor_tensor", out=tc[:, 0:nh, :], in0=x2, in1=cosb, op=ALU.mult), reads=[stn, "rope"], writes=[tcn])
                                S.add("dve", _mk("tensor_tensor", out=td[:, 0:nh, :], in0=x1, in1=sinb, op=ALU.mult), reads=[stn, "rope"], writes=[tdn])
                                S.add("dve", _mk("tensor_tensor", out=stv[:, :, 0:8], in0=ta[:, 0:nh, :], in1=tbb[:, 0:nh, :], op=ALU.subtract),
                                      reads=[tan, tbn], writes=[stn])
                                S.add("dve", _mk("tensor_tensor", out=stv[:, :, 8:16], in0=tc[:, 0:nh, :], in1=td[:, 0:nh, :], op=ALU.add),
                                      reads=[tcn, tdn], writes=[stn])
                            kind = ch["kind"]
                            if ch["out"] is not None:
                                oc0 = ch.get("oc0", 0)
                                S.dma("pool", _mk("dma_start", out=ch["out"][r0:r0 + 128, oc0:oc0 + n], in_=st_[:, 0:n]),
                                      reads=[stn])
                            if kind in ("K", "Q"):
                                kb_, kbn = kb.next()
                                S.add("act", _mk("activation", out=kb_[:, 0:n], in_=st_[:, 0:n], func=AF.Copy, scale=ch["scale"]),
                                      reads=[stn], writes=[kbn])
                                ng = (n + 127) // 128
                                w_ = min(n, 128)
                                pt_, ptn = ptr.next()
                                for gi in range(ng):
                                    S.add("pe", _mk("transpose", out=pt_[0:w_, gi, :], in_=kb_[:, gi * 128:gi * 128 + w_],
                                                                                                   identity=identb[:]), reads=[kbn], writes=[ptn])
                                kt_, ktn = kTst.next()
                                S.add("dve", _mk("tensor_copy", out=kt_[0:w_, 0:ng, :], in_=pt_[0:w_, 0:ng, :]),
                                      reads=[ptn], writes=[ktn])
                                rr0 = ch.get("r0", 0)
                                if n >= 128:
                                    dst = ch["scr"][rr0:rr0 + n, r0:r0 + 128].rearrange("(g p) t -> p g t", p=128)
                                    S.dma("pool", _mk("dma_start", out=dst, in_=kt_[:, 0:ng, :]), reads=[ktn])
                                else:
                                    dst = ch["scr"][0:n, r0:r0 + 128]
                                    S.dma("pool", _mk("dma_start", out=dst, in_=kt_[0:n, 0, :]), reads=[ktn])
                            elif kind == "V":
                                kb_, kbn = kb.next()
                                S.add("act", _mk("copy", out=kb_[:, 0:n], in_=st_[:, 0:n]), reads=[stn], writes=[kbn])
                                dv = ch["dv"]; h0 = ch.get("h0", 0); nh = n // dv
                                dst = ch["scr"][h0:h0 + nh, :, jt, :].rearrange("h p d -> p h d")
                                S.dma("pool", _mk("dma_start", out=dst, in_=kb_[:, 0:n].rearrange("p (h d) -> p h d", d=dv)),
                                      reads=[kbn])

                    for ft in fm_tiles:
                        if "nofm" in dbg or ("nof" in dbg and ft["kind"] == "f"):
                            continue
                        n = ft["n"]; c0 = ft["c0"]
                        ps, psn = pfm.next()
                        for kc in range(8):
                            S.add("pe", _mk("matmul", ps[0:n, 0:TB], lhsT=W[:, kc, c0:c0 + n], rhs=hT[:, kc, 0:TB],
                                                                                  start=(kc == 0), stop=(kc == 7)),
                                  reads=hTall + wres(c0, n), writes=[psn])
                        kind = ft["kind"]
                        if kind in ("q", "g", "go"):
                            fs, fsn = fmst.next()
                            if kind == "q":
                                S.add("act", _mk("activation", out=fs[:, 0:TB], in_=ps[:, 0:TB], func=AF.Copy, scale=0.125),
                                      reads=[psn], writes=[fsn])
                            else:
                                S.add("act", _mk("activation", out=fs[:, 0:TB], in_=ps[:, 0:TB], func=AF.Silu), reads=[psn], writes=[fsn])
                                if kind == "go":
                                    S.add("dve", _mk("tensor_scalar", out=fs[:, 0:TB], in0=fs[:, 0:TB], scalar1=gainp[:, 0:1], scalar2=None,
                                                                                   op0=ALU.mult), reads=[fsn, "gainp"], writes=[fsn])
                            rr0 = ft["r0"]
                            S.dma("pool", _mk("dma_start", out=ft["scr"][rr0:rr0 + 128, t0:t0 + TB], in_=fs[:, 0:TB]),
                                  reads=[fsn])
                        else:
                            S.add("act", _mk("activation", out=lfe[:, 0:TB], in_=ps[0:8, 0:TB], func=AF.Exp, scale=-1.0, bias=negb[:, 0:1]),
                                  reads=[psn, "negb"], writes=["lfe"])
                            S.add("act", _mk("activation", out=lfe[:, 0:TB], in_=lfe[:, 0:TB], func=AF.Ln, bias=1.0, scale=1.0),
                                  reads=["lfe"], writes=["lfe"])
                            S.add("dve", _mk("tensor_scalar", out=lfT[:, 0:TB], in0=lfe[:, 0:TB], scalar1=-1.0, scalar2=None, op0=ALU.mult),
                                  reads=["lfe"], writes=["lfT"])
                            for s in range(nsub):
                                S.add("pe", _mk("transpose", out=plt[:, s, :], in_=lfT[0:8, s * 128:(s + 1) * 128], identity=identf[0:8, 0:8]),
                                      reads=["lfT"], writes=["plt"])
                            if g == "p":
                                S.add("dve", _mk("tensor_tensor_scan", out=cumT[:, 0:TB], data0=ones8[:, 0:TB], data1=lfT[:, 0:TB], initial=carry[:, 0:1],
                                                                              op0=ALU.mult, op1=ALU.add), reads=["lfT", "carry"], writes=["cumT"])
                                S.add("dve", _mk("tensor_copy", out=carry[:, 0:1], in_=cumT[:, TB - 1:TB]), reads=["cumT"], writes=["carry"])
                                S.add("dve", _mk("tensor_copy", out=chi[:, 0:TB], in_=cumT[:, 0:TB]), reads=["cumT"], writes=["chi"])
                                S.add("dve", _mk("tensor_copy", out=chf[:, 0:TB], in_=chi[:, 0:TB]), reads=["chi"], writes=["chf"])
                                S.add("dve", _mk("tensor_tensor", out=clo[:, 0:TB], in0=cumT[:, 0:TB], in1=chf[:, 0:TB], op=ALU.subtract),
                                      reads=["cumT", "chf"], writes=["clo"])
                                S.dma("pool", _mk("dma_start", out=SC["cqa_p"][:, 0, t0:t0 + TB], in_=chi[:, 0:TB]), reads=["chi"])
                                S.dma("pool", _mk("dma_start", out=SC["cqa_p"][:, 1, t0:t0 + TB], in_=clo[:, 0:TB]), reads=["clo"])
                                for s in range(nsub):
                                    S.add("pe", _mk("transpose", out=plt[:, 4 + s, :], in_=cumT[0:8, s * 128:(s + 1) * 128], identity=identf[0:8, 0:8]),
                                          reads=["cumT"], writes=["plt"])
                            S.add("dve", _mk("tensor_copy", out=lftm[:, 0:nsub, :], in_=plt[:, 0:nsub, :]), reads=["plt"], writes=["lftm"])
                            S.dma("pool", _mk("dma_start", out=O["a_f_" + g][li][t0:t0 + TB, :].rearrange("(s p) h -> p s h", p=128),
                                                                         in_=lftm[:, 0:nsub, :]), reads=["lftm"])
                            if g == "p":
                                S.add("dve", _mk("tensor_scalar", out=lftm[:, 4:4 + nsub, :], in0=plt[:, 4:4 + nsub, :], scalar1=-1.0, scalar2=None, op0=ALU.mult),
                                      reads=["plt"], writes=["lftm2"])
                                S.dma("pool", _mk("dma_start", out=SC["nck_p"][t0:t0 + TB, :].rearrange("(s p) h -> p s h", p=128),
                                                                        in_=lftm[:, 4:4 + nsub, :]), reads=["lftm2"])
                            else:
                                S.dma("pool", _mk("dma_start", out=SC["nck_s"][0:128, :], in_=lftm[:, 0, :]), reads=["lftm"])
            S.flush()

    def phase_B(L):
        even = (L % 2 == 0)
        li = L // 2
        last = (L == NL - 1)
        wout = (I["w_out_even"] if even else I["w_out_odd"])[li]
        dbgB = cfg.get("dbg", "")
        NKMAX = max(NP, ((NKS + 127) // 128) * 128)
        with ExitStack() as ph:
            wo = sb(ph, "wo", [128, 8, D], BF16)
            gbf = sb(ph, "gbf", [128, D], F32)
            cmask = sb(ph, "cmasks", [128, 4, 512], BF16)
            pbuf = Rot([(sb(ph, f"pbuf{k}", [128, 512], BF16), f"pbuf{k}") for k in range(3)])
            qTt = Rot([(sb(ph, f"qTt{k}", [66, 512], BF16), f"qTt{k}") for k in range(2)])
            gt = Rot([(sb(ph, f"gt{k}", [128, 512], BF16), f"gt{k}") for k in range(2)])
            ftmp = Rot([(sb(ph, f"ftmp{k}", [128, 512], F32), f"ftmp{k}") for k in range(4)])
            sqb = sb(ph, "sqb", [128, 512], BF16)
            mix = sb(ph, "mix", [128, 8, 512], BF16)
            xo = Rot([(sb(ph, f"xo{k}", [128, D], F32), f"xo{k}") for k in range(1)])
            xn_ = Rot([(sb(ph, f"xn{k}", [128, D], F32), f"xn{k}") for k in range(1)])
            junkB = sb(ph, "junkB", [128, D], BF16)
            smallB = Rot([(sb(ph, f"smB{k}", [128, 2], F32), f"smB{k}") for k in range(2)])
            nck = sb(ph, "nck", [128, max(NTP, NPT + 1), 8], F32)
            neglam = sb(ph, "neglam", [128, 1], F32)
            lp = sb(ph, "lp", [128, 4, 64], F32)
            lpt = sb(ph, "lpt", [128, 64], F32)
            lps = sb(ph, "lps", [128, 2], F32)
            if even:
                qit = Rot([(sb(ph, f"qit{k}", [64, 8, 128], BF16), f"qit{k}") for k in range(2)])
                wit = Rot([(sb(ph, f"wit{k}", [128, 8], F32), f"wit{k}") for k in range(2)])
                diagw = sb(ph, "diagw", [128, 8, 128], BF16)
                rbuf = Rot([(sb(ph, f"rbuf{k}", [128, 512], BF16), f"rbuf{k}") for k in range(4)])
                bis = sb(ph, "bis", [128, 8], F32)
                hwtab = sb(ph, "hwtab", [128, NIT], F32)
            B = {}
            sbank = Rot([(psb(ph, f"sbank{k}", [128, 512], F32), f"sbank{k}") for k in range(2)])
            acc = [(psb(ph, f"acc{k}", [128, 512], F32), f"acc{k}") for k in range(4)]
            ptrB = psb(ph, "ptrB", [128, 8, 128], BF16)
            pmisc = psb(ph, "pmisc", [128, 512], F32)

            S.dma("pool", _mk("dma_start", out=wo[:], in_=wout.rearrange("(kc p) n -> p kc n", p=128)), writes=["wo"])
            if last:
                S.dma("sp", _mk("dma_start", out=gbf[:], in_=I["final_gain"].partition_broadcast(128)), writes=["gbf"])
            S.dma("pool", _mk("dma_start", out=cmask[:], in_=(I["cmask"] if even else I["ccmask"]).rearrange("j p q -> p j q")), writes=["const"])
            if even:
                S.dma("sp", _mk("dma_start", out=nck[:, 0:NTP, :], in_=SC["nck_p"].rearrange("(j p) h -> p j h", p=128)), writes=["nck"])
            else:
                S.dma("sp", _mk("dma_start", out=lp[:].rearrange("p a b -> p (a b)"),
                                                  in_=I["lam"][li].rearrange("a b -> (a b)").partition_broadcast(128)), writes=["lp"])
                for k in range(2):
                    S.add("dve", _mk("tensor_tensor", out=lpt[:], in0=lp[:, 2 * k, :], in1=lp[:, 2 * k + 1, :], op=ALU.mult),
                          reads=["lp"], writes=["lpt"])
                    S.add("dve", _mk("tensor_reduce", out=lps[:, k:k + 1], in_=lpt[:], axis=AX.X, op=ALU.add), reads=["lpt"], writes=["lps"])
                S.add("act", _mk("activation", out=lps[:], in_=lps[:], func=AF.Exp), reads=["lps"], writes=["lps"])
                S.add("dve", _mk("tensor_tensor", out=neglam[:], in0=lps[:, 1:2], in1=lps[:, 0:1], op=ALU.subtract), reads=["lps"], writes=["neglam"])
                S.add("dve", _mk("tensor_scalar", out=neglam[:], in0=neglam[:], scalar1=-LAM_INIT[L], scalar2=None, op0=ALU.add),
                      reads=["neglam"], writes=["neglam"])

            def attend(qT_ap, qres, QW, chunks, dv, O_ap, SM_ap, ores, preloaded=None, prefetch_cb=None):
                if isinstance(chunks, list) and chunks and isinstance(chunks[0], dict):
                    chunks = [(len(chunks), (lambda t=chunks: t))]
                n = sum(c[0] for c in chunks)
                starts = []
                acc_ = 0
                for c in chunks:
                    starts.append(acc_)
                    acc_ += c[0]
                loaded = []
                state = {"next": 0}

                def load_next():
                    if state["next"] < len(chunks):
                        loaded.extend(chunks[state["next"]][1]())
                        state["next"] += 1

                def stage_a(kt):
                    kn = kt["kn"]
                    ps, psn = sbank.next()
                    am = kt.get("addmask")
                    S.add("pe", _mk("matmul", ps[0:kn, 0:QW], lhsT=kt["kT"], rhs=qT_ap, start=True, stop=(am is None)),
                          reads=[kt["kres"], qres], writes=[psn])
                    if am is not None:
                        S.add("pe", _mk("matmul", ps[0:kn, 0:QW], lhsT=identb[0:kn, 0:kn], rhs=am, start=False, stop=True),
                              reads=["const"] + ([kt["mres"]] if "mres" in kt else []), writes=[psn])
                    pt, ptn = pbuf.next()
                    bias = kt.get("bias")
                    if bias is not None:
                        S.add("act", _mk("activation", out=pt[0:kn, 0:QW], in_=ps[0:kn, 0:QW], func=AF.Exp, bias=bias),
                              reads=[psn, kt["bres"]], writes=[ptn])
                    else:
                        S.add("act", _mk("activation", out=pt[0:kn, 0:QW], in_=ps[0:kn, 0:QW], func=AF.Exp),
                              reads=[psn], writes=[ptn])
                    return pt, ptn

                def stage_b(kt, pt, ptn, idx):
                    kn = kt["kn"]
                    S.add("pe", _mk("matmul", O_ap, lhsT=kt["v"], rhs=pt[0:kn, 0:QW], start=(idx == 0), stop=(idx == n - 1)),
                          reads=[ptn, kt["vres"]], writes=[ores[0]])
                    S.add("pe", _mk("matmul", SM_ap, lhsT=onesb[0:kn, 0:dv], rhs=pt[0:kn, 0:QW], start=(idx == 0), stop=(idx == n - 1)),
                          reads=[ptn], writes=[ores[1]])

                if preloaded is not None:
                    loaded.extend(preloaded)
                    state["next"] = 1
                else:
                    load_next()
                cur = stage_a(loaded[0])
                for t in range(n):
                    if t in starts:
                        load_next()
                    if t == starts[-1] and prefetch_cb is not None:
                        prefetch_cb()
                    nxt = None
                    if t + 1 < n:
                        while len(loaded) < t + 2:
                            load_next()
                        nxt = stage_a(loaded[t + 1])
                    stage_b(loaded[t], cur[0], cur[1], t)
                    cur = nxt

            def prompt_tiles(kscr, krow0, Kc, vscr, vh, vd0, dv, KE, bias_h=None, masks=None, j_diag0=None, mul=None, mres=None):
                chunks = []
                for c in range((KE + 15) // 16):
                    j0 = c * 16
                    nt = min(16, KE - j0)

                    def thunk(j0=j0, nt=nt):
                        tiles = []
                        kb_, kbn = kTb_.next()
                        vv, vvn = vb_.next()
                        S.dma("sp", _mk("dma_start", out=kb_[0:64, 0:nt * 128], in_=kscr[krow0:krow0 + 64, j0 * 128:(j0 + nt) * 128]),
                              writes=[kbn])
                        S.dma("sp", _mk("dma_start", out=vv[:, 0:nt, 0:dv], in_=vscr[vh, :, j0:j0 + nt, vd0:vd0 + dv]), writes=[vvn])
                        for jj in range(nt):
                            j = j0 + jj
                            t = dict(kn=128, kT=kb_[0:Kc, jj * 128:(jj + 1) * 128], kres=kbn, v=vv[:, jj, 0:dv], vres=vvn)
                            if bias_h is not None:
                                t["bias"] = nck[:, j, bias_h:bias_h + 1]; t["bres"] = "nck"
                            if masks is not None and j >= j_diag0:
                                t["addmask"] = masks[:, j - j_diag0, :]
                                if mres is not None:
                                    t["mres"] = mres
                            if mul is not None:
                                t["mulmask"] = mul[:, j, :]; t["mres"] = "selT"
                            tiles.append(t)
                        return tiles
                    chunks.append((nt, thunk))
                return chunks

            def fin_pair(O_, SM_, ores, gscr, grow0, qcol0, QW, ct, mcol0):
                g_, gn = gt.next()
                S.dma("sp", _mk("dma_start", out=g_[:, 0:QW], in_=gscr[grow0:grow0 + 128, qcol0:qcol0 + QW]), writes=[gn])
                r_, rn = ftmp.next()
                S.add("dve", _mk("reciprocal", out=r_[:, 0:QW], in_=SM_[:, 0:QW]), reads=[ores[1]], writes=[rn])
                S.add("dve", _mk("tensor_tensor", out=r_[:, 0:QW], in0=O_[:, 0:QW], in1=r_[:, 0:QW], op=ALU.mult), reads=[ores[0], rn], writes=[rn])
                S.add("dve", _mk("tensor_tensor", out=mix[:, ct, mcol0:mcol0 + QW], in0=r_[:, 0:QW], in1=g_[:, 0:QW], op=ALU.mult),
                      reads=[rn, gn], writes=[("mix", ct)])

            def fin_diff(gscr, grow0, qcol0, QW, ct, mcol0):
                g_, gn = gt.next()
                S.dma("sp", _mk("dma_start", out=g_[:, 0:QW], in_=gscr[grow0:grow0 + 128, qcol0:qcol0 + QW]), writes=[gn])
                r0_, r0n = ftmp.next(); r1_, r1n = ftmp.next()
                (O0, o0n), (S0, s0n), (O1, o1n), (S1, s1n) = acc
                S.add("dve", _mk("reciprocal", out=r0_[:, 0:QW], in_=S0[:, 0:QW]), reads=[s0n], writes=[r0n])
                S.add("dve", _mk("tensor_tensor", out=r0_[:, 0:QW], in0=O0[:, 0:QW], in1=r0_[:, 0:QW], op=ALU.mult), reads=[o0n, r0n], writes=[r0n])
                S.add("dve", _mk("reciprocal", out=r1_[:, 0:QW], in_=S1[:, 0:QW]), reads=[s1n], writes=[r1n])
                S.add("dve", _mk("tensor_tensor", out=r1_[:, 0:QW], in0=O1[:, 0:QW], in1=r1_[:, 0:QW], op=ALU.mult), reads=[o1n, r1n], writes=[r1n])
                S.add("dve", _mk("scalar_tensor_tensor", out=r0_[:, 0:QW], in0=r1_[:, 0:QW], scalar=neglam[:, 0:1], in1=r0_[:, 0:QW],
                                                               op0=ALU.mult, op1=ALU.add), reads=[r0n, r1n, "neglam"], writes=[r0n])
                S.add("act", _mk("activation", out=sqb[:, 0:QW], in_=r0_[:, 0:QW], func=AF.Square), reads=[r0n], writes=["sqb"])
                S.add("pe", _mk("matmul", pmisc[:, 0:QW], lhsT=onesdiv[:], rhs=sqb[:, 0:QW], start=True, stop=True), reads=["sqb"], writes=["pmisc"])
                S.add("dve", _mk("tensor_scalar", out=r1_[:, 0:QW], in0=pmisc[:, 0:QW], scalar1=1e-6, scalar2=None, op0=ALU.add),
                      reads=["pmisc"], writes=[r1n])
                S.add("act", _mk("activation", out=r1_[:, 0:QW], in_=r1_[:, 0:QW], func=AF.Sqrt), reads=[r1n], writes=[r1n])
                S.add("dve", _mk("reciprocal", out=r1_[:, 0:QW], in_=r1_[:, 0:QW]), reads=[r1n], writes=[r1n])
                S.add("dve", _mk("tensor_tensor", out=r0_[:, 0:QW], in0=r0_[:, 0:QW], in1=r1_[:, 0:QW], op=ALU.mult), reads=[r0n, r1n], writes=[r0n])
                S.add("dve", _mk("tensor_tensor", out=mix[:, ct, mcol0:mcol0 + QW], in0=r0_[:, 0:QW], in1=g_[:, 0:QW], op=ALU.mult),
                      reads=[r0n, gn], writes=[("mix", ct)])

            def out_proj(g, t0, nsub):
                xsrc = (I["xp"] if g == "p" else I["xs"]) if L == 0 else SC["resid_" + g]
                for s in range(nsub):
                    r0 = t0 + s * 128
                    x_, xn = xo.next()
                    S.dma("sp", _mk("dma_start", out=x_[:], in_=xsrc[r0:r0 + 128, :]), writes=[xn])
                    y_, yn = xn_.next()
                    for half in range(2):
                        pa, pan = acc[half]
                        for ct in range(8):
                            S.add("pe", _mk("matmul", pa[:, :], lhsT=mix[:, ct, s * 128:(s + 1) * 128],
                                                                                      rhs=wo[:, ct, half * 512:(half + 1) * 512], start=(ct == 0), stop=(ct == 7)),
                                  reads=[("mix", ct), "wo"], writes=[pan])
                        S.add("dve", _mk("tensor_tensor", out=y_[:, half * 512:(half + 1) * 512], in0=pa[:, :],
                                                                                            in1=x_[:, half * 512:(half + 1) * 512], op=ALU.add),
                              reads=[pan, xn], writes=[yn])
                    if not last:
                        S.dma("pool", _mk("dma_start", out=SC["resid_" + g][r0:r0 + 128, :], in_=y_[:]), reads=[yn])
                    else:
                        sm, smn = smallB.next()
                        S.add("act", _mk("activation", out=junkB[:], in_=y_[:], func=AF.Square, accum_out=sm[:, 0:1]),
                              reads=[yn], writes=["junkB", smn])
                        S.add("dve", _mk("tensor_scalar", out=sm[:, 1:2], in0=sm[:, 0:1], scalar1=1.0 / D, scalar2=1e-6, op0=ALU.mult, op1=ALU.add),
                              reads=[smn], writes=[smn])
                        S.add("act", _mk("activation", out=sm[:, 1:2], in_=sm[:, 1:2], func=AF.Sqrt), reads=[smn], writes=[smn])
                        S.add("dve", _mk("reciprocal", out=sm[:, 1:2], in_=sm[:, 1:2]), reads=[smn], writes=[smn])
                        S.add("dve", _mk("scalar_tensor_tensor", out=y_[:], in0=y_[:], scalar=sm[:, 1:2], in1=gbf[:], op0=ALU.mult, op1=ALU.mult),
                              reads=[yn, smn, "gbf"], writes=[yn])
                        S.dma("pool", _mk("dma_start", out=O["y_" + g][r0:r0 + 128, :], in_=y_[:]), reads=[yn])

            def indexer(qiT_src, qc0, nq, wi_src, wr0, kiT_of, NV, diag_j, ul, topk, tail_to=None):
                q_, qn = qit.next()
                S.dma("sp", _mk("dma_start", out=q_[:, :, 0:nq], in_=qiT_src.rearrange("(h d) t -> d h t", d=64)[:, :, qc0:qc0 + nq]), writes=[qn])
                w_, wn = wit.next()
                S.dma("sp", _mk("dma_start", out=w_[0:nq, :], in_=wi_src[wr0:wr0 + nq, :]), writes=[wn])
                for hh in range(8):
                    S.add("act", _mk("activation", out=diagw[0:nq, hh, 0:nq], in_=identb[0:nq, 0:nq], func=AF.Copy, scale=w_[0:nq, hh:hh + 1]),
                          reads=[wn, "const"], writes=["diagw"])
                sc_ps, scn = acc[0]
                for kg in range((NV + 511) // 512):
                    k0 = kg * 512
                    n = min(512, NV - k0)
                    kap, kres = kiT_of(k0, n)

                    def dots(hh):
                        ps, psn = sbank.next()
                        S.add("pe", _mk("matmul", ps[0:nq, 0:n], lhsT=q_[:, hh, 0:nq], rhs=kap, start=True, stop=True),
                              reads=[qn, kres], writes=[psn])
                        r_, rn = rbuf.next()
                        S.add("act", _mk("activation", out=r_[0:nq, 0:n], in_=ps[0:nq, 0:n], func=AF.Relu), reads=[psn], writes=[rn])
                        return r_, rn
                    cur = dots(0)
                    for hh in range(8):
                        nxt = dots(hh + 1) if hh + 1 < 8 else None
                        S.add("pe", _mk("matmul", sc_ps[0:nq, 0:n], lhsT=diagw[0:nq, hh, 0:nq], rhs=cur[0][0:nq, 0:n], start=(hh == 0), stop=(hh == 7)),
                              reads=[cur[1], "diagw"], writes=[scn])
                        cur = nxt
                    S.add("act", _mk("copy", out=B["scores"][0:nq, k0:k0 + n], in_=sc_ps[0:nq, 0:n]), reads=[scn], writes=["scores"])
                steps = []
                sc = B["scores"]; sl = B["sel"]

                def prep():
                    S.add("dve", _mk("tensor_reduce", out=bis[0:nq, 0:1], in_=sc[0:nq, 0:NV], axis=AX.X, op=ALU.min), reads=["scores"], writes=["bis"])
                    if diag_j is not None:
                        S.add("dve", _mk("tensor_tensor", out=sc[0:nq, diag_j * 128:(diag_j + 1) * 128], in0=sc[0:nq, diag_j * 128:(diag_j + 1) * 128],
                                         in1=idxmask[0:nq, :], op=ALU.add), reads=["scores", "const"], writes=["scores"])
                    S.add("dve", _mk("tensor_reduce", out=bis[0:nq, 1:2], in_=sc[0:nq, 0:NV], axis=AX.X, op=ALU.max), reads=["scores"], writes=["bis"])
                    S.add("dve", _mk("scalar_tensor_tensor", out=bis[0:nq, 1:2], in0=bis[0:nq, 1:2], scalar=1.0, in1=bis[0:nq, 0:1], op0=ALU.add, op1=ALU.subtract),
                          reads=["bis"], writes=["bis"])
                    S.add("dve", _mk("tensor_scalar", out=hwtab[0:nq, :], in0=pow2[0:nq, :], scalar1=bis[0:nq, 1:2], scalar2=None, op0=ALU.mult),
                          reads=["bis", "const"], writes=["hwtab"])
                steps.append(prep)

                def it(k):
                    S.add("dve", _mk("tensor_tensor", out=bis[0:nq, 2:3], in0=bis[0:nq, 0:1], in1=hwtab[0:nq, k:k + 1], op=ALU.add),
                          reads=["bis", "hwtab"], writes=["bis"])
                    S.add("dve", _mk("tensor_scalar", out=sl[0:nq, ul, 0:NV], in0=sc[0:nq, 0:NV], scalar1=bis[0:nq, 2:3], scalar2=0.0,
                                     op0=ALU.is_ge, op1=ALU.add, accum_out=bis[0:nq, 3:4]), reads=["scores", "bis"], writes=[("sel", ul), "bis"])
                    S.add("dve", _mk("scalar_tensor_tensor", out=bis[0:nq, 4:5], in0=bis[0:nq, 3:4], scalar=topk - 0.5, in1=hwtab[0:nq, k:k + 1],
                                     op0=ALU.is_ge, op1=ALU.mult), reads=["bis", "hwtab"], writes=["bis"])
                    S.add("dve", _mk("tensor_tensor", out=bis[0:nq, 0:1], in0=bis[0:nq, 0:1], in1=bis[0:nq, 4:5], op=ALU.add), reads=["bis"], writes=["bis"])
                for k in range(NIT if "nobis" not in dbgB else 0):
                    steps.append(lambda k=k: it(k))

                def fin():
                    S.add("dve", _mk("tensor_scalar", out=sl[0:nq, ul, 0:NV], in0=sc[0:nq, 0:NV], scalar1=bis[0:nq, 0:1], scalar2=None, op0=ALU.is_ge),
                          reads=["scores", "bis"], writes=[("sel", ul)])
                    if tail_to is not None and NV < tail_to:
                        S.add("pool", _mk("memset", sl[:, ul, NV:tail_to], 0.0), writes=[("sel", ul)])
                steps.append(fin)
                return steps

            with ExitStack() as sec:
                kTb_ = Rot([(sb(sec, f"kTbuf{k}", [66, 2048], BF16), f"kTbuf{k}") for k in range(2)])
                vb_ = Rot([(sb(sec, f"vbuf{k}", [128, 16, 128], BF16), f"vbuf{k}") for k in range(2)])
                for k in range(2):
                    t_, tn = kTb_.next()
                    S.add("pool", _mk("memset", t_[64:66, :], 1.0), writes=[tn])
                if even:
                    B["scores"] = sb(sec, "scores", [128, NP], F32)
                    B["sel"] = sb(sec, "sel", [128, 2, NP], BF16)
                    B["selT"] = sb(sec, "selT", [128, NTP, 256], BF16)
                    kic = Rot([(sb(sec, f"kic{k}", [64, 512], BF16), f"kic{k}") for k in range(2)])
                if even:
                    def head_prep(kind, m, u2, h):
                        if kind == "fox":
                            Q0 = 512 * m
                            KE = 4 * (m + 1)
                            q_, qn = qTt.next()
                            S.dma("sp", _mk("dma_start", out=q_[0:64, :], in_=SC["qTa_p"][h * 64:(h + 1) * 64, Q0:Q0 + 512]), writes=[qn])
                            S.dma("sp", _mk("dma_start", out=q_[64:66, :], in_=SC["cqa_p"][h, :, Q0:Q0 + 512]), writes=[qn])
                            chunks = prompt_tiles(SC["kTa_p"], h * 64, 66, SC["va_p"], h, 0, 64, KE, bias_h=h, masks=cmask, j_diag0=4 * m)
                        else:
                            q0 = 512 * m + 256 * u2
                            KE2 = 4 * m + 2 * u2 + 2
                            q_, qn = qTt.next()
                            S.dma("sp", _mk("dma_start", out=q_[0:64, 0:256], in_=SC["qTb_p"][h * 64:(h + 1) * 64, q0:q0 + 256]), writes=[qn])
                            chunks = prompt_tiles(SC["kTb_p"], h * 64, 64, SC["vb_p"], h, 0, 64, KE2, masks=B["selT"], j_diag0=0, mres="selT")
                        first = chunks[0][1]()
                        return dict(q=q_, qn=qn, chunks=chunks, first=first)

                    def head_run(kind, m, u2, h, ctx, cb):
                        q_, qn = ctx["q"], ctx["qn"]
                        pb = (h % 2) * 64
                        if kind == "fox":
                            Q0 = 512 * m
                            (Oa, on), (Sa, sn) = acc[0], acc[1]
                            attend(q_[0:66, 0:512], qn, 512, ctx["chunks"], 64, Oa[pb:pb + 64, :], Sa[pb:pb + 64, :], (on, sn),
                                   preloaded=ctx["first"], prefetch_cb=cb)
                            if h % 2 == 1:
                                fin_pair(Oa, Sa, (on, sn), SC["gTa_p"], (h // 2) * 128, Q0, 512, h // 2, 0)
                        else:
                            q0 = 512 * m + 256 * u2
                            (Oa, on), (Sa, sn) = acc[2], acc[3]
                            attend(q_[0:64, 0:256], qn, 256, ctx["chunks"], 64, Oa[pb:pb + 64, 0:256], Sa[pb:pb + 64, 0:256], (on, sn),
                                   preloaded=ctx["first"], prefetch_cb=cb)
                            if h % 2 == 1:
                                fin_pair(Oa, Sa, (on, sn), SC["gTb_p"], (h // 2) * 128, q0, 256, 4 + h // 2, 256 * u2)

                    def idx_part(m, u2, ul):
                        i = 4 * m + 2 * u2 + ul
                        NV = 128 * (i + 1)
                        KE2 = 4 * m + 2 * u2 + 2

                        def kiT_of(k0, n):
                            kc_, kcn = kic.next()
                            S.dma("sp", _mk("dma_start", out=kc_[:, 0:n], in_=SC["kiT_p"][:, k0:k0 + n]), writes=[kcn])
                            return kc_[:, 0:n], kcn
                        if "noidx" in dbgB:
                            return []
                        return indexer(SC["qiT_p"], i * 128, 128, SC["wi_p"], i * 128, kiT_of, NV, i, ul, TOPK_P, tail_to=KE2 * 128)

                    def sel_transposes(m, u2):
                        KE2 = 4 * m + 2 * u2 + 2
                        for j in range(KE2):
                            for ul in range(2):
                                S.add("pe", _mk("transpose", out=ptrB[:, ul, :], in_=B["sel"][:, ul, j * 128:(j + 1) * 128], identity=identb[:]),
                                      reads=[("sel", ul)], writes=["ptrB"])
                            S.add("act", _mk("activation", out=B["selT"][:, j, :], in_=ptrB[:, 0:2, :].rearrange("p a b -> p (a b)"), func=AF.Identity,
                                             scale=-NEGM, bias=NEGM), reads=["ptrB"], writes=["selT"])

                    subs = [(m, u2) for m in range(NB) for u2 in range(2)]
                    for ul in range(2):
                        for st_ in idx_part(subs[0][0], subs[0][1], ul):
                            st_()
                    for si, (m, u2) in enumerate(subs):
                        if "nodsaattn" not in dbgB:
                            sel_transposes(m, u2)
                        nxt = subs[si + 1] if si + 1 < len(subs) else None
                        heads = []
                        if "nofox" not in dbgB:
                            heads += [("fox", h) for h in (range(0, 4) if u2 == 0 else range(4, 8))]
                        if "nodsaattn" not in dbgB:
                            heads += [("dsa", h) for h in range(8)]
                        half = (len(heads) + 1) // 2
                        for part in range(2):
                            pend = idx_part(nxt[0], nxt[1], part) if nxt is not None else []
                            hs_ = heads[:half] if part == 0 else heads[half:]
                            per = (len(pend) + max(1, len(hs_)) - 1) // max(1, len(hs_))
                            ctxs = {}
                            for hi, (kind, h) in enumerate(hs_):
                                ctx = ctxs.pop(hi, None)
                                if ctx is None:
                                    ctx = head_prep(kind, m, u2, h)
                                cb = None
                                if hi + 1 < len(hs_):
                                    def cb(hi=hi):
                                        k2, h2 = hs_[hi + 1]
                                        ctxs[hi + 1] = head_prep(k2, m, u2, h2)
                                head_run(kind, m, u2, h, ctx, cb)
                                for _ in range(min(per, len(pend))):
                                    pend.pop(0)()
                            while pend:
                                pend.pop(0)()
                        if u2 == 1:
                            out_proj("p", 512 * m, 4)
                for m in range(NB if not even else 0):
                    Q0 = 512 * m
                    KE = 4 * (m + 1)
                    if even:
                        pass
                    else:
                        def dprep(hs):
                            q_, qn = qTt.next()
                            S.dma("sp", _mk("dma_start", out=q_[0:64, :], in_=SC["qTc_p"][hs * 64:(hs + 1) * 64, Q0:Q0 + 512]), writes=[qn])
                            chunks = prompt_tiles(SC["kTc_p"], hs * 64, 64, SC["vc_p"], hs // 2, 0, 128, KE, masks=cmask, j_diag0=4 * m)
                            return dict(q=q_, qn=qn, chunks=chunks, first=chunks[0][1]())
                        dctx = {}
                        for hs in range(16):
                            h, c = hs // 2, hs % 2
                            ctx = dctx.pop(hs, None)
                            if ctx is None:
                                ctx = dprep(hs)
                            cb = None
                            if hs + 1 < 16:
                                def cb(hs=hs):
                                    dctx[hs + 1] = dprep(hs + 1)
                            (Oa, on), (Sa, sn) = acc[2 * c], acc[2 * c + 1]
                            attend(ctx["q"][0:64, 0:512], ctx["qn"], 512, ctx["chunks"], 128, Oa[:, :], Sa[:, :], (on, sn),
                                   preloaded=ctx["first"], prefetch_cb=cb)
                            if c == 1:
                                fin_diff(SC["gTc_p"], h * 128, Q0, 512, h, 0)
                    out_proj("p", Q0, 4)

                S.flush()

            with ExitStack() as sec:
                NKSP = ((NKS + 127) // 128) * 128
                if even:
                    B["scores"] = sb(sec, "scores_s", [128, NKSP], F32)
                    B["sel"] = sb(sec, "sel_s", [128, 1, NKSP], BF16)
                    B["selT"] = sb(sec, "selT_s", [128, NPT + 1, 32], BF16)
                    B["kTs"] = sb(sec, "kTs", [64, 8, NKS], BF16)
                    B["vs"] = sb(sec, "vs", [128, NPT + 1, 512], BF16)
                    B["cst"] = sb(sec, "cst", [128, NPT, 512], BF16)
                    B["lfs"] = sb(sec, "lfs", [128, NPT + 1, 8], F32)
                    B["kis"] = sb(sec, "kis", [64, NKS], BF16)
                    B["csk"] = sb(sec, "csk", [128, NPT, 64], BF16)
                else:
                    B["kTs"] = sb(sec, "kTs", [64, 16, NKS], BF16)
                    B["vs"] = sb(sec, "vs", [128, NPT + 1, 1024], BF16)
                    B["cst"] = sb(sec, "cst", [128, NPT, 1024], BF16)
                for b in range(4 if "nosampleB" not in dbgB else 0):
                    qc0 = 32 * b
                    ntile = NPT + 1

                    def load_kT(cache_ap, ncols, nsh, new_scr, dst):
                        S.dma("pool", _mk("dma_start", out=B["cst"][:, :, 0:ncols], in_=cache_ap.rearrange("(j p) c -> p j c", p=128)), writes=["cst"])
                        for sh in range(nsh):
                            for j4 in range(0, NPT, 4):
                                nj = min(4, NPT - j4)
                                for jj in range(nj):
                                    S.add("pe", _mk("transpose", out=ptrB[0:64, jj, :], in_=B["cst"][:, j4 + jj, sh * 64:(sh + 1) * 64], identity=identb[:]),
                                          reads=["cst"], writes=["ptrB"])
                                S.add("dve", _mk("tensor_copy", out=dst[0:64, sh, j4 * 128:(j4 + nj) * 128].rearrange("p (j t) -> p j t", t=128),
                                                                                           in_=ptrB[0:64, 0:nj, :]), reads=["ptrB"], writes=["kTs"])
                        S.dma("sp", _mk("dma_start", out=dst[0:64, 0:nsh, PAST:PAST + 32], in_=new_scr.rearrange("(h d) t -> d h t", d=64)[:, :, qc0:qc0 + 32]),
                              writes=["kTs"])

                    def load_v(cache_ap, ncols, new_scr, dv):
                        S.dma("pool", _mk("dma_start", out=B["vs"][:, 0:NPT, 0:ncols], in_=cache_ap.rearrange("(j p) c -> p j c", p=128)), writes=["vs"])
                        S.dma("sp", _mk("dma_start", out=B["vs"][0:32, NPT, 0:ncols].rearrange("p (h d) -> p h d", d=dv),
                                                          in_=new_scr[:, qc0:qc0 + 32, 0, :].rearrange("h p d -> p h d")), writes=["vs"])

                    def sample_tiles(sh, h, dv, bias_h=None, mask_new=None, mul=None):
                        tiles = []
                        for j in range(ntile):
                            kn = 128 if j < NPT else 32
                            t = dict(kn=kn, kT=B["kTs"][0:64, sh, j * 128:j * 128 + kn], kres="kTs", v=B["vs"][0:kn, j, h * dv:(h + 1) * dv], vres="vs")
                            if bias_h is not None:
                                t["bias"] = nck[0:kn, j, bias_h:bias_h + 1]; t["bres"] = "nck"
                            if mask_new is not None and j == NPT:
                                t["addmask"] = mask_new
                            if mul is not None:
                                t["addmask"] = mul[0:kn, j, 0:32]; t["mres"] = "selT"
                            tiles.append(t)
                        return tiles

                    if even:
                        load_kT(I["ca_k"][li, b], 512, 8, SC["kTa_s"], B["kTs"])
                        load_v(I["ca_v"][li, b], 512, SC["va_s"], 64)
                        S.add("dve", _mk("memset", B["lfs"][:, NPT, :], 0.0), writes=["lfs"])
                        S.dma("sp", _mk("dma_start", out=B["lfs"][:, 0:NPT, :], in_=I["ca_f"][li, b].rearrange("(j p) h -> p j h", p=128)), writes=["lfs"])
                        S.dma("sp", _mk("dma_start", out=B["lfs"][0:32, NPT, :], in_=SC["nck_s"][qc0:qc0 + 32, :]), writes=["lfs"])
                        cps, cpn = acc[0]
                        tps, tpn = acc[1]
                        for j in range(ntile):
                            S.add("pe", _mk("matmul", cps[:, j * 8:(j + 1) * 8], lhsT=utf[:], rhs=B["lfs"][:, j, :], start=True, stop=(j == 0)),
                                  reads=["lfs", "const"], writes=[cpn])
                            for j2 in range(j):
                                S.add("pe", _mk("matmul", cps[:, j * 8:(j + 1) * 8], lhsT=onesf[:], rhs=B["lfs"][:, j2, :], start=False, stop=(j2 == j - 1)),
                                      reads=["lfs", "const"], writes=[cpn])
                        for j in range(ntile):
                            S.add("pe", _mk("matmul", tps[:, 0:8], lhsT=onesf[:], rhs=B["lfs"][:, j, :], start=(j == 0), stop=(j == ntile - 1)),
                                  reads=["lfs", "const"], writes=[tpn])
                        S.add("act", _mk("copy", out=B["lfs"][:, 0, :], in_=tps[:, 0:8]), reads=[tpn], writes=["lfs"])
                        S.add("dve", _mk("tensor_tensor", out=nck[:, 0:ntile, :], in0=B["lfs"][:, 0, :].unsqueeze(1).broadcast_to([128, ntile, 8]),
                                                               in1=cps[:, 0:ntile * 8].rearrange("p (j h) -> p j h", h=8), op=ALU.subtract),
                              reads=[cpn, "lfs"], writes=["nck"])
                        for h in range(8):
                            q_, qn = qTt.next()
                            S.dma("sp", _mk("dma_start", out=q_[0:64, 0:32], in_=SC["qTa_s"][h * 64:(h + 1) * 64, qc0:qc0 + 32]), writes=[qn])
                            tiles = sample_tiles(h, h, 64, bias_h=h, mask_new=cmask_s[0:32, 0:32])
                            pb = (h % 2) * 64
                            (Oa, on), (Sa, sn) = acc[2], acc[3]
                            attend(q_[0:64, 0:32], qn, 32, tiles, 64, Oa[pb:pb + 64, 0:32], Sa[pb:pb + 64, 0:32], (on, sn))
                            if h % 2 == 1:
                                fin_pair(Oa, Sa, (on, sn), SC["gTa_s"], (h // 2) * 128, qc0, 32, h // 2, qc0)
                        S.dma("pool", _mk("dma_start", out=B["csk"][:], in_=I["cb_ki"][li, b].rearrange("(j p) c -> p j c", p=128)), writes=["csk"])
                        for j4 in range(0, NPT, 4):
                            nj = min(4, NPT - j4)
                            for jj in range(nj):
                                S.add("pe", _mk("transpose", out=ptrB[0:64, jj, :], in_=B["csk"][:, j4 + jj, :], identity=identb[:]), reads=["csk"], writes=["ptrB"])
                            S.add("dve", _mk("tensor_copy", out=B["kis"][0:64, j4 * 128:(j4 + nj) * 128].rearrange("p (j t) -> p j t", t=128), in_=ptrB[0:64, 0:nj, :]),
                                  reads=["ptrB"], writes=["kis"])
                        S.dma("sp", _mk("dma_start", out=B["kis"][0:64, PAST:PAST + 32], in_=SC["kiT_s"][:, qc0:qc0 + 32]), writes=["kis"])
                        for st_ in indexer(SC["qiT_s"], qc0, 32, SC["wi_s"], qc0, lambda k0, n: (B["kis"][0:64, k0:k0 + n], "kis"), NKS, None, 0, TOPK_S):
                            st_()
                        for j in range(ntile):
                            kn = 128 if j < NPT else 32
                            S.add("pe", _mk("transpose", out=ptrB[0:kn, 0, 0:32], in_=B["sel"][0:32, 0, j * 128:j * 128 + kn], identity=identb[0:32, 0:32]),
                                  reads=[("sel", 0)], writes=["ptrB"])
                            S.add("act", _mk("activation", out=B["selT"][0:kn, j, 0:32], in_=ptrB[0:kn, 0, 0:32], func=AF.Identity, scale=-NEGM, bias=NEGM), reads=["ptrB"], writes=["selT"])
                        load_kT(I["cb_k"][li, b], 512, 8, SC["kTb_s"], B["kTs"])
                        load_v(I["cb_v"][li, b], 512, SC["vb_s"], 64)
                        for h in range(8):
                            q_, qn = qTt.next()
                            S.dma("sp", _mk("dma_start", out=q_[0:64, 0:32], in_=SC["qTb_s"][h * 64:(h + 1) * 64, qc0:qc0 + 32]), writes=[qn])
                            tiles = sample_tiles(h, h, 64, mul=B["selT"])
                            pb = (h % 2) * 64
                            (Oa, on), (Sa, sn) = acc[2], acc[3]
                            attend(q_[0:64, 0:32], qn, 32, tiles, 64, Oa[pb:pb + 64, 0:32], Sa[pb:pb + 64, 0:32], (on, sn))
                            if h % 2 == 1:
                                fin_pair(Oa, Sa, (on, sn), SC["gTb_s"], (h // 2) * 128, qc0, 32, 4 + h // 2, qc0)
                    else:
                        load_kT(I["cc_k"][li, b], 1024, 16, SC["kTc_s"], B["kTs"])
                        load_v(I["cc_v"][li, b], 1024, SC["vc_s"], 128)
                        for h in range(8):
                            for c in range(2):
                                hs = 2 * h + c
                                q_, qn = qTt.next()
                                S.dma("sp", _mk("dma_start", out=q_[0:64, 0:32], in_=SC["qTc_s"][hs * 64:(hs + 1) * 64, qc0:qc0 + 32]), writes=[qn])
                                tiles = sample_tiles(hs, h, 128)
                                (Oa, on), (Sa, sn) = acc[2 * c], acc[2 * c + 1]
                                attend(q_[0:64, 0:32], qn, 32, tiles, 128, Oa[:, 0:32], Sa[:, 0:32], (on, sn))
                            fin_diff(SC["gTc_s"], h * 128, qc0, 32, h, qc0)
                out_proj("s", 0, 1)
                S.flush()

    only = cfg.get("only")
    for L in range(NL):
        if only is None or f"A{L}" in only:
            phase_A(L)
        if only is None or f"B{L}" in only:
            phase_B(L)

    top.close()
    return nc, None


def rope_tables(pos):
    half = 8
    inv = (500000.0 ** (-np.arange(half, dtype=np.float32) * 2.0 / 16)).astype(np.float32)
    ang = pos.astype(np.float32)[:, None] * inv[None, :]
    return np.cos(ang).astype(np.float32), np.sin(ang).astype(np.float32)


def make_consts(NP, PAST):
    c = {}
    c["ident"] = np.eye(128, dtype=np.float32)
    cp, sp = rope_tables(np.arange(NP))
    c["cosp"], c["sinp"] = cp, sp
    pos_s = PAST + (np.arange(128) % 32)
    cs, ss = rope_tables(pos_s)
    c["coss"], c["sins"] = cs, ss
    k = np.arange(128)[:, None]
    q = np.arange(512)[None, :]
    cm = np.zeros((4, 128, 512), np.float32)
    ccm = np.zeros((4, 128, 512), np.float32)
    for jj in range(4):
        kp = 128 * jj + k
        cm[jj] = np.where(kp <= q, 0.0, NEGM)
        ccm[jj] = np.where(kp // 64 <= q // 64, 0.0, NEGM)
    c["cmask"], c["ccmask"] = cm, ccm
    qq = np.arange(128)[:, None]
    kk = np.arange(128)[None, :]
    c["idxmask"] = np.where(kk // 64 <= qq // 64, 0.0, -1e30).astype(np.float32)
    k32 = np.arange(32)[:, None]
    q32 = np.arange(32)[None, :]
    c["cmask_s"] = np.where(k32 <= q32, 0.0, NEGM).astype(np.float32)
    c["ut"] = np.triu(np.ones((128, 128), np.float32))
    c["pow2"] = np.tile((0.5 ** (np.arange(NIT) + 1)).astype(np.float32)[None, :], (128, 1))
    return c


_CACHE = {}


def run(inputs, cfg, n_cores, prompt_of_core, sample_of_core):
    key = tuple(sorted((k, str(v)) for k, v in cfg.items()))
    if key not in _CACHE:
        _CACHE[key] = build(cfg)
    nc, _ = _CACHE[key]
    NP, PAST = cfg["NP"], cfg["PAST"]
    consts = make_consts(NP, PAST)
    f = lambda a: np.ascontiguousarray(np.asarray(a, dtype=np.float32))
    in_maps = []
    NO = cfg["NL"] // 2
    for c in range(n_cores):
        pb = prompt_of_core[c]
        sbs = sample_of_core[c]
        m = dict(consts)
        m["xp"] = f(inputs["x_prompt"][pb])
        m["xs"] = f(np.asarray(inputs["x_sample"])[sbs].reshape(128, D))
        m["ca_k"] = f(np.asarray(inputs["cache_a_k"])[:, sbs].reshape(-1, 4, PAST, 512))
        m["ca_v"] = f(np.asarray(inputs["cache_a_v"])[:, sbs].reshape(-1, 4, PAST, 512))
        m["ca_f"] = f(np.asarray(inputs["cache_a_logf"])[:, sbs])
        m["cb_k"] = f(np.asarray(inputs["cache_b_k"])[:, sbs].reshape(-1, 4, PAST, 512))
        m["cb_v"] = f(np.asarray(inputs["cache_b_v"])[:, sbs].reshape(-1, 4, PAST, 512))
        m["cb_ki"] = f(np.asarray(inputs["cache_b_kidx"])[:, sbs])
        if NO:
            m["cc_k"] = f(np.asarray(inputs["cache_c_k"])[:, sbs].reshape(-1, 4, PAST, 1024))
            m["cc_v"] = f(np.asarray(inputs["cache_c_v"])[:, sbs].reshape(-1, 4, PAST, 1024))
            m["w_in_odd"] = f(inputs["w_in_odd"]); m["w_out_odd"] = f(inputs["w_out_odd"])
            m["lam"] = f(inputs["lambda_params"]); m["chg"] = f(inputs["c_head_gain"])
        m["w_in_even"] = f(inputs["w_in_even"]); m["w_out_even"] = f(inputs["w_out_even"])
        m["norm_gain"] = f(inputs["norm_gain"]); m["final_gain"] = f(inputs["final_gain"])
        m["b_forget"] = f(inputs["b_forget"])
        in_maps.append(m)
    res = run_bass_kernel_spmd(nc, in_maps, core_ids=list(range(n_cores)))
    return res.results


def assemble(results, cfg, n_prompt, n_sample_total, prompt_of_core, sample_of_core):
    NP, NL = cfg["NP"], cfg["NL"]
    NE, NO = (NL + 1) // 2, NL // 2
    first_core = {}
    for c, pb in enumerate(prompt_of_core):
        first_core.setdefault(pb, c)

    def P(name, shape_tail, lead=None):
        arrs = [np.asarray(results[first_core[pb]][name]) for pb in range(n_prompt)]
        if lead is None:
            return np.stack(arrs, 0).reshape((n_prompt, NP) + shape_tail)
        return np.stack(arrs, 1).reshape((lead, n_prompt, NP) + shape_tail)

    def Sm(name, shape_tail, lead=None):
        if lead is None:
            out = np.zeros((n_sample_total, 32) + shape_tail, np.float32)
            for c, sbs in enumerate(sample_of_core):
                out[sbs] = np.asarray(results[c][name]).reshape((4, 32) + shape_tail)
            return out
        out = np.zeros((lead, n_sample_total, 32) + shape_tail, np.float32)
        for c, sbs in enumerate(sample_of_core):
            out[:, sbs] = np.asarray(results[c][name]).reshape((lead, 4, 32) + shape_tail)
        return out

    outs = [P("y_p", (D,)), Sm("y_s", (D,))]
    outs += [P("oa_k_p", (8, 64), NE), P("oa_v_p", (8, 64), NE), P("oa_f_p", (8,), NE),
             P("ob_k_p", (8, 64), NE), P("ob_v_p", (8, 64), NE), P("ob_ki_p", (64,), NE)]
    if NO:
        outs += [P("oc_k_p", (8, 128), NO), P("oc_v_p", (8, 128), NO)]
    outs += [Sm("oa_k_s", (8, 64), NE), Sm("oa_v_s", (8, 64), NE), Sm("oa_f_s", (8,), NE),
             Sm("ob_k_s", (8, 64), NE), Sm("ob_v_s", (8, 64), NE), Sm("ob_ki_s", (64,), NE)]
    if NO:
        outs += [Sm("oc_k_s", (8, 128), NO), Sm("oc_v_s", (8, 128), NO)]
    return tuple(np.ascontiguousarray(o, dtype=np.float32) for o in outs)


def kernel(**inputs):
    cfg = dict(NP=8192, NL=4, PAST=1024, TOPK_P=256, TOPK_S=256)
    n_cores = 8
    prompt_of_core = [c % 2 for c in range(n_cores)]
    sample_of_core = [list(range(4 * c, 4 * c + 4)) for c in range(n_cores)]
    results = run(inputs, cfg, n_cores, prompt_of_core, sample_of_core)
    return assemble(results, cfg, 2, 32, prompt_of_core, sample_of_core)
```

```python
import math
from contextlib import ExitStack
import numpy as np
import concourse.bass as bass
import concourse.mybir as mybir
from concourse.bass_utils import run_bass_kernel_spmd

F32 = mybir.dt.float32
BF16 = mybir.dt.bfloat16
AF = mybir.ActivationFunctionType
ALU = mybir.AluOpType
AX = mybir.AxisListType

ENGS = ("pe", "act", "dve", "pool", "sp")
N_DMA_SLOTS = 12


class Sched:
    def __init__(self, nc, stack):
        self.nc = nc
        self.ops = []
        self.start = 0
        self.res = {}
        self.dma_count = {e: 0 for e in ENGS}
        self.dma_slot_last = {}
        self.cnt = {e: 0 for e in ENGS}
        self.prev_final = {}
        self.excl = set()
        self.sems = {e: stack.enter_context(nc.semaphore("s_" + e)) for e in ENGS}
        self.dsems = {}
        for q in ("sp", "pool"):
            for sl in range(N_DMA_SLOTS):
                self.dsems[(q, sl)] = stack.enter_context(nc.semaphore(f"d_{q}_{sl}"))

    def _deps_for(self, reads, writes):
        deps = set()
        for r in reads:
            st = self.res.get(r)
            if st:
                deps.update(st["w"].values())
        for w in writes:
            st = self.res.get(w)
            if st:
                deps.update(st["w"].values())
                deps.update(st["r"])
        return deps

    def _commit(self, opid, key, reads, writes):
        for r in reads:
            st = self.res.setdefault(r, {"w": {}, "r": []})
            st["r"].append(opid)
        for w in writes:
            self.res[w] = {"w": {key: opid}, "r": []}

    def _split(self, reads, writes):
        ex = [r for r in reads if r in self.excl]
        if ex:
            reads = [r for r in reads if r not in self.excl]
            writes = list(writes) + [r for r in ex if r not in writes]
        return reads, writes

    def add(self, eng, fn, reads=(), writes=(), extra_deps=()):
        reads, writes = self._split(reads, writes)
        deps = self._deps_for(reads, writes)
        deps.update(extra_deps)
        opid = len(self.ops)
        self.ops.append(dict(id=opid, eng=eng, fn=fn, deps=deps, dma=False, signal=False))
        self._commit(opid, eng, reads, writes)
        return opid

    def dma(self, queue, fn, reads=(), writes=(), extra_deps=()):
        reads, writes = self._split(reads, writes)
        deps = self._deps_for(reads, writes)
        deps.update(extra_deps)
        n = self.dma_count[queue]
        self.dma_count[queue] += 1
        slot = n % N_DMA_SLOTS
        prev = self.dma_slot_last.get((queue, slot))
        if prev is not None:
            deps.add(prev)
        opid = len(self.ops)
        self.ops.append(dict(id=opid, eng=queue, fn=fn, deps=deps, dma=True, slot=slot,
                             dma_val=16 * (n // N_DMA_SLOTS + 1), signal=True))
        self.dma_slot_last[(queue, slot)] = opid
        self._commit(opid, ("dma", queue, slot), reads, writes)
        return opid

    def flush(self):
        nc = self.nc
        ops = self.ops
        start = self.start
        ph = ops[start:]
        eng_ops = {e: [o for o in ph if o["eng"] == e] for e in ENGS}
        for op in ph:
            for d in op["deps"]:
                if d < start:
                    continue
                p = ops[d]
                if p["dma"]:
                    continue
                if p["eng"] == op["eng"] and p["eng"] == "pe":
                    continue
                p["signal"] = True
        for e in ENGS:
            comp = [o for o in eng_ops[e] if not o["dma"]]
            if comp:
                comp[-1]["signal"] = True
        for e in ENGS:
            for op in eng_ops[e]:
                if op["dma"]:
                    continue
                if op["signal"]:
                    self.cnt[e] += 1
                    op["sig"] = self.cnt[e]
        sems, dsems = self.sems, self.dsems
        prev_final = dict(self.prev_final)
        engobj = {"pe": "tensor", "act": "scalar", "dve": "vector", "pool": "gpsimd", "sp": "sync"}
        with nc.Block() as block:
            def make(e):
                def body(eng):
                    waited = {}
                    for k, v in prev_final.items():
                        sem = sems[k[1]] if k[0] == "c" else dsems[(k[1], k[2])]
                        eng.wait_ge(sem, v)
                        waited[k] = v
                    for op in eng_ops[e]:
                        need = {}
                        for d in op["deps"]:
                            if d < start:
                                continue
                            p = ops[d]
                            if p["dma"]:
                                k = ("d", p["eng"], p["slot"])
                                v = p["dma_val"]
                            else:
                                if p["eng"] == e and e == "pe":
                                    continue
                                k = ("c", p["eng"])
                                v = p["sig"]
                            if v > need.get(k, 0):
                                need[k] = v
                        for k, v in need.items():
                            if waited.get(k, 0) >= v:
                                continue
                            waited[k] = v
                            sem = sems[k[1]] if k[0] == "c" else dsems[(k[1], k[2])]
                            eng.wait_ge(sem, v)
                        inst = op["fn"](eng)
                        if op["dma"]:
                            inst.then_inc(dsems[(e, op["slot"])], 16)
                        elif op["signal"]:
                            inst.then_inc(sems[e], 1)
                    for (q, sl), opid in self.dma_slot_last.items():
                        if q == e and opid >= start:
                            v = ops[opid]["dma_val"]
                            if waited.get(("d", q, sl), 0) < v:
                                eng.wait_ge(dsems[(q, sl)], v)
                return body

            for e in ENGS:
                if eng_ops[e] or prev_final:
                    getattr(block, engobj[e])(make(e))
        for e in ENGS:
            if self.cnt[e]:
                self.prev_final[("c", e)] = self.cnt[e]
        for (q, sl), opid in self.dma_slot_last.items():
            self.prev_final[("d", q, sl)] = ops[opid]["dma_val"]
        self.start = len(ops)
        self.res = {}
        for op in ph:
            op["fn"] = None


def _mk(name, *a, **k):
    return lambda e: getattr(e, name)(*a, **k)


class Rot:
    def __init__(self, items):
        self.items = items
        self.i = 0

    def next(self):
        it = self.items[self.i % len(self.items)]
        self.i += 1
        return it


D = 1024
NEGM = -30000.0
LAM_INIT = {1: 0.8 - 0.6 * math.exp(-0.3 * 1), 3: 0.8 - 0.6 * math.exp(-0.3 * 3)}
EVEN_COLS = dict(aq=0, ak=512, av=1024, ag=1536, af=2048, bq=2056, bk=2568, bv=3080, bg=3592,
                 qi=4104, ki=4616, wi=4680)
NIT = 18


def build(cfg):
    NP = cfg["NP"]
    NL = cfg["NL"]
    PAST = cfg["PAST"]
    TOPK_P = cfg["TOPK_P"]
    TOPK_S = cfg["TOPK_S"]
    NE, NO = (NL + 1) // 2, NL // 2
    NTP = NP // 128
    NPT = PAST // 128
    NB = NP // 512
    NKS = PAST + 32

    nc = bass.Bass("TRN2", target_bir_lowering=False)

    def din(name, shape, dt=F32):
        return nc.dram_tensor(name, list(shape), dt, kind="ExternalInput").ap()

    def dout(name, shape, dt=F32):
        return nc.dram_tensor(name, list(shape), dt, kind="ExternalOutput").ap()

    def dscr(name, shape, dt=F32):
        return nc.dram_tensor(name, list(shape), dt, kind="Internal").ap()

    I = {}
    I["xp"] = din("xp", [NP, D])
    I["xs"] = din("xs", [128, D])
    I["ca_k"] = din("ca_k", [NE, 4, PAST, 512]); I["ca_v"] = din("ca_v", [NE, 4, PAST, 512])
    I["ca_f"] = din("ca_f", [NE, 4, PAST, 8])
    I["cb_k"] = din("cb_k", [NE, 4, PAST, 512]); I["cb_v"] = din("cb_v", [NE, 4, PAST, 512])
    I["cb_ki"] = din("cb_ki", [NE, 4, PAST, 64])
    if NO:
        I["cc_k"] = din("cc_k", [NO, 4, PAST, 1024]); I["cc_v"] = din("cc_v", [NO, 4, PAST, 1024])
        I["w_in_odd"] = din("w_in_odd", [NO, D, 4096]); I["w_out_odd"] = din("w_out_odd", [NO, D, D])
        I["lam"] = din("lam", [NO, 4, 64]); I["chg"] = din("chg", [NO, 128])
    I["w_in_even"] = din("w_in_even", [NE, D, 4688]); I["w_out_even"] = din("w_out_even", [NE, D, D])
    I["norm_gain"] = din("norm_gain", [NL, D]); I["final_gain"] = din("final_gain", [D])
    I["b_forget"] = din("b_forget", [NE, 8])
    I["ident"] = din("ident", [128, 128])
    I["cosp"] = din("cosp", [NP, 8]); I["sinp"] = din("sinp", [NP, 8])
    I["coss"] = din("coss", [128, 8]); I["sins"] = din("sins", [128, 8])
    I["cmask"] = din("cmask", [4, 128, 512]); I["ccmask"] = din("ccmask", [4, 128, 512])
    I["idxmask"] = din("idxmask", [128, 128]); I["cmask_s"] = din("cmask_s", [32, 32])
    I["ut"] = din("ut", [128, 128]); I["pow2"] = din("pow2", [128, NIT])

    O = {}
    O["y_p"] = dout("y_p", [NP, D]); O["y_s"] = dout("y_s", [128, D])
    for g, T in (("p", NP), ("s", 128)):
        O["a_k_" + g] = dout("oa_k_" + g, [NE, T, 512]); O["a_v_" + g] = dout("oa_v_" + g, [NE, T, 512])
        O["a_f_" + g] = dout("oa_f_" + g, [NE, T, 8])
        O["b_k_" + g] = dout("ob_k_" + g, [NE, T, 512]); O["b_v_" + g] = dout("ob_v_" + g, [NE, T, 512])
        O["b_ki_" + g] = dout("ob_ki_" + g, [NE, T, 64])
        if NO:
            O["c_k_" + g] = dout("oc_k_" + g, [NO, T, 1024]); O["c_v_" + g] = dout("oc_v_" + g, [NO, T, 1024])

    SC = {}
    for g, T in (("p", NP), ("s", 128)):
        NT = T // 128
        SC["resid_" + g] = dscr("resid_" + g, [T, D])
        for nm in ("qTa", "kTa", "gTa", "qTb", "kTb", "gTb", "qiT"):
            SC[nm + "_" + g] = dscr(nm + "_" + g, [512, T], BF16)
        SC["kiT_" + g] = dscr("kiT_" + g, [64, T], BF16)
        SC["va_" + g] = dscr("va_" + g, [8, 128, NT, 64], BF16)
        SC["vb_" + g] = dscr("vb_" + g, [8, 128, NT, 64], BF16)
        SC["wi_" + g] = dscr("wi_" + g, [T, 8])
        SC["cqa_" + g] = dscr("cqa_" + g, [8, 2, T], BF16)
        SC["nck_" + g] = dscr("nck_" + g, [T, 8])
        for nm in ("qTc", "kTc", "gTc"):
            SC[nm + "_" + g] = dscr(nm + "_" + g, [1024, T], BF16)
        SC["vc_" + g] = dscr("vc_" + g, [8, 128, NT, 128], BF16)

    top = ExitStack()
    S = Sched(nc, top)

    uniq = [0]

    def sb(stack, name, shape, dt):
        uniq[0] += 1
        return stack.enter_context(nc.sbuf_tensor(f"{name}_{uniq[0]}", list(shape), dt))

    def psb(stack, name, shape, dt):
        S.excl.add(name)
        uniq[0] += 1
        return stack.enter_context(nc.psum_tensor(f"{name}_{uniq[0]}", list(shape), dt))

    identf = sb(top, "identf", [128, 128], F32)
    identb = sb(top, "identb", [128, 128], BF16)
    onesb = sb(top, "onesb", [128, 128], BF16)
    onesdiv = sb(top, "onesdiv", [128, 128], BF16)
    onesf = sb(top, "onesf", [128, 128], F32)
    utf = sb(top, "utf", [128, 128], F32)
    pow2 = sb(top, "pow2s", [128, NIT], F32)
    cmask_s = sb(top, "cmask_ss", [32, 32], BF16)
    idxmask = sb(top, "idxmasks", [128, 128], F32)
    ones8 = sb(top, "ones8", [8, 512], F32)

    S.dma("sp", _mk("dma_start", out=identf[:], in_=I["ident"][:, :]), writes=["identf"])
    S.dma("sp", _mk("dma_start", out=utf[:], in_=I["ut"][:, :]), writes=["const"])
    S.dma("sp", _mk("dma_start", out=pow2[:], in_=I["pow2"][:, :]), writes=["const"])
    S.dma("sp", _mk("dma_start", out=idxmask[:], in_=I["idxmask"][:, :]), writes=["const"])
    S.dma("pool", _mk("dma_start", out=cmask_s[:], in_=I["cmask_s"][:, :]), writes=["const"])
    S.add("dve", _mk("tensor_copy", out=identb[:], in_=identf[:]), reads=["identf"], writes=["const"])
    S.add("dve", _mk("memset", onesb[:], 1.0), writes=["const"])
    S.add("dve", _mk("memset", onesdiv[:], 1.0 / 128.0), writes=["const"])
    S.add("dve", _mk("memset", onesf[:], 1.0), writes=["const"])
    S.add("dve", _mk("memset", ones8[:], 1.0), writes=["const"])
    S.flush()

    GRP = {
        "p": dict(T=NP, TB=512),
        "s": dict(T=128, TB=128),
    }

    def phase_A(L):
        even = (L % 2 == 0)
        li = L // 2
        NCOL = 4688 if even else 4096
        win = (I["w_in_even"] if even else I["w_in_odd"])[li]
        with ExitStack() as ph:
            W = sb(ph, "W", [128, 8, NCOL], BF16)
            gb = sb(ph, "gb", [128, D], F32)
            cosp = sb(ph, "cosps", [128, NTP, 8], F32); sinp = sb(ph, "sinps", [128, NTP, 8], F32)
            coss = sb(ph, "cosss", [128, 1, 8], F32); sins = sb(ph, "sinss", [128, 1, 8], F32)
            GRP["p"]["cos"], GRP["p"]["sin"] = cosp, sinp
            GRP["s"]["cos"], GRP["s"]["sin"] = coss, sins
            S.dma("sp", _mk("dma_start", out=cosp[:], in_=I["cosp"].rearrange("(j p) e -> p j e", p=128)), writes=["rope"])
            S.dma("sp", _mk("dma_start", out=sinp[:], in_=I["sinp"].rearrange("(j p) e -> p j e", p=128)), writes=["rope"])
            S.dma("sp", _mk("dma_start", out=coss[:, 0, :], in_=I["coss"][:, :]), writes=["rope"])
            S.dma("sp", _mk("dma_start", out=sins[:, 0, :], in_=I["sins"][:, :]), writes=["rope"])
            xt = Rot([(sb(ph, f"xt{k}", [128, D], F32), f"xt{k}") for k in range(2)])
            junk = sb(ph, "junkA", [128, D], BF16)
            small = Rot([(sb(ph, f"smA{k}", [128, 2], F32), f"smA{k}") for k in range(2)])
            hb = Rot([(sb(ph, f"hb{k}", [128, D], BF16), f"hb{k}") for k in range(2)])
            hT = sb(ph, "hT", [128, 8, 512], BF16)
            st = Rot([(sb(ph, f"st{k}", [128, 512], F32), f"st{k}") for k in range(3)])
            kb = Rot([(sb(ph, f"kb{k}", [128, 512], BF16), f"kb{k}") for k in range(3)])
            kTst = Rot([(sb(ph, f"kTst{k}", [128, 4, 128], BF16), f"kTst{k}") for k in range(3)])
            fmst = Rot([(sb(ph, f"fmst{k}", [128, 512], BF16), f"fmst{k}") for k in range(3)])
            rtmp = Rot([(sb(ph, f"rtmp{k}", [128, 8, 8], F32), f"rtmp{k}") for k in range(4)])
            negb = sb(ph, "negb", [8, 1], F32)
            gainp = sb(ph, "gainp", [128, 1], F32)
            lfe = sb(ph, "lfe", [8, 512], F32)
            lfT = sb(ph, "lfT", [8, 512], F32)
            cumT = sb(ph, "cumT", [8, 512], F32)
            carry = sb(ph, "carry", [8, 1], F32)
            chi = sb(ph, "chi", [8, 512], BF16)
            clo = sb(ph, "clo", [8, 512], BF16)
            chf = sb(ph, "chf", [8, 512], F32)
            lftm = sb(ph, "lftm", [128, 8, 8], F32)
            pT = psb(ph, "pT", [128, 8, 128], BF16)
            ptm = Rot([(psb(ph, f"ptm{k}", [128, 512], F32), f"ptm{k}") for k in range(2)])
            pfm = Rot([(psb(ph, f"pfm{k}", [128, 512], F32), f"pfm{k}") for k in range(2)])
            ptr = Rot([(psb(ph, f"ptr{k}", [128, 8, 128], BF16), f"ptr{k}") for k in range(2)])
            plt = psb(ph, "plt", [128, 64, 8], F32)

            wsrc = win.rearrange("(kc p) n -> p kc n", p=128)
            for c0 in range(0, NCOL, 512):
                c1 = min(NCOL, c0 + 512)
                S.dma("pool", _mk("dma_start", out=W[:, :, c0:c1], in_=wsrc[:, :, c0:c1]),
                      writes=[("W", c0 // 512)])

            def wres(c0, n):
                return [("W", k) for k in range(c0 // 512, (c0 + n - 1) // 512 + 1)]

            S.dma("sp", _mk("dma_start", out=gb[:], in_=I["norm_gain"][L].partition_broadcast(128)), writes=["gb"])
            if even:
                S.dma("sp", _mk("dma_start", out=negb[:], in_=I["b_forget"][li].unsqueeze(1)), writes=["negb"])
                S.add("dve", _mk("tensor_scalar", out=negb[:], in0=negb[:], scalar1=-1.0, scalar2=None, op0=ALU.mult),
                      reads=["negb"], writes=["negb"])
            else:
                S.dma("sp", _mk("dma_start", out=gainp[:], in_=I["chg"][li].unsqueeze(1)), writes=["gainp"])
                S.add("dve", _mk("tensor_scalar", out=gainp[:], in0=gainp[:], scalar1=1.0 - LAM_INIT[L], scalar2=None,
                                                        op0=ALU.mult), reads=["gainp"], writes=["gainp"])

            dbg = cfg.get("dbg", "")
            for g in ("p", "s"):
                if g == "s" and "nosample" in dbg:
                    continue
                G = GRP[g]
                T, TB = G["T"], G["TB"]
                nsub = TB // 128
                xsrc = (I["xp"] if g == "p" else I["xs"]) if L == 0 else SC["resid_" + g]
                if even:
                    S.add("dve", _mk("memset", carry[:], 0.0), writes=["carry"])
                    tm_chunks = [
                        dict(c0=EVEN_COLS["ak"], n=512, kind="K", rope=False, out=O["a_k_" + g][li], scr=SC["kTa_" + g], scale=1.0),
                        dict(c0=EVEN_COLS["av"], n=512, kind="V", rope=False, out=O["a_v_" + g][li], scr=SC["va_" + g], dv=64),
                        dict(c0=EVEN_COLS["bq"], n=512, kind="Q", rope=True, out=None, scr=SC["qTb_" + g], scale=0.125),
                        dict(c0=EVEN_COLS["bk"], n=512, kind="K", rope=True, out=O["b_k_" + g][li], scr=SC["kTb_" + g], scale=1.0),
                        dict(c0=EVEN_COLS["bv"], n=512, kind="V", rope=False, out=O["b_v_" + g][li], scr=SC["vb_" + g], dv=64),
                        dict(c0=EVEN_COLS["qi"], n=512, kind="Q", rope=True, out=None, scr=SC["qiT_" + g], scale=0.125),
                        dict(c0=EVEN_COLS["ki"], n=64, kind="K", rope=True, out=O["b_ki_" + g][li], scr=SC["kiT_" + g], scale=1.0),
                        dict(c0=EVEN_COLS["wi"], n=8, kind="W", rope=False, out=SC["wi_" + g], scr=None),
                    ]
                    fm_tiles = [dict(c0=EVEN_COLS["aq"] + 128 * k, n=128, kind="q", scr=SC["qTa_" + g], r0=128 * k) for k in range(4)]
                    fm_tiles += [dict(c0=EVEN_COLS["ag"] + 128 * k, n=128, kind="g", scr=SC["gTa_" + g], r0=128 * k) for k in range(4)]
                    fm_tiles += [dict(c0=EVEN_COLS["bg"] + 128 * k, n=128, kind="g", scr=SC["gTb_" + g], r0=128 * k) for k in range(4)]
                    fm_tiles += [dict(c0=EVEN_COLS["af"], n=8, kind="f")]
                else:
                    tm_chunks = []
                    for k in range(2):
                        tm_chunks.append(dict(c0=512 * k, n=512, kind="Q", rope=True, out=None, scr=SC["qTc_" + g], scale=0.125, r0=512 * k))
                    for k in range(2):
                        tm_chunks.append(dict(c0=1024 + 512 * k, n=512, kind="K", rope=True, out=O["c_k_" + g][li], scr=SC["kTc_" + g],
                                              scale=1.0, r0=512 * k, oc0=512 * k))
                    for k in range(2):
                        tm_chunks.append(dict(c0=2048 + 512 * k, n=512, kind="V", rope=False, out=O["c_v_" + g][li], scr=SC["vc_" + g],
                                              dv=128, h0=4 * k, oc0=512 * k))
                    fm_tiles = [dict(c0=3072 + 128 * k, n=128, kind="go", scr=SC["gTc_" + g], r0=128 * k) for k in range(8)]

                for tb in range(T // TB):
                    t0 = tb * TB
                    for s in range(nsub):
                        r0 = t0 + s * 128
                        x_, xn = xt.next()
                        S.dma("sp", _mk("dma_start", out=x_[:], in_=xsrc[r0:r0 + 128, :]), writes=[xn])
                        sm, smn = small.next()
                        S.add("act", _mk("activation", out=junk[:], in_=x_[:], func=AF.Square, accum_out=sm[:, 0:1]),
                              reads=[xn], writes=["junkA", smn])
                        S.add("dve", _mk("tensor_scalar", out=sm[:, 1:2], in0=sm[:, 0:1], scalar1=1.0 / D, scalar2=1e-6,
                                                                       op0=ALU.mult, op1=ALU.add), reads=[smn], writes=[smn])
                        S.add("act", _mk("activation", out=sm[:, 1:2], in_=sm[:, 1:2], func=AF.Sqrt), reads=[smn], writes=[smn])
                        S.add("dve", _mk("reciprocal", out=sm[:, 1:2], in_=sm[:, 1:2]), reads=[smn], writes=[smn])
                        h_, hn = hb.next()
                        S.add("dve", _mk("scalar_tensor_tensor", out=h_[:], in0=x_[:], scalar=sm[:, 1:2], in1=gb[:],
                                                                                         op0=ALU.mult, op1=ALU.mult),
                              reads=[xn, smn, "gb"], writes=[hn])
                        for kc in range(8):
                            S.add("pe", _mk("transpose", out=pT[:, kc, :], in_=h_[:, kc * 128:(kc + 1) * 128], identity=identb[:]),
                                  reads=[hn], writes=["pT"])
                        S.add("act", _mk("copy", out=hT[:, :, s * 128:(s + 1) * 128], in_=pT[:]), reads=["pT"], writes=[("hT", s)])
                    hTall = [("hT", s) for s in range(nsub)]

                    for s in range(nsub if "notm" not in dbg else 0):
                        r0 = t0 + s * 128
                        jt = r0 // 128
                        for ch in tm_chunks:
                            n = ch["n"]; c0 = ch["c0"]
                            ps, psn = ptm.next()
                            for kc in range(8):
                                S.add("pe", _mk("matmul",
                                    ps[:, 0:n], lhsT=hT[:, kc, s * 128:(s + 1) * 128], rhs=W[:, kc, c0:c0 + n], start=(kc == 0), stop=(kc == 7)),
                                    reads=[("hT", s)] + wres(c0, n), writes=[psn])
                            st_, stn = st.next()
                            S.add("act", _mk("copy", out=st_[:, 0:n], in_=ps[:, 0:n]), reads=[psn], writes=[stn])
                            if ch["rope"] and "norope" not in dbg:
                                nh = n // 64
                                psv = ps[:, 0:n].rearrange("p (h d) -> p h d", d=64)
                                stv = st_[:, 0:n].rearrange("p (h d) -> p h d", d=64)
                                cosb = G["cos"][:, jt, :].unsqueeze(1).broadcast_to([128, nh, 8])
                                sinb = G["sin"][:, jt, :].unsqueeze(1).broadcast_to([128, nh, 8])
                                x1 = stv[:, :, 0:8]; x2 = stv[:, :, 8:16]
                                ta, tan = rtmp.next(); tbb, tbn = rtmp.next(); tc, tcn = rtmp.next(); td, tdn = rtmp.next()
                                S.add("dve", _mk("tensor_tensor", out=ta[:, 0:nh, :], in0=x1, in1=cosb, op=ALU.mult), reads=[stn, "rope"], writes=[tan])
                                S.add("dve", _mk("tensor_tensor", out=tbb[:, 0:nh, :], in0=x2, in1=sinb, op=ALU.mult), reads=[stn, "rope"], writes=[tbn])
                                S.add("dve", _mk("tensor_tensor", out=tc[:, 0:nh, :], in0=x2, in1=cosb, op=ALU.mult), reads=[stn, "rope"], writes=[tcn])
                                S.add("dve", _mk("tensor_tensor", out=td[:, 0:nh, :], in0=x1, in1=sinb, op=ALU.mult), reads=[stn, "rope"], writes=[tdn])
                                S.add("dve", _mk("tensor_tensor", out=stv[:, :, 0:8], in0=ta[:, 0:nh, :], in1=tbb[:, 0:nh, :], op=ALU.subtract),
                                      reads=[tan, tbn], writes=[stn])
                                S.add("dve", _mk("tensor_tensor", out=stv[:, :, 8:16], in0=tc[:, 0:nh, :], in1=td[:, 0:nh, :], op=ALU.add),
                                      reads=[tcn, tdn], writes=[stn])
                            kind = ch["kind"]
                            if ch["out"] is not None:
                                oc0 = ch.get("oc0", 0)
                                S.dma("pool", _mk("dma_start", out=ch["out"][r0:r0 + 128, oc0:oc0 + n], in_=st_[:, 0:n]),
                                      reads=[stn])
                            if kind in ("K", "Q"):
                                kb_, kbn = kb.next()
                                S.add("act", _mk("activation", out=kb_[:, 0:n], in_=st_[:, 0:n], func=AF.Copy, scale=ch["scale"]),
                                      reads=[stn], writes=[kbn])
                                ng = (n + 127) // 128
                                w_ = min(n, 128)
                                pt_, ptn = ptr.next()
                                for gi in range(ng):
                                    S.add("pe", _mk("transpose", out=pt_[0:w_, gi, :], in_=kb_[:, gi * 128:gi * 128 + w_],
                                                                                                   identity=identb[:]), reads=[kbn], writes=[ptn])
                                kt_, ktn = kTst.next()
                                S.add("dve", _mk("tensor_copy", out=kt_[0:w_, 0:ng, :], in_=pt_[0:w_, 0:ng, :]),
                                      reads=[ptn], writes=[ktn])
                                rr0 = ch.get("r0", 0)
                                if n >= 128:
                                    dst = ch["scr"][rr0:rr0 + n, r0:r0 + 128].rearrange("(g p) t -> p g t", p=128)
                                    S.dma("pool", _mk("dma_start", out=dst, in_=kt_[:, 0:ng, :]), reads=[ktn])
                                else:
                                    dst = ch["scr"][0:n, r0:r0 + 128]
                                    S.dma("pool", _mk("dma_start", out=dst, in_=kt_[0:n, 0, :]), reads=[ktn])
                            elif kind == "V":
                                kb_, kbn = kb.next()
                                S.add("act", _mk("copy", out=kb_[:, 0:n], in_=st_[:, 0:n]), reads=[stn], writes=[kbn])
                                dv = ch["dv"]; h0 = ch.get("h0", 0); nh = n // dv
                                dst = ch["scr"][h0:h0 + nh, :, jt, :].rearrange("h p d -> p h d")
                                S.dma("pool", _mk("dma_start", out=dst, in_=kb_[:, 0:n].rearrange("p (h d) -> p h d", d=dv)),
                                      reads=[kbn])

                    for ft in fm_tiles:
                        if "nofm" in dbg or ("nof" in dbg and ft["kind"] == "f"):
                            continue
                        n = ft["n"]; c0 = ft["c0"]
                        ps, psn = pfm.next()
                        for kc in range(8):
                            S.add("pe", _mk("matmul", ps[0:n, 0:TB], lhsT=W[:, kc, c0:c0 + n], rhs=hT[:, kc, 0:TB],
                                                                                  start=(kc == 0), stop=(kc == 7)),
                                  reads=hTall + wres(c0, n), writes=[psn])
                        kind = ft["kind"]
                        if kind in ("q", "g", "go"):
                            fs, fsn = fmst.next()
                            if kind == "q":
                                S.add("act", _mk("activation", out=fs[:, 0:TB], in_=ps[:, 0:TB], func=AF.Copy, scale=0.125),
                                      reads=[psn], writes=[fsn])
                            else:
                                S.add("act", _mk("activation", out=fs[:, 0:TB], in_=ps[:, 0:TB], func=AF.Silu), reads=[psn], writes=[fsn])
                                if kind == "go":
                                    S.add("dve", _mk("tensor_scalar", out=fs[:, 0:TB], in0=fs[:, 0:TB], scalar1=gainp[:, 0:1], scalar2=None,
                                                                                   op0=ALU.mult), reads=[fsn, "gainp"], writes=[fsn])
                            rr0 = ft["r0"]
                            S.dma("pool", _mk("dma_start", out=ft["scr"][rr0:rr0 + 128, t0:t0 + TB], in_=fs[:, 0:TB]),
                                  reads=[fsn])
                        else:
                            S.add("act", _mk("activation", out=lfe[:, 0:TB], in_=ps[0:8, 0:TB], func=AF.Exp, scale=-1.0, bias=negb[:, 0:1]),
                                  reads=[psn, "negb"], writes=["lfe"])
                            S.add("act", _mk("activation", out=lfe[:, 0:TB], in_=lfe[:, 0:TB], func=AF.Ln, bias=1.0, scale=1.0),
                                  reads=["lfe"], writes=["lfe"])
                            S.add("dve", _mk("tensor_scalar", out=lfT[:, 0:TB], in0=lfe[:, 0:TB], scalar1=-1.0, scalar2=None, op0=ALU.mult),
                                  reads=["lfe"], writes=["lfT"])
                            for s in range(nsub):
                                S.add("pe", _mk("transpose", out=plt[:, s, :], in_=lfT[0:8, s * 128:(s + 1) * 128], identity=identf[0:8, 0:8]),
                                      reads=["lfT"], writes=["plt"])
                            if g == "p":
                                S.add("dve", _mk("tensor_tensor_scan", out=cumT[:, 0:TB], data0=ones8[:, 0:TB], data1=lfT[:, 0:TB], initial=carry[:, 0:1],
                                                                              op0=ALU.mult, op1=ALU.add), reads=["lfT", "carry"], writes=["cumT"])
                                S.add("dve", _mk("tensor_copy", out=carry[:, 0:1], in_=cumT[:, TB - 1:TB]), reads=["cumT"], writes=["carry"])
                                S.add("dve", _mk("tensor_copy", out=chi[:, 0:TB], in_=cumT[:, 0:TB]), reads=["cumT"], writes=["chi"])
                                S.add("dve", _mk("tensor_copy", out=chf[:, 0:TB], in_=chi[:, 0:TB]), reads=["chi"], writes=["chf"])
                                S.add("dve", _mk("tensor_tensor", out=clo[:, 0:TB], in0=cumT[:, 0:TB], in1=chf[:, 0:TB], op=ALU.subtract),
                                      reads=["cumT", "chf"], writes=["clo"])
                                S.dma("pool", _mk("dma_start", out=SC["cqa_p"][:, 0, t0:t0 + TB], in_=chi[:, 0:TB]), reads=["chi"])
                                S.dma("pool", _mk("dma_start", out=SC["cqa_p"][:, 1, t0:t0 + TB], in_=clo[:, 0:TB]), reads=["clo"])
                                for s in range(nsub):
                                    S.add("pe", _mk("transpose", out=plt[:, 4 + s, :], in_=cumT[0:8, s * 128:(s + 1) * 128], identity=identf[0:8, 0:8]),
                                          reads=["cumT"], writes=["plt"])
                            S.add("dve", _mk("tensor_copy", out=lftm[:, 0:nsub, :], in_=plt[:, 0:nsub, :]), reads=["plt"], writes=["lftm"])
                            S.dma("pool", _mk("dma_start", out=O["a_f_" + g][li][t0:t0 + TB, :].rearrange("(s p) h -> p s h", p=128),
                                                                         in_=lftm[:, 0:nsub, :]), reads=["lftm"])
                            if g == "p":
                                S.add("dve", _mk("tensor_scalar", out=lftm[:, 4:4 + nsub, :], in0=plt[:, 4:4 + nsub, :], scalar1=-1.0, scalar2=None, op0=ALU.mult),
                                      reads=["plt"], writes=["lftm2"])
                                S.dma("pool", _mk("dma_start", out=SC["nck_p"][t0:t0 + TB, :].rearrange("(s p) h -> p s h", p=128),
                                                                        in_=lftm[:, 4:4 + nsub, :]), reads=["lftm2"])
                            else:
                                S.dma("pool", _mk("dma_start", out=SC["nck_s"][0:128, :], in_=lftm[:, 0, :]), reads=["lftm"])
            S.flush()

    def phase_B(L):
        even = (L % 2 == 0)
        li = L // 2
        last = (L == NL - 1)
        wout = (I["w_out_even"] if even else I["w_out_odd"])[li]
        dbgB = cfg.get("dbg", "")
        NKMAX = max(NP, ((NKS + 127) // 128) * 128)
        with ExitStack() as ph:
            wo = sb(ph, "wo", [128, 8, D], BF16)
            gbf = sb(ph, "gbf", [128, D], F32)
            cmask = sb(ph, "cmasks", [128, 4, 512], BF16)
            pbuf = Rot([(sb(ph, f"pbuf{k}", [128, 512], BF16), f"pbuf{k}") for k in range(3)])
            qTt = Rot([(sb(ph, f"qTt{k}", [66, 512], BF16), f"qTt{k}") for k in range(2)])
            gt = Rot([(sb(ph, f"gt{k}", [128, 512], BF16), f"gt{k}") for k in range(2)])
            ftmp = Rot([(sb(ph, f"ftmp{k}", [128, 512], F32), f"ftmp{k}") for k in range(4)])
            sqb = sb(ph, "sqb", [128, 512], BF16)
            mix = sb(ph, "mix", [128, 8, 512], BF16)
            xo = Rot([(sb(ph, f"xo{k}", [128, D], F32), f"xo{k}") for k in range(1)])
            xn_ = Rot([(sb(ph, f"xn{k}", [128, D], F32), f"xn{k}") for k in range(1)])
            junkB = sb(ph, "junkB", [128, D], BF16)
            smallB = Rot([(sb(ph, f"smB{k}", [128, 2], F32), f"smB{k}") for k in range(2)])
            nck = sb(ph, "nck", [128, max(NTP, NPT + 1), 8], F32)
            neglam = sb(ph, "neglam", [128, 1], F32)
            lp = sb(ph, "lp", [128, 4, 64], F32)
            lpt = sb(ph, "lpt", [128, 64], F32)
            lps = sb(ph, "lps", [128, 2], F32)
            if even:
                qit = Rot([(sb(ph, f"qit{k}", [64, 8, 128], BF16), f"qit{k}") for k in range(2)])
                wit = Rot([(sb(ph, f"wit{k}", [128, 8], F32), f"wit{k}") for k in range(2)])
                diagw = sb(ph, "diagw", [128, 8, 128], BF16)
                rbuf = Rot([(sb(ph, f"rbuf{k}", [128, 512], BF16), f"rbuf{k}") for k in range(4)])
                bis = sb(ph, "bis", [128, 8], F32)
                hwtab = sb(ph, "hwtab", [128, NIT], F32)
            B = {}
            pmisc = psb(ph, "pmisc", [128, 512], F32)
            sbank = Rot([(psb(ph, f"sbank{k}", [128, 512], F32), f"sbank{k}") for k in range(2)] + [(pmisc, "pmisc")])
            acc = [(psb(ph, f"acc{k}", [128, 512], F32), f"acc{k}") for k in range(4)]
            ptrB = psb(ph, "ptrB", [128, 8, 128], BF16)

            S.dma("pool", _mk("dma_start", out=wo[:], in_=wout.rearrange("(kc p) n -> p kc n", p=128)), writes=["wo"])
            if last:
                S.dma("sp", _mk("dma_start", out=gbf[:], in_=I["final_gain"].partition_broadcast(128)), writes=["gbf"])
            S.dma("pool", _mk("dma_start", out=cmask[:], in_=(I["cmask"] if even else I["ccmask"]).rearrange("j p q -> p j q")), writes=["const"])
            if even:
                S.dma("sp", _mk("dma_start", out=nck[:, 0:NTP, :], in_=SC["nck_p"].rearrange("(j p) h -> p j h", p=128)), writes=["nck"])
            else:
                S.dma("sp", _mk("dma_start", out=lp[:].rearrange("p a b -> p (a b)"),
                                                  in_=I["lam"][li].rearrange("a b -> (a b)").partition_broadcast(128)), writes=["lp"])
                for k in range(2):
                    S.add("dve", _mk("tensor_tensor", out=lpt[:], in0=lp[:, 2 * k, :], in1=lp[:, 2 * k + 1, :], op=ALU.mult),
                          reads=["lp"], writes=["lpt"])
                    S.add("dve", _mk("tensor_reduce", out=lps[:, k:k + 1], in_=lpt[:], axis=AX.X, op=ALU.add), reads=["lpt"], writes=["lps"])
                S.add("act", _mk("activation", out=lps[:], in_=lps[:], func=AF.Exp), reads=["lps"], writes=["lps"])
                S.add("dve", _mk("tensor_tensor", out=neglam[:], in0=lps[:, 1:2], in1=lps[:, 0:1], op=ALU.subtract), reads=["lps"], writes=["neglam"])
                S.add("dve", _mk("tensor_scalar", out=neglam[:], in0=neglam[:], scalar1=-LAM_INIT[L], scalar2=None, op0=ALU.add),
                      reads=["neglam"], writes=["neglam"])

            def attend(qT_ap, qres, QW, chunks, dv, O_ap, SM_ap, ores, preloaded=None, prefetch_cb=None):
                if isinstance(chunks, list) and chunks and isinstance(chunks[0], dict):
                    chunks = [(len(chunks), (lambda t=chunks: t))]
                n = sum(c[0] for c in chunks)
                starts = []
                acc_ = 0
                for c in chunks:
                    starts.append(acc_)
                    acc_ += c[0]
                loaded = []
                state = {"next": 0}

                def load_next():
                    if state["next"] < len(chunks):
                        loaded.extend(chunks[state["next"]][1]())
                        state["next"] += 1

                def stage_a(kt):
                    kn = kt["kn"]
                    ps, psn = sbank.next()
                    am = kt.get("addmask")
                    S.add("pe", _mk("matmul", ps[0:kn, 0:QW], lhsT=kt["kT"], rhs=qT_ap, start=True, stop=(am is None)),
                          reads=[kt["kres"], qres], writes=[psn])
                    if am is not None:
                        S.add("pe", _mk("matmul", ps[0:kn, 0:QW], lhsT=identb[0:kn, 0:kn], rhs=am, start=False, stop=True),
                              reads=["const"] + ([kt["mres"]] if "mres" in kt else []), writes=[psn])
                    pt, ptn = pbuf.next()
                    bias = kt.get("bias")
                    if bias is not None:
                        S.add("act", _mk("activation", out=pt[0:kn, 0:QW], in_=ps[0:kn, 0:QW], func=AF.Exp, bias=bias),
                              reads=[psn, kt["bres"]], writes=[ptn])
                    else:
                        S.add("act", _mk("activation", out=pt[0:kn, 0:QW], in_=ps[0:kn, 0:QW], func=AF.Exp),
                              reads=[psn], writes=[ptn])
                    return pt, ptn

                def stage_b(kt, pt, ptn, idx):
                    kn = kt["kn"]
                    S.add("pe", _mk("matmul", O_ap, lhsT=kt["v"], rhs=pt[0:kn, 0:QW], start=(idx == 0), stop=(idx == n - 1)),
                          reads=[ptn, kt["vres"]], writes=[ores[0]])
                    S.add("pe", _mk("matmul", SM_ap, lhsT=onesb[0:kn, 0:dv], rhs=pt[0:kn, 0:QW], start=(idx == 0), stop=(idx == n - 1)),
                          reads=[ptn], writes=[ores[1]])

                if preloaded is not None:
                    loaded.extend(preloaded)
                    state["next"] = 1
                else:
                    load_next()
                LA = 2
                queue = []
                issued = 0
                while issued < min(LA, n):
                    while len(loaded) < issued + 1:
                        load_next()
                    queue.append(stage_a(loaded[issued]))
                    issued += 1
                for t in range(n):
                    if t in starts:
                        load_next()
                    if t == starts[-1] and prefetch_cb is not None:
                        prefetch_cb()
                    if issued < n:
                        while len(loaded) < issued + 1:
                            load_next()
                        queue.append(stage_a(loaded[issued]))
                        issued += 1
                    cur = queue.pop(0)
                    stage_b(loaded[t], cur[0], cur[1], t)

            def prompt_tiles(kscr, krow0, Kc, vscr, vh, vd0, dv, KE, bias_h=None, masks=None, j_diag0=None, mul=None, mres=None):
                chunks = []
                for c in range((KE + 15) // 16):
                    j0 = c * 16
                    nt = min(16, KE - j0)

                    def thunk(j0=j0, nt=nt):
                        tiles = []
                        kb_, kbn = kTb_.next()
                        vv, vvn = vb_.next()
                        S.dma("sp", _mk("dma_start", out=kb_[0:64, 0:nt * 128], in_=kscr[krow0:krow0 + 64, j0 * 128:(j0 + nt) * 128]),
                              writes=[kbn])
                        S.dma("sp", _mk("dma_start", out=vv[:, 0:nt, 0:dv], in_=vscr[vh, :, j0:j0 + nt, vd0:vd0 + dv]), writes=[vvn])
                        for jj in range(nt):
                            j = j0 + jj
                            t = dict(kn=128, kT=kb_[0:Kc, jj * 128:(jj + 1) * 128], kres=kbn, v=vv[:, jj, 0:dv], vres=vvn)
                            if bias_h is not None:
                                t["bias"] = nck[:, j, bias_h:bias_h + 1]; t["bres"] = "nck"
                            if masks is not None and j >= j_diag0:
                                t["addmask"] = masks[:, j - j_diag0, :]
                                if mres is not None:
                                    t["mres"] = mres
                            if mul is not None:
                                t["mulmask"] = mul[:, j, :]; t["mres"] = "selT"
                            tiles.append(t)
                        return tiles
                    chunks.append((nt, thunk))
                return chunks

            def fin_pair(O_, SM_, ores, gscr, grow0, qcol0, QW, ct, mcol0):
                g_, gn = gt.next()
                S.dma("sp", _mk("dma_start", out=g_[:, 0:QW], in_=gscr[grow0:grow0 + 128, qcol0:qcol0 + QW]), writes=[gn])
                r_, rn = ftmp.next()
                S.add("dve", _mk("reciprocal", out=r_[:, 0:QW], in_=SM_[:, 0:QW]), reads=[ores[1]], writes=[rn])
                S.add("dve", _mk("tensor_tensor", out=r_[:, 0:QW], in0=O_[:, 0:QW], in1=r_[:, 0:QW], op=ALU.mult), reads=[ores[0], rn], writes=[rn])
                S.add("dve", _mk("tensor_tensor", out=mix[:, ct, mcol0:mcol0 + QW], in0=r_[:, 0:QW], in1=g_[:, 0:QW], op=ALU.mult),
                      reads=[rn, gn], writes=[("mix", ct)])

            def fin_diff(gscr, grow0, qcol0, QW, ct, mcol0):
                g_, gn = gt.next()
                S.dma("sp", _mk("dma_start", out=g_[:, 0:QW], in_=gscr[grow0:grow0 + 128, qcol0:qcol0 + QW]), writes=[gn])
                r0_, r0n = ftmp.next(); r1_, r1n = ftmp.next()
                (O0, o0n), (S0, s0n), (O1, o1n), (S1, s1n) = acc
                S.add("dve", _mk("reciprocal", out=r0_[:, 0:QW], in_=S0[:, 0:QW]), reads=[s0n], writes=[r0n])
                S.add("dve", _mk("tensor_tensor", out=r0_[:, 0:QW], in0=O0[:, 0:QW], in1=r0_[:, 0:QW], op=ALU.mult), reads=[o0n, r0n], writes=[r0n])
                S.add("dve", _mk("reciprocal", out=r1_[:, 0:QW], in_=S1[:, 0:QW]), reads=[s1n], writes=[r1n])
                S.add("dve", _mk("tensor_tensor", out=r1_[:, 0:QW], in0=O1[:, 0:QW], in1=r1_[:, 0:QW], op=ALU.mult), reads=[o1n, r1n], writes=[r1n])
                S.add("dve", _mk("scalar_tensor_tensor", out=r0_[:, 0:QW], in0=r1_[:, 0:QW], scalar=neglam[:, 0:1], in1=r0_[:, 0:QW],
                                                               op0=ALU.mult, op1=ALU.add), reads=[r0n, r1n, "neglam"], writes=[r0n])
                S.add("act", _mk("activation", out=sqb[:, 0:QW], in_=r0_[:, 0:QW], func=AF.Square), reads=[r0n], writes=["sqb"])
                S.add("pe", _mk("matmul", pmisc[:, 0:QW], lhsT=onesdiv[:], rhs=sqb[:, 0:QW], start=True, stop=True), reads=["sqb"], writes=["pmisc"])
                S.add("dve", _mk("tensor_scalar", out=r1_[:, 0:QW], in0=pmisc[:, 0:QW], scalar1=1e-6, scalar2=None, op0=ALU.add),
                      reads=["pmisc"], writes=[r1n])
                S.add("act", _mk("activation", out=r1_[:, 0:QW], in_=r1_[:, 0:QW], func=AF.Sqrt), reads=[r1n], writes=[r1n])
                S.add("dve", _mk("reciprocal", out=r1_[:, 0:QW], in_=r1_[:, 0:QW]), reads=[r1n], writes=[r1n])
                S.add("dve", _mk("tensor_tensor", out=r0_[:, 0:QW], in0=r0_[:, 0:QW], in1=r1_[:, 0:QW], op=ALU.mult), reads=[r0n, r1n], writes=[r0n])
                S.add("dve", _mk("tensor_tensor", out=mix[:, ct, mcol0:mcol0 + QW], in0=r0_[:, 0:QW], in1=g_[:, 0:QW], op=ALU.mult),
                      reads=[r0n, gn], writes=[("mix", ct)])

            def out_proj(g, t0, nsub):
                xsrc = (I["xp"] if g == "p" else I["xs"]) if L == 0 else SC["resid_" + g]
                for s in range(nsub):
                    r0 = t0 + s * 128
                    x_, xn = xo.next()
                    S.dma("sp", _mk("dma_start", out=x_[:], in_=xsrc[r0:r0 + 128, :]), writes=[xn])
                    y_, yn = xn_.next()
                    for half in range(2):
                        pa, pan = acc[half]
                        for ct in range(8):
                            S.add("pe", _mk("matmul", pa[:, :], lhsT=mix[:, ct, s * 128:(s + 1) * 128],
                                                                                      rhs=wo[:, ct, half * 512:(half + 1) * 512], start=(ct == 0), stop=(ct == 7)),
                                  reads=[("mix", ct), "wo"], writes=[pan])
                        S.add("dve", _mk("tensor_tensor", out=y_[:, half * 512:(half + 1) * 512], in0=pa[:, :],
                                                                                            in1=x_[:, half * 512:(half + 1) * 512], op=ALU.add),
                              reads=[pan, xn], writes=[yn])
                    if not last:
                        S.dma("pool", _mk("dma_start", out=SC["resid_" + g][r0:r0 + 128, :], in_=y_[:]), reads=[yn])
                    else:
                        sm, smn = smallB.next()
                        S.add("act", _mk("activation", out=junkB[:], in_=y_[:], func=AF.Square, accum_out=sm[:, 0:1]),
                              reads=[yn], writes=["junkB", smn])
                        S.add("dve", _mk("tensor_scalar", out=sm[:, 1:2], in0=sm[:, 0:1], scalar1=1.0 / D, scalar2=1e-6, op0=ALU.mult, op1=ALU.add),
                              reads=[smn], writes=[smn])
                        S.add("act", _mk("activation", out=sm[:, 1:2], in_=sm[:, 1:2], func=AF.Sqrt), reads=[smn], writes=[smn])
                        S.add("dve", _mk("reciprocal", out=sm[:, 1:2], in_=sm[:, 1:2]), reads=[smn], writes=[smn])
                        S.add("dve", _mk("scalar_tensor_tensor", out=y_[:], in0=y_[:], scalar=sm[:, 1:2], in1=gbf[:], op0=ALU.mult, op1=ALU.mult),
                              reads=[yn, smn, "gbf"], writes=[yn])
                        S.dma("pool", _mk("dma_start", out=O["y_" + g][r0:r0 + 128, :], in_=y_[:]), reads=[yn])

            def indexer(qiT_src, qc0, nq, wi_src, wr0, kiT_of, NV, diag_j, ul, topk, tail_to=None):
                q_, qn = qit.next()
                S.dma("sp", _mk("dma_start", out=q_[:, :, 0:nq], in_=qiT_src.rearrange("(h d) t -> d h t", d=64)[:, :, qc0:qc0 + nq]), writes=[qn])
                w_, wn = wit.next()
                S.dma("sp", _mk("dma_start", out=w_[0:nq, :], in_=wi_src[wr0:wr0 + nq, :]), writes=[wn])
                for hh in range(8):
                    S.add("act", _mk("activation", out=diagw[0:nq, hh, 0:nq], in_=identb[0:nq, 0:nq], func=AF.Copy, scale=w_[0:nq, hh:hh + 1]),
                          reads=[wn, "const"], writes=["diagw"])
                sc_ps, scn = acc[0]
                for kg in range((NV + 511) // 512):
                    k0 = kg * 512
                    n = min(512, NV - k0)
                    kap, kres = kiT_of(k0, n)

                    def dots(hh):
                        ps, psn = sbank.next()
                        S.add("pe", _mk("matmul", ps[0:nq, 0:n], lhsT=q_[:, hh, 0:nq], rhs=kap, start=True, stop=True),
                              reads=[qn, kres], writes=[psn])
                        r_, rn = rbuf.next()
                        S.add("act", _mk("activation", out=r_[0:nq, 0:n], in_=ps[0:nq, 0:n], func=AF.Relu), reads=[psn], writes=[rn])
                        return r_, rn
                    cur = dots(0)
                    for hh in range(8):
                        nxt = dots(hh + 1) if hh + 1 < 8 else None
                        S.add("pe", _mk("matmul", sc_ps[0:nq, 0:n], lhsT=diagw[0:nq, hh, 0:nq], rhs=cur[0][0:nq, 0:n], start=(hh == 0), stop=(hh == 7)),
                              reads=[cur[1], "diagw"], writes=[scn])
                        cur = nxt
                    S.add("act", _mk("copy", out=B["scores"][0:nq, k0:k0 + n], in_=sc_ps[0:nq, 0:n]), reads=[scn], writes=["scores"])
                steps = []
                sc = B["scores"]; sl = B["sel"]

                def prep():
                    S.add("dve", _mk("tensor_reduce", out=bis[0:nq, 0:1], in_=sc[0:nq, 0:NV], axis=AX.X, op=ALU.min), reads=["scores"], writes=["bis"])
                    if diag_j is not None:
                        S.add("dve", _mk("tensor_tensor", out=sc[0:nq, diag_j * 128:(diag_j + 1) * 128], in0=sc[0:nq, diag_j * 128:(diag_j + 1) * 128],
                                         in1=idxmask[0:nq, :], op=ALU.add), reads=["scores", "const"], writes=["scores"])
                    S.add("dve", _mk("tensor_reduce", out=bis[0:nq, 1:2], in_=sc[0:nq, 0:NV], axis=AX.X, op=ALU.max), reads=["scores"], writes=["bis"])
                    S.add("dve", _mk("scalar_tensor_tensor", out=bis[0:nq, 1:2], in0=bis[0:nq, 1:2], scalar=1.0, in1=bis[0:nq, 0:1], op0=ALU.add, op1=ALU.subtract),
                          reads=["bis"], writes=["bis"])
                    S.add("dve", _mk("tensor_scalar", out=hwtab[0:nq, :], in0=pow2[0:nq, :], scalar1=bis[0:nq, 1:2], scalar2=None, op0=ALU.mult),
                          reads=["bis", "const"], writes=["hwtab"])
                steps.append(prep)

                def it(k):
                    S.add("dve", _mk("tensor_tensor", out=bis[0:nq, 2:3], in0=bis[0:nq, 0:1], in1=hwtab[0:nq, k:k + 1], op=ALU.add),
                          reads=["bis", "hwtab"], writes=["bis"])
                    S.add("dve", _mk("tensor_scalar", out=sl[0:nq, ul, 0:NV], in0=sc[0:nq, 0:NV], scalar1=bis[0:nq, 2:3], scalar2=0.0,
                                     op0=ALU.is_ge, op1=ALU.add, accum_out=bis[0:nq, 3:4]), reads=["scores", "bis"], writes=[("sel", ul), "bis"])
                    S.add("dve", _mk("scalar_tensor_tensor", out=bis[0:nq, 4:5], in0=bis[0:nq, 3:4], scalar=topk - 0.5, in1=hwtab[0:nq, k:k + 1],
                                     op0=ALU.is_ge, op1=ALU.mult), reads=["bis", "hwtab"], writes=["bis"])
                    S.add("dve", _mk("tensor_tensor", out=bis[0:nq, 0:1], in0=bis[0:nq, 0:1], in1=bis[0:nq, 4:5], op=ALU.add), reads=["bis"], writes=["bis"])
                for k in range(NIT if "nobis" not in dbgB else 0):
                    steps.append(lambda k=k: it(k))

                def fin():
                    S.add("dve", _mk("tensor_scalar", out=sl[0:nq, ul, 0:NV], in0=sc[0:nq, 0:NV], scalar1=bis[0:nq, 0:1], scalar2=None, op0=ALU.is_ge),
                          reads=["scores", "bis"], writes=[("sel", ul)])
                    if tail_to is not None and NV < tail_to:
                        S.add("pool", _mk("memset", sl[:, ul, NV:tail_to], 0.0), writes=[("sel", ul)])
                steps.append(fin)
                return steps

            with ExitStack() as sec:
                kTb_ = Rot([(sb(sec, f"kTbuf{k}", [66, 2048], BF16), f"kTbuf{k}") for k in range(2)])
                vb_ = Rot([(sb(sec, f"vbuf{k}", [128, 16, 128], BF16), f"vbuf{k}") for k in range(2)])
                for k in range(2):
                    t_, tn = kTb_.next()
                    S.add("pool", _mk("memset", t_[64:66, :], 1.0), writes=[tn])
                if even:
                    B["scores"] = sb(sec, "scores", [128, NP], F32)
                    B["sel"] = sb(sec, "sel", [128, 2, NP], BF16)
                    B["selT"] = sb(sec, "selT", [128, NTP, 256], BF16)
                    kic = Rot([(sb(sec, f"kic{k}", [64, 512], BF16), f"kic{k}") for k in range(2)])
                if even:
                    def head_prep(kind, m, u2, h):
                        if kind == "fox":
                            Q0 = 512 * m
                            KE = 4 * (m + 1)
                            q_, qn = qTt.next()
                            S.dma("sp", _mk("dma_start", out=q_[0:64, :], in_=SC["qTa_p"][h * 64:(h + 1) * 64, Q0:Q0 + 512]), writes=[qn])
                            S.dma("sp", _mk("dma_start", out=q_[64:66, :], in_=SC["cqa_p"][h, :, Q0:Q0 + 512]), writes=[qn])
                            chunks = prompt_tiles(SC["kTa_p"], h * 64, 66, SC["va_p"], h, 0, 64, KE, bias_h=h, masks=cmask, j_diag0=4 * m)
                        else:
                            q0 = 512 * m + 256 * u2
                            KE2 = 4 * m + 2 * u2 + 2
                            q_, qn = qTt.next()
                            S.dma("sp", _mk("dma_start", out=q_[0:64, 0:256], in_=SC["qTb_p"][h * 64:(h + 1) * 64, q0:q0 + 256]), writes=[qn])
                            chunks = prompt_tiles(SC["kTb_p"], h * 64, 64, SC["vb_p"], h, 0, 64, KE2, masks=B["selT"], j_diag0=0, mres="selT")
                        first = chunks[0][1]()
                        return dict(q=q_, qn=qn, chunks=chunks, first=first)

                    def head_run(kind, m, u2, h, ctx, cb):
                        q_, qn = ctx["q"], ctx["qn"]
                        pb = (h % 2) * 64
                        if kind == "fox":
                            Q0 = 512 * m
                            (Oa, on), (Sa, sn) = acc[0], acc[1]
                            attend(q_[0:66, 0:512], qn, 512, ctx["chunks"], 64, Oa[pb:pb + 64, :], Sa[pb:pb + 64, :], (on, sn),
                                   preloaded=ctx["first"], prefetch_cb=cb)
                            if h % 2 == 1:
                                fin_pair(Oa, Sa, (on, sn), SC["gTa_p"], (h // 2) * 128, Q0, 512, h // 2, 0)
                        else:
                            q0 = 512 * m + 256 * u2
                            (Oa, on), (Sa, sn) = acc[2], acc[3]
                            attend(q_[0:64, 0:256], qn, 256, ctx["chunks"], 64, Oa[pb:pb + 64, 0:256], Sa[pb:pb + 64, 0:256], (on, sn),
                                   preloaded=ctx["first"], prefetch_cb=cb)
                            if h % 2 == 1:
                                fin_pair(Oa, Sa, (on, sn), SC["gTb_p"], (h // 2) * 128, q0, 256, 4 + h // 2, 256 * u2)

                    def idx_part(m, u2, ul):
                        i = 4 * m + 2 * u2 + ul
                        NV = 128 * (i + 1)
                        KE2 = 4 * m + 2 * u2 + 2

                        def kiT_of(k0, n):
                            kc_, kcn = kic.next()
                            S.dma("sp", _mk("dma_start", out=kc_[:, 0:n], in_=SC["kiT_p"][:, k0:k0 + n]), writes=[kcn])
                            return kc_[:, 0:n], kcn
                        if "noidx" in dbgB:
                            return []
                        return indexer(SC["qiT_p"], i * 128, 128, SC["wi_p"], i * 128, kiT_of, NV, i, ul, TOPK_P, tail_to=KE2 * 128)

                    def sel_transposes(m, u2):
                        KE2 = 4 * m + 2 * u2 + 2
                        for j in range(KE2):
                            for ul in range(2):
                                S.add("pe", _mk("transpose", out=ptrB[:, ul, :], in_=B["sel"][:, ul, j * 128:(j + 1) * 128], identity=identb[:]),
                                      reads=[("sel", ul)], writes=["ptrB"])
                            S.add("act", _mk("activation", out=B["selT"][:, j, :], in_=ptrB[:, 0:2, :].rearrange("p a b -> p (a b)"), func=AF.Identity,
                                             scale=-NEGM, bias=NEGM), reads=["ptrB"], writes=["selT"])

                    subs = [(m, u2) for m in range(NB) for u2 in range(2)]
                    for ul in range(2):
                        for st_ in idx_part(subs[0][0], subs[0][1], ul):
                            st_()
                    for si, (m, u2) in enumerate(subs):
                        if "nodsaattn" not in dbgB:
                            sel_transposes(m, u2)
                        nxt = subs[si + 1] if si + 1 < len(subs) else None
                        heads = []
                        if "nofox" not in dbgB:
                            heads += [("fox", h) for h in (range(0, 4) if u2 == 0 else range(4, 8))]
                        if "nodsaattn" not in dbgB:
                            heads += [("dsa", h) for h in range(8)]
                        half = (len(heads) + 1) // 2
                        for part in range(2):
                            pend = idx_part(nxt[0], nxt[1], part) if nxt is not None else []
                            hs_ = heads[:half] if part == 0 else heads[half:]
                            per = (len(pend) + max(1, len(hs_)) - 1) // max(1, len(hs_))
                            ctxs = {}
                            for hi, (kind, h) in enumerate(hs_):
                                ctx = ctxs.pop(hi, None)
                                if ctx is None:
                                    ctx = head_prep(kind, m, u2, h)
                                cb = None
                                if hi + 1 < len(hs_):
                                    def cb(hi=hi):
                                        k2, h2 = hs_[hi + 1]
                                        ctxs[hi + 1] = head_prep(k2, m, u2, h2)
                                head_run(kind, m, u2, h, ctx, cb)
                                for _ in range(min(per, len(pend))):
                                    pend.pop(0)()
                            while pend:
                                pend.pop(0)()
                        if u2 == 1:
                            out_proj("p", 512 * m, 4)
                for m in range(NB if not even else 0):
                    Q0 = 512 * m
                    KE = 4 * (m + 1)
                    if even:
                        pass
                    else:
                        def dprep(hs):
                            q_, qn = qTt.next()
                            S.dma("sp", _mk("dma_start", out=q_[0:64, :], in_=SC["qTc_p"][hs * 64:(hs + 1) * 64, Q0:Q0 + 512]), writes=[qn])
                            chunks = prompt_tiles(SC["kTc_p"], hs * 64, 64, SC["vc_p"], hs // 2, 0, 128, KE, masks=cmask, j_diag0=4 * m)
                            return dict(q=q_, qn=qn, chunks=chunks, first=chunks[0][1]())
                        dctx = {}
                        for hs in range(16):
                            h, c = hs // 2, hs % 2
                            ctx = dctx.pop(hs, None)
                            if ctx is None:
                                ctx = dprep(hs)
                            cb = None
                            if hs + 1 < 16:
                                def cb(hs=hs):
                                    dctx[hs + 1] = dprep(hs + 1)
                            (Oa, on), (Sa, sn) = acc[2 * c], acc[2 * c + 1]
                            attend(ctx["q"][0:64, 0:512], ctx["qn"], 512, ctx["chunks"], 128, Oa[:, :], Sa[:, :], (on, sn),
                                   preloaded=ctx["first"], prefetch_cb=cb)
                            if c == 1:
                                fin_diff(SC["gTc_p"], h * 128, Q0, 512, h, 0)
                    out_proj("p", Q0, 4)

                S.flush()

            with ExitStack() as sec:
                NKSP = ((NKS + 127) // 128) * 128
                if even:
                    B["scores"] = sb(sec, "scores_s", [128, NKSP], F32)
                    B["sel"] = sb(sec, "sel_s", [128, 1, NKSP], BF16)
                    B["selT"] = sb(sec, "selT_s", [128, NPT + 1, 32], BF16)
                    B["kTs"] = sb(sec, "kTs", [64, 8, NKS], BF16)
                    B["vs"] = sb(sec, "vs", [128, NPT + 1, 512], BF16)
                    B["cst"] = sb(sec, "cst", [128, NPT, 512], BF16)
                    B["lfs"] = sb(sec, "lfs", [128, NPT + 1, 8], F32)
                    B["kis"] = sb(sec, "kis", [64, NKS], BF16)
                    B["csk"] = sb(sec, "csk", [128, NPT, 64], BF16)
                else:
                    B["kTs"] = sb(sec, "kTs", [64, 16, NKS], BF16)
                    B["vs"] = sb(sec, "vs", [128, NPT + 1, 1024], BF16)
                    B["cst"] = sb(sec, "cst", [128, NPT, 1024], BF16)
                for b in range(4 if "nosampleB" not in dbgB else 0):
                    qc0 = 32 * b
                    ntile = NPT + 1

                    def load_kT(cache_ap, ncols, nsh, new_scr, dst):
                        S.dma("pool", _mk("dma_start", out=B["cst"][:, :, 0:ncols], in_=cache_ap.rearrange("(j p) c -> p j c", p=128)), writes=["cst"])
                        for sh in range(nsh):
                            for j4 in range(0, NPT, 4):
                                nj = min(4, NPT - j4)
                                for jj in range(nj):
                                    S.add("pe", _mk("transpose", out=ptrB[0:64, jj, :], in_=B["cst"][:, j4 + jj, sh * 64:(sh + 1) * 64], identity=identb[:]),
                                          reads=["cst"], writes=["ptrB"])
                                S.add("dve", _mk("tensor_copy", out=dst[0:64, sh, j4 * 128:(j4 + nj) * 128].rearrange("p (j t) -> p j t", t=128),
                                                                                           in_=ptrB[0:64, 0:nj, :]), reads=["ptrB"], writes=["kTs"])
                        S.dma("sp", _mk("dma_start", out=dst[0:64, 0:nsh, PAST:PAST + 32], in_=new_scr.rearrange("(h d) t -> d h t", d=64)[:, :, qc0:qc0 + 32]),
                              writes=["kTs"])

                    def load_v(cache_ap, ncols, new_scr, dv):
                        S.dma("pool", _mk("dma_start", out=B["vs"][:, 0:NPT, 0:ncols], in_=cache_ap.rearrange("(j p) c -> p j c", p=128)), writes=["vs"])
                        S.dma("sp", _mk("dma_start", out=B["vs"][0:32, NPT, 0:ncols].rearrange("p (h d) -> p h d", d=dv),
                                                          in_=new_scr[:, qc0:qc0 + 32, 0, :].rearrange("h p d -> p h d")), writes=["vs"])

                    def sample_tiles(sh, h, dv, bias_h=None, mask_new=None, mul=None):
                        tiles = []
                        for j in range(ntile):
                            kn = 128 if j < NPT else 32
                            t = dict(kn=kn, kT=B["kTs"][0:64, sh, j * 128:j * 128 + kn], kres="kTs", v=B["vs"][0:kn, j, h * dv:(h + 1) * dv], vres="vs")
                            if bias_h is not None:
                                t["bias"] = nck[0:kn, j, bias_h:bias_h + 1]; t["bres"] = "nck"
                            if mask_new is not None and j == NPT:
                                t["addmask"] = mask_new
                            if mul is not None:
                                t["addmask"] = mul[0:kn, j, 0:32]; t["mres"] = "selT"
                            tiles.append(t)
                        return tiles

                    if even:
                        load_kT(I["ca_k"][li, b], 512, 8, SC["kTa_s"], B["kTs"])
                        load_v(I["ca_v"][li, b], 512, SC["va_s"], 64)
                        S.add("dve", _mk("memset", B["lfs"][:, NPT, :], 0.0), writes=["lfs"])
                        S.dma("sp", _mk("dma_start", out=B["lfs"][:, 0:NPT, :], in_=I["ca_f"][li, b].rearrange("(j p) h -> p j h", p=128)), writes=["lfs"])
                        S.dma("sp", _mk("dma_start", out=B["lfs"][0:32, NPT, :], in_=SC["nck_s"][qc0:qc0 + 32, :]), writes=["lfs"])
                        cps, cpn = acc[0]
                        tps, tpn = acc[1]
                        for j in range(ntile):
                            S.add("pe", _mk("matmul", cps[:, j * 8:(j + 1) * 8], lhsT=utf[:], rhs=B["lfs"][:, j, :], start=True, stop=(j == 0)),
                                  reads=["lfs", "const"], writes=[cpn])
                            for j2 in range(j):
                                S.add("pe", _mk("matmul", cps[:, j * 8:(j + 1) * 8], lhsT=onesf[:], rhs=B["lfs"][:, j2, :], start=False, stop=(j2 == j - 1)),
                                      reads=["lfs", "const"], writes=[cpn])
                        for j in range(ntile):
                            S.add("pe", _mk("matmul", tps[:, 0:8], lhsT=onesf[:], rhs=B["lfs"][:, j, :], start=(j == 0), stop=(j == ntile - 1)),
                                  reads=["lfs", "const"], writes=[tpn])
                        S.add("act", _mk("copy", out=B["lfs"][:, 0, :], in_=tps[:, 0:8]), reads=[tpn], writes=["lfs"])
                        S.add("dve", _mk("tensor_tensor", out=nck[:, 0:ntile, :], in0=B["lfs"][:, 0, :].unsqueeze(1).broadcast_to([128, ntile, 8]),
                                                               in1=cps[:, 0:ntile * 8].rearrange("p (j h) -> p j h", h=8), op=ALU.subtract),
                              reads=[cpn, "lfs"], writes=["nck"])
                        for h in range(8):
                            q_, qn = qTt.next()
                            S.dma("sp", _mk("dma_start", out=q_[0:64, 0:32], in_=SC["qTa_s"][h * 64:(h + 1) * 64, qc0:qc0 + 32]), writes=[qn])
                            tiles = sample_tiles(h, h, 64, bias_h=h, mask_new=cmask_s[0:32, 0:32])
                            pb = (h % 2) * 64
                            (Oa, on), (Sa, sn) = acc[2], acc[3]
                            attend(q_[0:64, 0:32], qn, 32, tiles, 64, Oa[pb:pb + 64, 0:32], Sa[pb:pb + 64, 0:32], (on, sn))
                            if h % 2 == 1:
                                fin_pair(Oa, Sa, (on, sn), SC["gTa_s"], (h // 2) * 128, qc0, 32, h // 2, qc0)
                        S.dma("pool", _mk("dma_start", out=B["csk"][:], in_=I["cb_ki"][li, b].rearrange("(j p) c -> p j c", p=128)), writes=["csk"])
                        for j4 in range(0, NPT, 4):
                            nj = min(4, NPT - j4)
                            for jj in range(nj):
                                S.add("pe", _mk("transpose", out=ptrB[0:64, jj, :], in_=B["csk"][:, j4 + jj, :], identity=identb[:]), reads=["csk"], writes=["ptrB"])
                            S.add("dve", _mk("tensor_copy", out=B["kis"][0:64, j4 * 128:(j4 + nj) * 128].rearrange("p (j t) -> p j t", t=128), in_=ptrB[0:64, 0:nj, :]),
                                  reads=["ptrB"], writes=["kis"])
                        S.dma("sp", _mk("dma_start", out=B["kis"][0:64, PAST:PAST + 32], in_=SC["kiT_s"][:, qc0:qc0 + 32]), writes=["kis"])
                        for st_ in indexer(SC["qiT_s"], qc0, 32, SC["wi_s"], qc0, lambda k0, n: (B["kis"][0:64, k0:k0 + n], "kis"), NKS, None, 0, TOPK_S):
                            st_()
                        for j in range(ntile):
                            kn = 128 if j < NPT else 32
                            S.add("pe", _mk("transpose", out=ptrB[0:kn, 0, 0:32], in_=B["sel"][0:32, 0, j * 128:j * 128 + kn], identity=identb[0:32, 0:32]),
                                  reads=[("sel", 0)], writes=["ptrB"])
                            S.add("act", _mk("activation", out=B["selT"][0:kn, j, 0:32], in_=ptrB[0:kn, 0, 0:32], func=AF.Identity, scale=-NEGM, bias=NEGM), reads=["ptrB"], writes=["selT"])
                        load_kT(I["cb_k"][li, b], 512, 8, SC["kTb_s"], B["kTs"])
                        load_v(I["cb_v"][li, b], 512, SC["vb_s"], 64)
                        for h in range(8):
                            q_, qn = qTt.next()
                            S.dma("sp", _mk("dma_start", out=q_[0:64, 0:32], in_=SC["qTb_s"][h * 64:(h + 1) * 64, qc0:qc0 + 32]), writes=[qn])
                            tiles = sample_tiles(h, h, 64, mul=B["selT"])
                            pb = (h % 2) * 64
                            (Oa, on), (Sa, sn) = acc[2], acc[3]
                            attend(q_[0:64, 0:32], qn, 32, tiles, 64, Oa[pb:pb + 64, 0:32], Sa[pb:pb + 64, 0:32], (on, sn))
                            if h % 2 == 1:
                                fin_pair(Oa, Sa, (on, sn), SC["gTb_s"], (h // 2) * 128, qc0, 32, 4 + h // 2, qc0)
                    else:
                        load_kT(I["cc_k"][li, b], 1024, 16, SC["kTc_s"], B["kTs"])
                        load_v(I["cc_v"][li, b], 1024, SC["vc_s"], 128)
                        for h in range(8):
                            for c in range(2):
                                hs = 2 * h + c
                                q_, qn = qTt.next()
                                S.dma("sp", _mk("dma_start", out=q_[0:64, 0:32], in_=SC["qTc_s"][hs * 64:(hs + 1) * 64, qc0:qc0 + 32]), writes=[qn])
                                tiles = sample_tiles(hs, h, 128)
                                (Oa, on), (Sa, sn) = acc[2 * c], acc[2 * c + 1]
                                attend(q_[0:64, 0:32], qn, 32, tiles, 128, Oa[:, 0:32], Sa[:, 0:32], (on, sn))
                            fin_diff(SC["gTc_s"], h * 128, qc0, 32, h, qc0)
                out_proj("s", 0, 1)
                S.flush()

    only = cfg.get("only")
    for L in range(NL):
        if only is None or f"A{L}" in only:
            phase_A(L)
        if only is None or f"B{L}" in only:
            phase_B(L)

    top.close()
    return nc, None


def rope_tables(pos):
    half = 8
    inv = (500000.0 ** (-np.arange(half, dtype=np.float32) * 2.0 / 16)).astype(np.float32)
    ang = pos.astype(np.float32)[:, None] * inv[None, :]
    return np.cos(ang).astype(np.float32), np.sin(ang).astype(np.float32)


def make_consts(NP, PAST):
    c = {}
    c["ident"] = np.eye(128, dtype=np.float32)
    cp, sp = rope_tables(np.arange(NP))
    c["cosp"], c["sinp"] = cp, sp
    pos_s = PAST + (np.arange(128) % 32)
    cs, ss = rope_tables(pos_s)
    c["coss"], c["sins"] = cs, ss
    k = np.arange(128)[:, None]
    q = np.arange(512)[None, :]
    cm = np.zeros((4, 128, 512), np.float32)
    ccm = np.zeros((4, 128, 512), np.float32)
    for jj in range(4):
        kp = 128 * jj + k
        cm[jj] = np.where(kp <= q, 0.0, NEGM)
        ccm[jj] = np.where(kp // 64 <= q // 64, 0.0, NEGM)
    c["cmask"], c["ccmask"] = cm, ccm
    qq = np.arange(128)[:, None]
    kk = np.arange(128)[None, :]
    c["idxmask"] = np.where(kk // 64 <= qq // 64, 0.0, -1e30).astype(np.float32)
    k32 = np.arange(32)[:, None]
    q32 = np.arange(32)[None, :]
    c["cmask_s"] = np.where(k32 <= q32, 0.0, NEGM).astype(np.float32)
    c["ut"] = np.triu(np.ones((128, 128), np.float32))
    c["pow2"] = np.tile((0.5 ** (np.arange(NIT) + 1)).astype(np.float32)[None, :], (128, 1))
    return c


_CACHE = {}


def run(inputs, cfg, n_cores, prompt_of_core, sample_of_core):
    key = tuple(sorted((k, str(v)) for k, v in cfg.items()))
    if key not in _CACHE:
        _CACHE[key] = build(cfg)
    nc, _ = _CACHE[key]
    NP, PAST = cfg["NP"], cfg["PAST"]
    consts = make_consts(NP, PAST)
    f = lambda a: np.ascontiguousarray(np.asarray(a, dtype=np.float32))
    in_maps = []
    NO = cfg["NL"] // 2
    for c in range(n_cores):
        pb = prompt_of_core[c]
        sbs = sample_of_core[c]
        m = dict(consts)
        m["xp"] = f(inputs["x_prompt"][pb])
        m["xs"] = f(np.asarray(inputs["x_sample"])[sbs].reshape(128, D))
        m["ca_k"] = f(np.asarray(inputs["cache_a_k"])[:, sbs].reshape(-1, 4, PAST, 512))
        m["ca_v"] = f(np.asarray(inputs["cache_a_v"])[:, sbs].reshape(-1, 4, PAST, 512))
        m["ca_f"] = f(np.asarray(inputs["cache_a_logf"])[:, sbs])
        m["cb_k"] = f(np.asarray(inputs["cache_b_k"])[:, sbs].reshape(-1, 4, PAST, 512))
        m["cb_v"] = f(np.asarray(inputs["cache_b_v"])[:, sbs].reshape(-1, 4, PAST, 512))
        m["cb_ki"] = f(np.asarray(inputs["cache_b_kidx"])[:, sbs])
        if NO:
            m["cc_k"] = f(np.asarray(inputs["cache_c_k"])[:, sbs].reshape(-1, 4, PAST, 1024))
            m["cc_v"] = f(np.asarray(inputs["cache_c_v"])[:, sbs].reshape(-1, 4, PAST, 1024))
            m["w_in_odd"] = f(inputs["w_in_odd"]); m["w_out_odd"] = f(inputs["w_out_odd"])
            m["lam"] = f(inputs["lambda_params"]); m["chg"] = f(inputs["c_head_gain"])
        m["w_in_even"] = f(inputs["w_in_even"]); m["w_out_even"] = f(inputs["w_out_even"])
        m["norm_gain"] = f(inputs["norm_gain"]); m["final_gain"] = f(inputs["final_gain"])
        m["b_forget"] = f(inputs["b_forget"])
        in_maps.append(m)
    res = run_bass_kernel_spmd(nc, in_maps, core_ids=list(range(n_cores)))
    return res.results


def assemble(results, cfg, n_prompt, n_sample_total, prompt_of_core, sample_of_core):
    NP, NL = cfg["NP"], cfg["NL"]
    NE, NO = (NL + 1) // 2, NL // 2
    first_core = {}
    for c, pb in enumerate(prompt_of_core):
        first_core.setdefault(pb, c)

    def P(name, shape_tail, lead=None):
        arrs = [np.asarray(results[first_core[pb]][name]) for pb in range(n_prompt)]
        if lead is None:
            return np.stack(arrs, 0).reshape((n_prompt, NP) + shape_tail)
        return np.stack(arrs, 1).reshape((lead, n_prompt, NP) + shape_tail)

    def Sm(name, shape_tail, lead=None):
        if lead is None:
            out = np.zeros((n_sample_total, 32) + shape_tail, np.float32)
            for c, sbs in enumerate(sample_of_core):
                out[sbs] = np.asarray(results[c][name]).reshape((4, 32) + shape_tail)
            return out
        out = np.zeros((lead, n_sample_total, 32) + shape_tail, np.float32)
        for c, sbs in enumerate(sample_of_core):
            out[:, sbs] = np.asarray(results[c][name]).reshape((lead, 4, 32) + shape_tail)
        return out

    outs = [P("y_p", (D,)), Sm("y_s", (D,))]
    outs += [P("oa_k_p", (8, 64), NE), P("oa_v_p", (8, 64), NE), P("oa_f_p", (8,), NE),
             P("ob_k_p", (8, 64), NE), P("ob_v_p", (8, 64), NE), P("ob_ki_p", (64,), NE)]
    if NO:
        outs += [P("oc_k_p", (8, 128), NO), P("oc_v_p", (8, 128), NO)]
    outs += [Sm("oa_k_s", (8, 64), NE), Sm("oa_v_s", (8, 64), NE), Sm("oa_f_s", (8,), NE),
             Sm("ob_k_s", (8, 64), NE), Sm("ob_v_s", (8, 64), NE), Sm("ob_ki_s", (64,), NE)]
    if NO:
        outs += [Sm("oc_k_s", (8, 128), NO), Sm("oc_v_s", (8, 128), NO)]
    return tuple(np.ascontiguousarray(o, dtype=np.float32) for o in outs)


def kernel(**inputs):
    cfg = dict(NP=8192, NL=4, PAST=1024, TOPK_P=256, TOPK_S=256)
    n_cores = 8
    prompt_of_core = [c % 2 for c in range(n_cores)]
    sample_of_core = [list(range(4 * c, 4 * c + 4)) for c in range(n_cores)]
    results = run(inputs, cfg, n_cores, prompt_of_core, sample_of_core)
    return assemble(results, cfg, 2, 32, prompt_of_core, sample_of_core)
```
